# Optimizing a Trainium2 kernel written in Bass

```python
import math
import jax, jax.numpy as jnp
from jax import lax
import numpy as np

D_MODEL = 1024
BATCH = 4
SEQ = 4096
DEPTH = 2

D_RG = D_MODEL
RG_HEADS = 8
RG_HEAD_DIM = D_RG // RG_HEADS
D_ML = D_MODEL
ML_HEADS = 4
ML_HEAD_DIM = D_ML // ML_HEADS
D_MIX = D_RG + D_ML
D_IN = 2 * D_RG + 3 * D_ML
CONV_WIDTH = 4
RG_C = 8.0
ML_CHUNK = 128
EPS = 1e-6

kernel_name = "hymba_style_rglru_mlstm_hybrid"


def rms_norm(x, g):
    x32 = x.astype(jnp.float32)
    y = x32 * lax.rsqrt(jnp.mean(x32 * x32, axis=-1, keepdims=True) + EPS)
    return (y * g.astype(jnp.float32)).astype(x.dtype)


def causal_depthwise_conv(x, w, b):
    ch = x.shape[-1]
    y = lax.conv_general_dilated(
        x, w[:, None, :].astype(x.dtype), window_strides=(1,),
        padding=[(CONV_WIDTH - 1, 0)], dimension_numbers=("NWC", "WIO", "NWC"),
        feature_group_count=ch)
    return y + b.astype(x.dtype)


def block_diag(x, w):
    h, dh, dout = w.shape
    xb = x.reshape(x.shape[:-1] + (h, dh))
    return jnp.einsum("bshd,hde->bshe", xb, w).reshape(x.shape[:-1] + (h * dout,))


def rglru(x, w_a, b_a, w_x, b_x, lam):
    r = jax.nn.sigmoid(block_diag(x, w_a) + b_a).astype(jnp.float32)
    i = jax.nn.sigmoid(block_diag(x, w_x) + b_x).astype(jnp.float32)
    log_a = -RG_C * r * jax.nn.softplus(-lam.astype(jnp.float32))
    a = jnp.exp(log_a)
    u = jnp.sqrt(-jnp.expm1(2.0 * log_a)) * (i * x.astype(jnp.float32))

    def combine(lhs, rhs):
        a1, b1 = lhs
        a2, b2 = rhs
        return a1 * a2, a2 * b1 + b2

    _, h = lax.associative_scan(combine, (a, u), axis=1)
    return h.astype(x.dtype)


def mlstm_chunkwise(q, k, v, log_i, log_f):
    bsz, s_len, nh, dh = q.shape
    nc = s_len // ML_CHUNK

    def to_chunks(t):
        return t.reshape(bsz, nc, ML_CHUNK, nh, dh).transpose(1, 0, 3, 2, 4)

    def gate_chunks(t):
        return t.reshape(bsz, nc, ML_CHUNK, nh).transpose(1, 0, 3, 2)

    causal = jnp.tril(jnp.ones((ML_CHUNK, ML_CHUNK), dtype=bool))

    def step(carry, xs):
        c_st, n_st, m_st = carry
        qc, kc, vc, li, lf = xs
        b = jnp.cumsum(lf, axis=-1)
        b_last = b[..., -1]
        d = jnp.where(causal, b[..., :, None] - b[..., None, :] + li[..., None, :], -jnp.inf)
        m_inter = b + m_st[..., None]
        m_t = jnp.maximum(m_inter, jnp.max(d, axis=-1))
        w_intra = jnp.exp(d - m_t[..., None])
        w_inter = jnp.exp(m_inter - m_t)
        s = jnp.einsum("bhtk,bhsk->bhts", qc, kc) * w_intra
        num = (jnp.einsum("bhts,bhsv->bhtv", s, vc)
               + w_inter[..., None] * jnp.einsum("bhtk,bhkv->bhtv", qc, c_st))
        den = jnp.sum(s, axis=-1) + w_inter * jnp.einsum("bhtk,bhk->bht", qc, n_st)
        h = num / jnp.maximum(jnp.abs(den), jnp.exp(-m_t))[..., None]
        g = b_last[..., None] - b + li
        m_new = jnp.maximum(b_last + m_st, jnp.max(g, axis=-1))
        w_state = jnp.exp(g - m_new[..., None])
        decay = jnp.exp(b_last + m_st - m_new)
        kw = kc * w_state[..., None]
        c_new = decay[..., None, None] * c_st + jnp.einsum("bhsk,bhsv->bhkv", kw, vc)
        n_new = decay[..., None] * n_st + jnp.sum(kw, axis=2)
        return (c_new, n_new, m_new), h

    init = (jnp.zeros((bsz, nh, dh, dh), jnp.float32),
            jnp.zeros((bsz, nh, dh), jnp.float32),
            jnp.zeros((bsz, nh), jnp.float32))
    _, h = lax.scan(step, init, (to_chunks(q), to_chunks(k), to_chunks(v),
                                 gate_chunks(log_i), gate_chunks(log_f)))
    return h.transpose(1, 0, 3, 2, 4).reshape(bsz, s_len, nh, dh)


def mlstm_branch(xm, o_pre, conv_w, conv_b, w_q, w_k, w_v, w_if, b_if, head_g):
    bsz, s_len, _ = xm.shape
    xc = jax.nn.silu(causal_depthwise_conv(xm, conv_w, conv_b))
    q = block_diag(xc, w_q)
    k = block_diag(xc, w_k)
    v = block_diag(xm, w_v)
    gates = (jnp.concatenate([q, k, v], axis=-1) @ w_if + b_if).astype(jnp.float32)
    log_i = gates[..., :ML_HEADS]
    log_f = jax.nn.log_sigmoid(gates[..., ML_HEADS:])

    def heads(t):
        return t.reshape(bsz, s_len, ML_HEADS, ML_HEAD_DIM).astype(jnp.float32)

    cell = mlstm_chunkwise(heads(q), heads(k) * (ML_HEAD_DIM ** -0.5), heads(v), log_i, log_f)
    h = jax.nn.sigmoid(heads(o_pre)) * cell
    h = h * lax.rsqrt(jnp.mean(h * h, axis=-1, keepdims=True) + EPS)
    return (h.reshape(bsz, s_len, D_ML) * head_g.astype(jnp.float32)).astype(xm.dtype)


def setup_inputs(seed: int = 0) -> dict:
    key = jax.random.key(seed)
    ks = jax.random.split(key, 24)
    f32 = jnp.float32

    def nrm(k, shape, scale):
        return jax.random.normal(k, shape, f32) * scale

    x = jax.random.normal(ks[0], (BATCH, SEQ, D_MODEL), f32)
    c = jax.random.normal(ks[1], (BATCH, D_MODEL), f32)
    norm_g = 1.0 + nrm(ks[2], (DEPTH, D_MODEL), 0.02)
    w_ada = nrm(ks[3], (DEPTH, D_MODEL, 3 * D_MODEL), 0.3 * D_MODEL ** -0.5)
    b_ada = nrm(ks[4], (DEPTH, 3 * D_MODEL), 0.02)
    w_in = nrm(ks[5], (DEPTH, D_MODEL, D_IN), D_MODEL ** -0.5)
    rg_conv_w = nrm(ks[6], (DEPTH, CONV_WIDTH, D_RG), CONV_WIDTH ** -0.5)
    rg_conv_b = nrm(ks[7], (DEPTH, D_RG), 0.02)
    rg_w_a = nrm(ks[8], (DEPTH, RG_HEADS, RG_HEAD_DIM, RG_HEAD_DIM), RG_HEAD_DIM ** -0.5)
    rg_b_a = nrm(ks[9], (DEPTH, D_RG), 0.02)
    rg_w_x = nrm(ks[10], (DEPTH, RG_HEADS, RG_HEAD_DIM, RG_HEAD_DIM), RG_HEAD_DIM ** -0.5)
    rg_b_x = nrm(ks[11], (DEPTH, D_RG), 0.02)
    a_c = jax.random.uniform(ks[12], (DEPTH, D_RG), f32, 0.9, 0.999)
    a0 = a_c ** (1.0 / RG_C)
    rg_lambda = jnp.log(a0) - jnp.log1p(-a0)
    ml_conv_w = nrm(ks[13], (DEPTH, CONV_WIDTH, D_ML), CONV_WIDTH ** -0.5)
    ml_conv_b = nrm(ks[14], (DEPTH, D_ML), 0.02)
    ml_w_q = nrm(ks[15], (DEPTH, ML_HEADS, ML_HEAD_DIM, ML_HEAD_DIM), ML_HEAD_DIM ** -0.5)
    ml_w_k = nrm(ks[16], (DEPTH, ML_HEADS, ML_HEAD_DIM, ML_HEAD_DIM), ML_HEAD_DIM ** -0.5)
    ml_w_v = nrm(ks[17], (DEPTH, ML_HEADS, ML_HEAD_DIM, ML_HEAD_DIM), ML_HEAD_DIM ** -0.5)
    ml_w_if = nrm(ks[18], (DEPTH, 3 * D_ML, 2 * ML_HEADS), 0.1 * (3 * D_ML) ** -0.5)
    b_i = nrm(ks[19], (DEPTH, ML_HEADS), 0.1) - 1.0
    b_f = jnp.linspace(3.0, 6.0, ML_HEADS, dtype=f32)[None, :] + nrm(ks[20], (DEPTH, ML_HEADS), 0.1)
    ml_b_if = jnp.concatenate([b_i, b_f], axis=-1)
    ml_norm_g = 1.0 + nrm(ks[21], (DEPTH, D_ML), 0.02)
    w_out = nrm(ks[22], (DEPTH, D_MIX, D_MODEL), D_MIX ** -0.5)
    final_g = 1.0 + nrm(ks[23], (D_MODEL,), 0.02)
    return {"x": x, "c": c, "norm_g": norm_g, "w_ada": w_ada, "b_ada": b_ada,
            "w_in": w_in, "rg_conv_w": rg_conv_w, "rg_conv_b": rg_conv_b,
            "rg_w_a": rg_w_a, "rg_b_a": rg_b_a, "rg_w_x": rg_w_x, "rg_b_x": rg_b_x,
            "rg_lambda": rg_lambda, "ml_conv_w": ml_conv_w, "ml_conv_b": ml_conv_b,
            "ml_w_q": ml_w_q, "ml_w_k": ml_w_k, "ml_w_v": ml_w_v, "ml_w_if": ml_w_if,
            "ml_b_if": ml_b_if, "ml_norm_g": ml_norm_g, "w_out": w_out, "final_g": final_g}


def reference(x, c, norm_g, w_ada, b_ada, w_in, rg_conv_w, rg_conv_b, rg_w_a, rg_b_a,
              rg_w_x, rg_b_x, rg_lambda, ml_conv_w, ml_conv_b, ml_w_q, ml_w_k, ml_w_v,
              ml_w_if, ml_b_if, ml_norm_g, w_out, final_g):
    split_pts = [D_RG, 2 * D_RG, 2 * D_RG + D_ML, 2 * D_RG + 2 * D_ML]
    c_act = jax.nn.silu(c)
    for l in range(DEPTH):
        mod = c_act @ w_ada[l] + b_ada[l]
        shift, scale, gate = jnp.split(mod, 3, axis=-1)
        h = rms_norm(x, norm_g[l]) * (1.0 + scale[:, None, :]) + shift[:, None, :]
        u = h @ w_in[l]
        rg_x, rg_z, ml_x, ml_o, ml_z = jnp.split(u, split_pts, axis=-1)
        y_rg = rglru(causal_depthwise_conv(rg_x, rg_conv_w[l], rg_conv_b[l]),
                     rg_w_a[l], rg_b_a[l], rg_w_x[l], rg_b_x[l], rg_lambda[l]) * jax.nn.silu(rg_z)
        y_ml = mlstm_branch(ml_x, ml_o, ml_conv_w[l], ml_conv_b[l], ml_w_q[l], ml_w_k[l],
                            ml_w_v[l], ml_w_if[l], ml_b_if[l], ml_norm_g[l]) * jax.nn.silu(ml_z)
        y = jnp.concatenate([y_rg, y_ml], axis=-1) @ w_out[l]
        x = x + gate[:, None, :] * y
    return rms_norm(x, final_g)
```

```python
import contextlib
import math
import numpy as np
import concourse.bass as bass
import concourse.mybir as mybir
from concourse.bass_utils import run_bass_kernel_spmd

F32 = mybir.dt.float32
BF16 = mybir.dt.bfloat16
ALU = mybir.AluOpType
AF = mybir.ActivationFunctionType

D = 1024
S = 4096
NL = 2
T = 256
NT = S // T
NCH = T // 128
EPS = 1e-6
LP = 144
NPV = NL * LP + 16
SLOT = 8448
NSLOT = 3


class Buf:
    __slots__ = ("name", "lw", "rd", "excl", "norec")

    def __init__(self, name, excl=False):
        self.name = name
        self.lw = None
        self.rd = {}
        self.excl = excl
        self.norec = False


class FW:
    def __init__(self, nc):
        self.nc = nc
        self.eng = {"pe": nc.tensor, "act": nc.scalar, "dve": nc.vector, "pool": nc.gpsimd, "sp": nc.sync}
        self.sems = {}
        self.cnt = {}
        self.seen = {e: {} for e in self.eng}
        for e in self.eng:
            self.sems[e] = nc.alloc_semaphore("s_" + e)
            self.cnt[e] = 0

    def _wait(self, e, key, val):
        if self.seen[e].get(key, 0) >= val:
            return
        self.eng[e].wait_ge(self.sems[key], val)
        self.seen[e][key] = val

    def deps(self, e, reads, writes, emit=True):
        pend = {}
        seen = self.seen[e]

        def need(k, v):
            if seen.get(k, 0) >= v:
                return
            if pend.get(k, 0) < v:
                pend[k] = v

        for b in reads:
            if b.lw is not None:
                k, v = b.lw
                if not (k == e and e == "pe"):
                    need(k, v)
        for b in writes:
            if b.lw is not None:
                k, v = b.lw
                if k != e:
                    need(k, v)
            for k, v in b.rd.items():
                if k != e:
                    need(k, v)
        waits = list(pend.items())
        if emit:
            for k, v in waits:
                self._wait(e, k, v)
            return []
        for k, v in waits:
            seen[k] = v
        return waits

    def op(self, e, ins_fn, reads=(), writes=(), signal=True):
        xr = [b for b in reads if b.excl]
        if xr:
            reads = [b for b in reads if not b.excl]
            writes = list(writes) + xr
        if e in ("act", "dve", "pool"):
            waits = self.deps(e, reads, writes, emit=False)
            for k, v in waits[:-1]:
                self.eng[e].wait_ge(self.sems[k], v)
            ins = ins_fn(self.eng[e])
            if waits:
                k, v = waits[-1]
                ins._wait_ge(self.sems[k], v)
        else:
            self.deps(e, reads, writes)
            ins = ins_fn(self.eng[e])
        n = self.cnt[e] + 1
        if signal:
            ins.then_inc(self.sems[e], 1)
            self.cnt[e] = n
        for b in writes:
            b.lw = (e, n)
            b.rd = {}
        for b in reads:
            if b.rd.get(e, 0) < n:
                b.rd[e] = n
        return ins

    def dma(self, q, key, out, in_, reads=(), writes=()):
        if key not in self.sems:
            self.sems[key] = self.nc.alloc_semaphore("d_" + key)
            self.cnt[key] = 0
        self.deps(q, reads, writes)
        ins = self.eng[q].dma_start(out=out, in_=in_)
        v = self.cnt[key] + 16
        ins.then_inc(self.sems[key], 16)
        self.cnt[key] = v
        for b in writes:
            b.lw = (key, v)
            b.rd = {}
        for b in reads:
            if not b.norec:
                b.rd[key] = v
        return ins


class _Stop(Exception):
    pass


def build_nc(nt=NT, nl=NL, dbg=None):
    nc = bass.Bass("TRN2", target_bir_lowering=False)
    dt_in = lambda name, shape: nc.dram_tensor(name, shape, F32, kind="ExternalInput").ap()
    xT_d = dt_in("xT", [D, S])
    pv_d = dt_in("pv", [128, NPV])
    bif_d = dt_in("bif", [1, NL * 8])
    w_ada_d = dt_in("w_ada", [NL, D, 3 * D])
    w_in_d = dt_in("w_in", [NL, D, 5 * D])
    w_a_d = dt_in("rg_w_a", [NL, 8, 128, 128])
    w_x_d = dt_in("rg_w_x", [NL, 8, 128, 128])
    w_q_d = dt_in("ml_w_q", [NL, 4, 256, 256])
    w_k_d = dt_in("ml_w_k", [NL, 4, 256, 256])
    w_v_d = dt_in("ml_w_v", [NL, 4, 256, 256])
    w_if_d = dt_in("ml_w_if", [NL, 3 * D, 8])
    w_out_d = dt_in("w_out", [NL, 2 * D, D])
    tri_d = dt_in("tri", [128, 128])
    oT_d = nc.dram_tensor("oT", [D, S], F32, kind="ExternalOutput").ap()

    fw = FW(nc)
    with contextlib.ExitStack() as es:
        def SB(name, shape, dt):
            return es.enter_context(nc.sbuf_tensor(name, shape, dt))

        def PS(name):
            return es.enter_context(nc.psum_tensor(name, [128, 512], F32))

        ring = [SB("ring%d" % i, [128, SLOT], BF16) for i in range(NSLOT)]
        ring_b = [Buf("ring%d" % i) for i in range(NSLOT)]
        xT = SB("xT_sb", [128, 8, T], F32)
        xT_b = [Buf("xT%d" % c) for c in range(8)]
        sq = SB("sq", [128, 8, T], BF16)
        sq_b = [Buf("sq%d" % c) for c in range(8)]
        hb = SB("hb", [128, 8, T], BF16)
        hb_b = [Buf("hb%d" % c) for c in range(8)]
        rstd = SB("rstd", [128, T], F32)
        rstd_b = Buf("rstd")
        lnr = SB("lnr", [128, T], F32)
        lnr_b = Buf("lnr")
        rgx = SB("rgx", [128, 8, 4 + T], BF16)
        rgx_b = [Buf("rgx%d" % c) for c in range(8)]
        mlx = SB("mlx", [128, 8, 4 + T], BF16)
        mlx_b = [Buf("mlx%d" % c) for c in range(8)]
        yT = SB("yT", [128, 16, T], BF16)
        yT_b = [Buf("yT%d" % c) for c in range(16)]
        mxc = SB("mxc", [128, 8, T], BF16)
        mxc_b = [Buf("mxc%d" % c) for c in range(8)]
        qkv = SB("qkv", [128, 24, T], BF16)
        qkv_b = [Buf("qkv%d" % c) for c in range(24)]
        sgo = SB("sgo", [128, 8, T], F32)
        sgo_b = [Buf("sgo%d" % c) for c in range(8)]
        zsm = SB("zsm", [128, 8, T], F32)
        zsm_b = [Buf("zsm%d" % c) for c in range(8)]
        zsr = SB("zsr", [128, 8, T], F32)
        zsr_b = [Buf("zsr%d" % c) for c in range(8)]
        cell = SB("cell", [128, 8, T], F32)
        cell_b = [Buf("cell%d" % c) for c in range(8)]
        NTMP = 4
        tmp = [[SB("tmp%d_%d" % (s, i), [128, T], F32) for i in range(NTMP)] for s in range(2)]
        tmp_b = [[Buf("tmp%d_%d" % (s, i)) for i in range(NTMP)] for s in range(2)]
        NB = 4
        bX = SB("bX", [128, NB, T], F32)
        bR = SB("bR", [128, NB, T], F32)
        bI = SB("bI", [128, NB, T], F32)
        bA = SB("bA", [128, NB, T], F32)
        bM = SB("bM", [128, NB, T], F32)
        bX_b = [Buf("bX%d" % i) for i in range(NB)]
        bR_b = [Buf("bR%d" % i) for i in range(NB)]
        bI_b = [Buf("bI%d" % i) for i in range(NB)]
        bA_b = [Buf("bA%d" % i) for i in range(NB)]
        bM_b = [Buf("bM%d" % i) for i in range(NB)]
        xcb = SB("xcb", [128, NB, T], BF16)
        xcb_b = [Buf("xcb%d" % i) for i in range(NB)]
        dg = [SB("dg%d" % s, [128, 4, 128], BF16) for s in range(NB)]
        dg_b = [Buf("dg%d" % s) for s in range(NB)]
        eb16 = SB("eb16", [128, 4, 128], F32)
        eb16_b = Buf("eb16")
        lftri = SB("lftri", [128, 4, 128], F32)
        lftri_b = Buf("lftri")
        wT = [SB("wT%d" % s, [128, 128], F32) for s in range(2)]
        wT_b = [Buf("wT%d" % s) for s in range(2)]
        pT = [SB("pT%d" % s, [128, 128], BF16) for s in range(2)]
        pT_b = [Buf("pT%d" % s) for s in range(2)]
        qs = [SB("qs%d" % s, [128, 2, 128], BF16) for s in range(2)]
        qs_b = [Buf("qs%d" % s) for s in range(2)]
        vx = [SB("vx%d" % s, [128, 258], BF16) for s in range(2)]
        vx_b = [Buf("vx%d" % s) for s in range(2)]
        kw = [SB("kw%d" % s, [128, 256], BF16) for s in range(2)]
        kw_b = [Buf("kw%d" % s) for s in range(2)]
        rec = [SB("rec%d" % s, [128, 128], F32) for s in range(2)]
        rec_b = [Buf("rec%d" % s) for s in range(2)]
        nbc = [SB("nbc%d" % s, [128, 2, 128], BF16) for s in range(2)]
        nbc_b = [Buf("nbc%d" % s) for s in range(2)]
        gsb = SB("gsb", [128, NCH, 8], F32)
        gsb_b = Buf("gsb")
        nlf = SB("nlf", [128, NCH, 4], F32)
        nlf_b = Buf("nlf")
        ctm = SB("ctm", [128, NCH, 4], F32)
        ctm_b = Buf("ctm")
        ec = SB("ec", [128, NCH, 4], F32)
        ec_b = Buf("ec")
        wst = SB("wst", [128, NCH, 4], F32)
        wst_b = Buf("wst")
        dec = SB("dec", [128, NCH, 4], F32)
        dec_b = Buf("dec")
        halo_rg = [SB("halo_rg%d" % l, [128, 8, 3], BF16) for l in range(NL)]
        halo_ml = [SB("halo_ml%d" % l, [128, 8, 3], BF16) for l in range(NL)]
        halo_rg_b = [Buf("halo_rg%d" % l) for l in range(NL)]
        halo_ml_b = [Buf("halo_ml%d" % l) for l in range(NL)]
        rgh = [SB("rgh%d" % l, [128, 8], F32) for l in range(NL)]
        rgh_b = [[Buf("rgh%d_%d" % (l, c)) for c in range(8)] for l in range(NL)]
        C32 = [SB("C32_%d" % l, [128, 4, 2, 258], F32) for l in range(NL)]
        Cb = [SB("Cb_%d" % l, [128, 4, 2, 258], BF16) for l in range(NL)]
        C32_b = [[Buf("C32_%d_%d" % (l, h)) for h in range(4)] for l in range(NL)]
        Cb_b = [[Buf("Cb_%d_%d" % (l, h)) for h in range(4)] for l in range(NL)]
        ident = SB("ident", [128, 128], BF16)
        ones_b = SB("ones_b", [128, 128], BF16)
        inv1024 = SB("inv1024", [128, 128], BF16)
        inv256 = SB("inv256", [128, 128], BF16)
        tri32 = SB("tri_sb", [128, 128], F32)
        ones32 = SB("ones32", [128, 128], F32)
        pv = SB("pv_sb", [128, NPV], F32)
        dv = SB("dv", [128, NL, 64], F32)
        modT = SB("modT", [128, NL, 24], F32)
        bifb = SB("bif_sb", [128, NL * 8], F32)
        cact = SB("cact", [128, 8], F32)
        ctmp = SB("ctmp", [128, 8], F32)
        const_b = Buf("consts")
        par_b = Buf("params")
        psg = [PS("psg%d" % i) for i in range(3)]
        psg_b = [Buf("psg%d" % i, True) for i in range(3)]
        psS = PS("psS")
        psS_b = Buf("psS_S", True)
        psND = PS("psND")
        psND_b = Buf("psND", True)
        psVK = PS("psVK")
        psV_b = Buf("psV", True)
        psC = PS("psC")
        psC_b = Buf("psC", True)
        psM = PS("psM")
        psM_b = Buf("psM", True)

        nc._sbuf_left = nc.sbuf_bytes_remaining
        gctr = [0]

        def gen_bank():
            i = gctr[0] % 3
            gctr[0] += 1
            return psg[i], psg_b[i]

        def ACT(out, in_, func, reads, writes, bias=None, scale=None):
            kw_ = {}
            if bias is not None:
                kw_["bias"] = bias
            if scale is not None:
                kw_["scale"] = scale
            return fw.op("act", lambda e: e.activation(out=out, in_=in_, func=func, **kw_), reads, writes)

        def TT(eng, out, in0, in1, op, reads, writes):
            return fw.op(eng, lambda e: e.tensor_tensor(out=out, in0=in0, in1=in1, op=op), reads, writes)

        def TS(eng, out, in0, s1, s2, op0, op1, reads, writes):
            if s2 is None:
                return fw.op(eng, lambda e: e.tensor_scalar(out=out, in0=in0, scalar1=s1, scalar2=None, op0=op0), reads, writes)
            return fw.op(eng, lambda e: e.tensor_scalar(out=out, in0=in0, scalar1=s1, scalar2=s2, op0=op0, op1=op1), reads, writes)

        def STT(out, in0, scalar, in1, op0, op1, reads, writes):
            return fw.op("dve", lambda e: e.scalar_tensor_tensor(out=out, in0=in0, scalar=scalar, in1=in1, op0=op0, op1=op1), reads, writes)

        def CP(eng, out, in_, reads, writes):
            if eng == "act":
                return fw.op("act", lambda e: e.copy(out=out, in_=in_), reads, writes)
            return fw.op(eng, lambda e: e.tensor_copy(out=out, in_=in_), reads, writes)

        def RECIP(out, in_, reads, writes):
            return fw.op("dve", lambda e: e.reciprocal(out=out, in_=in_), reads, writes)

        def MM(out, lhsT, rhs, start, stop, reads, writes, signal=True):
            return fw.op("pe", lambda e: e.matmul(out, lhsT=lhsT, rhs=rhs, start=start, stop=stop), reads, writes, signal=signal)

        def MSET(eng, ap, val, writes):
            return fw.op(eng, lambda e: e.memset(ap, val), (), writes)

        MSET("pool", ones_b[:], 1.0, [const_b])
        MSET("pool", inv1024[:], 1.0 / 1024.0, [const_b])
        MSET("pool", inv256[:], 1.0 / 256.0, [const_b])
        MSET("pool", ones32[:], 1.0, [const_b])
        for s in range(2):
            MSET("pool", vx[s][:], 1.0, [vx_b[s]])
        for l in range(NL):
            MSET("pool", halo_rg[l][:], 0.0, [halo_rg_b[l]])
            MSET("pool", halo_ml[l][:], 0.0, [halo_ml_b[l]])
            MSET("pool", rgh[l][:], 0.0, rgh_b[l])
            MSET("pool", C32[l][:], 0.0, C32_b[l])
            MSET("pool", Cb[l][:], 0.0, Cb_b[l])
        fw.dma("sp", "c_tri", tri32[:], tri_d, writes=[const_b])
        fw.dma("sp", "c_pv", pv[:], pv_d, writes=[par_b])
        fw.dma("sp", "c_bif", bifb[:], bif_d.partition_broadcast(128), writes=[par_b])
        CP("dve", ident[:, 0:1], tri32[:, 0:1], [const_b], [const_b])
        TT("dve", ident[:, 1:128], tri32[:, 1:128], tri32[:, 0:127], ALU.subtract, [const_b], [const_b])

        def pcol(l, off, n=8):
            b = l * LP + off
            return pv[:, b:b + n]

        O_NG, O_RCW, O_RCB, O_BA, O_BX, O_LAM, O_MCW, O_MCB, O_MNG, O_BADA = 0, 8, 40, 48, 56, 64, 72, 104, 112, 120
        O_FG = NL * LP
        O_CT = NL * LP + 8
        for l in range(NL):
            TS("dve", dv[:, l, 0:8], pcol(l, O_BA), -1.0, None, ALU.mult, None, [par_b], [par_b])
            TS("dve", dv[:, l, 8:16], pcol(l, O_BX), -1.0, None, ALU.mult, None, [par_b], [par_b])
            TS("dve", dv[:, l, 16:24], pcol(l, O_MCB), -1.0, None, ALU.mult, None, [par_b], [par_b])
            ACT(dv[:, l, 48:56], pcol(l, O_LAM), AF.Exp, [par_b], [par_b], scale=-1.0)
            ACT(dv[:, l, 48:56], dv[:, l, 48:56], AF.Ln, [par_b], [par_b], bias=1.0)
            TS("dve", dv[:, l, 24:32], dv[:, l, 48:56], -8.0, None, ALU.mult, None, [par_b], [par_b])
            TS("dve", dv[:, l, 32:40], dv[:, l, 48:56], -16.0, None, ALU.mult, None, [par_b], [par_b])
        ACT(ctmp[:], pv[:, O_CT:O_CT + 8], AF.Exp, [par_b], [par_b], scale=-1.0)
        TS("dve", ctmp[:], ctmp[:], 1.0, None, ALU.add, None, [par_b], [par_b])
        RECIP(ctmp[:], ctmp[:], [par_b], [par_b])
        TT("dve", cact[:], pv[:, O_CT:O_CT + 8], ctmp[:], ALU.mult, [par_b], [par_b])
        slot_ctr = [0]

        def next_slot():
            i = slot_ctr[0] % NSLOT
            slot_ctr[0] += 1
            return i

        for l in range(NL):
            for pi in range(6):
                si = next_slot()
                r32 = ring[si][:].bitcast(F32)
                dst = r32[:, 0:4096].rearrange("p (kc n) -> p kc n", kc=8)
                src = w_ada_d[l].rearrange("(kc p) n -> p kc n", p=128)[:, :, pi * 512:(pi + 1) * 512]
                fw.dma("sp", "ada%d" % si, dst, src, writes=[ring_b[si]])
                for mm_ in range(4):
                    m = pi * 4 + mm_
                    for kc in range(8):
                        MM(psM[:, m:m + 1], dst[:, kc, mm_ * 128:(mm_ + 1) * 128], cact[:, kc:kc + 1], kc == 0, kc == 7,
                           [ring_b[si], par_b], [psM_b], signal=(kc == 7))
            TT("dve", modT[:, l, :], psM[:, 0:24], pcol(l, O_BADA, 24), ALU.add, [psM_b, par_b], [par_b])
            STT(dv[:, l, 40:48], modT[:, l, 8:16], 1.0, pcol(l, O_NG), ALU.add, ALU.mult, [par_b], [par_b])

        def load_piece(si, l, piece):
            key = "ring%d" % si
            r = ring[si]
            if piece < 5:
                dst = r[:, 0:8192].rearrange("p (kc n) -> p kc n", kc=8)
                src = w_in_d[l].rearrange("(kc p) n -> p kc n", p=128)[:, :, piece * 1024:(piece + 1) * 1024]
                fw.dma("pool", key, dst, src, reads=[xl_b], writes=[ring_b[si]])
            elif piece == 5:
                fw.dma("pool", key, r[:, 0:1024].rearrange("p (h e) -> p h e", h=8),
                       w_a_d[l].rearrange("h d e -> d h e"), reads=[xl_b], writes=[ring_b[si]])
                fw.dma("pool", key, r[:, 1024:2048].rearrange("p (h e) -> p h e", h=8),
                       w_x_d[l].rearrange("h d e -> d h e"), writes=[])
                for wi, wd in enumerate((w_q_d, w_k_d, w_v_d)):
                    for h in range(4):
                        o = 2048 + wi * 2048 + h * 512
                        fw.dma("pool", key, r[:, o:o + 512].rearrange("p (dc e) -> p dc e", dc=2),
                               wd[l, h].rearrange("(dc p) e -> p dc e", p=128), writes=[])
                fw.dma("pool", key, r[:, 8192:8384].rearrange("p (kc g) -> p kc g", kc=24),
                       w_if_d[l].rearrange("(kc p) g -> p kc g", p=128), writes=[])
                ring_b[si].lw = (key, fw.cnt[key])
            else:
                hf = piece - 6
                dst = r[:, 0:8192].rearrange("p (kc n) -> p kc n", kc=8)
                src = w_out_d[l].rearrange("(kc p) n -> p kc n", p=128)[:, hf * 8:(hf + 1) * 8, :]
                fw.dma("pool", key, dst, src, reads=[xl_b], writes=[ring_b[si]])

        steps = [(ti, l) for ti in range(nt) for l in range(nl)]
        ORDER = [0, 1, 2, 5, 3, 4, 6, 7]
        piece_seq = [(si_, p) for si_ in range(len(steps)) for p in ORDER]
        slot_of = {}
        free_slots = list(range(NSLOT))
        next_load = [0]

        xl_b = Buf("xload_done")
        xl_b.norec = True
        deferred_rel = []

        def pump():
            while next_load[0] < len(piece_seq) and free_slots:
                key_ = piece_seq[next_load[0]]
                si = free_slots.pop(0)
                load_piece(si, steps[key_[0]][1], key_[1])
                slot_of[key_] = si
                next_load[0] += 1

        def get_piece(sidx, p):
            pump()
            assert (sidx, p) in slot_of, ("ring deadlock", sidx, p)
            si = slot_of[(sidx, p)]
            return ring[si], ring_b[si]

        def release(sidx, p):
            si = slot_of.pop((sidx, p))
            free_slots.append(si)
            pump()

        def dump(srcs):
            MSET("dve", cell[:], 0.0, cell_b)
            for (dst, ap_, bufs) in srcs:
                CP("dve", dst, ap_, bufs, cell_b)
            fw.dma("sp", "ostore", oT_d.rearrange("(c p) t -> p c t", p=128)[:, :, 0:T], cell[:], reads=cell_b, writes=[])
            raise _Stop()

        try:
          if dbg == 0:
            dump([(cell[:, 0, 0:64], dv[:, 0, :], [par_b]), (cell[:, 1, 0:24], modT[:, 0, :], [par_b]),
                  (cell[:, 2, 0:128], ident[:], [const_b]), (cell[:, 3, 0:16], bifb[:], [par_b])])
          out_sem_keys = []
          for sidx, (ti, l) in enumerate(steps):
              t0 = ti * T
              if l == 0:
                  xsrc = xT_d.rearrange("(c p) t -> p c t", p=128)
                  fw.dma("sp", "xloadA", xT[:, 0:4, :], xsrc[:, 0:4, t0:t0 + T], reads=[], writes=xT_b[0:4] + [xl_b])
                  fw.dma("sp", "xloadB", xT[:, 4:8, :], xsrc[:, 4:8, t0:t0 + T], reads=[], writes=xT_b[4:8] + [xl_b])
              for (rs_, rp_) in deferred_rel:
                  release(rs_, rp_)
              del deferred_rel[:]
              if l == 0:
                  for kc in range(8):
                      ACT(sq[:, kc, :], xT[:, kc, :], AF.Square, [xT_b[kc]], [sq_b[kc]])
                      MM(psM[:, 128:128 + T], inv1024[:], sq[:, kc, :], kc == 0, kc == 7, [const_b, sq_b[kc]], [psM_b], signal=(kc == 7))
              ACT(lnr[:], psM[:, 128:128 + T], AF.Ln, [psM_b], [lnr_b], bias=EPS)
              ACT(rstd[:], lnr[:], AF.Exp, [lnr_b], [rstd_b], scale=-0.5)
              for c in range(8):
                  s = c % 2
                  t_ = tmp[s][0]
                  STT(t_[:], xT[:, c, :], dv[:, l, 40 + c:41 + c], rstd[:], ALU.mult, ALU.mult,
                      [xT_b[c], rstd_b, par_b], [tmp_b[s][0]])
                  ACT(hb[:, c, :], t_[:], AF.Identity, [tmp_b[s][0], par_b], [hb_b[c]], bias=modT[:, l, c:c + 1])

              if dbg == 1 and sidx == len(steps) - 1:
                  dump([(cell[:], hb[:], hb_b)])
              def win_chunk(piece, c):
                  r, rb = get_piece(sidx, piece)
                  w3 = r[:, 0:8192].rearrange("p (kc n) -> p kc n", kc=8)
                  pb_, pbb_ = gen_bank()
                  for kc in range(8):
                      MM(pb_[:, 0:T], w3[:, kc, c * 128:(c + 1) * 128], hb[:, kc, :], kc == 0, kc == 7,
                         [rb, hb_b[kc]], [pbb_], signal=(kc == 7))
                  return pb_, pbb_

              def sigmoid_from(out, src, reads, wbuf, nbias=None):
                  if nbias is None:
                      ACT(out, src, AF.Exp, reads, [wbuf], scale=-1.0)
                  else:
                      ACT(out, src, AF.Exp, reads + [par_b], [wbuf], scale=-1.0, bias=nbias)
                  ACT(out, out, AF.Ln, [wbuf], [wbuf], bias=1.0)
                  ACT(out, out, AF.Exp, [wbuf], [wbuf], scale=-1.0)

              CP("pool", rgx[:, :, 1:4], halo_rg[l][:], [halo_rg_b[l]], rgx_b)
              CP("pool", mlx[:, :, 1:4], halo_ml[l][:], [halo_ml_b[l]], mlx_b)

              smallw = [None]

              def get_small():
                  if smallw[0] is None:
                      smallw[0] = get_piece(sidx, 5)
                  return smallw[0]

              for c in range(8):
                  pb, pbb = win_chunk(0, c)
                  CP("act", rgx[:, c, 4:4 + T], pb[:, 0:T], [pbb], [rgx_b[c]])
              CP("pool", halo_rg[l][:], rgx[:, :, T + 1:T + 4], rgx_b, [halo_rg_b[l]])
              release(sidx, 0)
              if dbg == 2 and sidx == len(steps) - 1:
                  dump([(cell[:], rgx[:, :, 4:4 + T], rgx_b)])
              for c in range(8):
                  pb, pbb = win_chunk(1, c)
                  s = c % 2
                  sz = tmp[s][1]
                  sigmoid_from(sz[:], pb[:, 0:T], [pbb], tmp_b[s][1])
                  TT("dve", zsr[:, c, :], pb[:, 0:T], sz[:], ALU.mult, [pbb, tmp_b[s][1]], [zsr_b[c]])
              release(sidx, 1)
              for c in range(8):
                  pb, pbb = win_chunk(2, c)
                  CP("act", mlx[:, c, 4:4 + T], pb[:, 0:T], [pbb], [mlx_b[c]])
              CP("pool", halo_ml[l][:], mlx[:, :, T + 1:T + 4], mlx_b, [halo_ml_b[l]])
              release(sidx, 2)

              sw, swb = get_small()
              w_a3 = sw[:, 0:1024].rearrange("p (h e) -> p h e", h=8)
              w_x3 = sw[:, 1024:2048].rearrange("p (h e) -> p h e", h=8)
              w_q4 = sw[:, 2048:4096].rearrange("p (h dc e) -> p h dc e", h=4, dc=2)
              w_k4 = sw[:, 4096:6144].rearrange("p (h dc e) -> p h dc e", h=4, dc=2)
              w_v4 = sw[:, 6144:8192].rearrange("p (h dc e) -> p h dc e", h=4, dc=2)
              w_if3 = sw[:, 8192:8384].rearrange("p (kc g) -> p kc g", kc=24)

              def diag_build(ci, c, wcol_off):
                  for k in range(4):
                      col = l * LP + wcol_off + k * 8 + c
                      if k < 2:
                          ACT(dg[ci][:, k, :], ident[:], AF.Identity, [const_b, par_b], [dg_b[ci]], scale=pv[:, col:col + 1])
                      else:
                          TS("dve", dg[ci][:, k, :], ident[:], pv[:, col:col + 1], None, ALU.mult, None,
                             [const_b, par_b], [dg_b[ci]])

              def conv_mm(ci, c, src, src_b):
                  pb_, pbb_ = gen_bank()
                  for k in range(4):
                      MM(pb_[:, 0:T], dg[ci][:, k, :], src[:, c, 1 + k:1 + k + T], k == 0, k == 3, [dg_b[ci], src_b[c]], [pbb_],
                         signal=(k == 3))
                  return pb_, pbb_

              for bt in range(2):
                  cs = [bt * NB + ci for ci in range(NB)]
                  for ci, c in enumerate(cs):
                      diag_build(ci, c, O_MCW)
                  for ci, c in enumerate(cs):
                      pb, pbb = conv_mm(ci, c, mlx, mlx_b)
                      TS("dve", bR[:, ci, :], pb[:, 0:T], pcol(l, O_MCB)[:, c:c + 1], None, ALU.add, None, [pbb, par_b], [bR_b[ci]])
                  for hf in range(2):
                      sl = slice(2 * hf, 2 * hf + 2)
                      ACT(bX[:, sl, :], bR[:, sl, :], AF.Exp, bR_b[sl], bX_b[sl], scale=-1.0)
                  for hf in range(2):
                      sl = slice(2 * hf, 2 * hf + 2)
                      ACT(bX[:, sl, :], bX[:, sl, :], AF.Ln, bX_b[sl], bX_b[sl], bias=1.0)
                  for hf in range(2):
                      sl = slice(2 * hf, 2 * hf + 2)
                      ACT(bX[:, sl, :], bX[:, sl, :], AF.Exp, bX_b[sl], bX_b[sl], scale=-1.0)
                  for hf in range(2):
                      sl = slice(2 * hf, 2 * hf + 2)
                      c0_ = bt * NB + 2 * hf
                      TT("dve", mxc[:, c0_:c0_ + 2, :], bR[:, sl, :], bX[:, sl, :], ALU.mult, bR_b[sl] + bX_b[sl], mxc_b[c0_:c0_ + 2])
              if dbg == 4 and sidx == len(steps) - 1:
                  dump([(cell[:], mxc[:], mxc_b)])
              for wi, (w4, src, src_b, off) in enumerate(((w_q4, mxc, mxc_b, 0), (w_k4, mxc, mxc_b, 0), (w_v4, mlx, mlx_b, 4))):
                  for h in range(4):
                      for ecx in range(2):
                          pb, pbb = gen_bank()
                          for dc in range(2):
                              MM(pb[:, 0:T], w4[:, h, dc, ecx * 128:(ecx + 1) * 128], src[:, 2 * h + dc, off:off + T],
                                 dc == 0, dc == 1, [swb, src_b[2 * h + dc]], [pbb], signal=(dc == 1))
                          oc = wi * 8 + 2 * h + ecx
                          eng = "act" if (oc % 2 == 0) else "dve"
                          CP(eng, qkv[:, oc, :], pb[:, 0:T], [pbb], [qkv_b[oc]])
              for j in range(NCH):
                  for kc in range(24):
                      MM(psM[:, 32 + j * 8:32 + (j + 1) * 8], qkv[:, kc, j * 128:(j + 1) * 128], w_if3[:, kc, :], kc == 0, kc == 23,
                         [qkv_b[kc], swb], [psM_b], signal=(kc == 23))
              g_ps = psM[:, 32:32 + NCH * 8].rearrange("p (j g) -> p j g", j=NCH)
              TT("dve", gsb[:], g_ps, bifb[:, l * 8:(l + 1) * 8].unsqueeze(1).broadcast_to([128, NCH, 8]), ALU.add,
                 [psM_b, par_b], [gsb_b])
              if dbg == 5 and sidx == len(steps) - 1:
                  dump([(cell[:, 0, 0:NCH * 8], gsb[:].rearrange("p j g -> p (j g)"), [gsb_b])])
              ACT(nlf[:], gsb[:, :, 4:8], AF.Exp, [gsb_b], [nlf_b], scale=-1.0)
              ACT(nlf[:], nlf[:], AF.Ln, [nlf_b], [nlf_b], bias=1.0)
              nlf2 = nlf[:].rearrange("p j h -> p (j h)")
              MM(psM[:, 64:64 + NCH * 4], tri32[:], nlf2, True, True, [const_b, nlf_b], [psM_b])
              MM(psM[:, 96:96 + NCH * 4], ones32[:], nlf2, True, True, [const_b, nlf_b], [psM_b])
              nb_ps = psM[:, 64:64 + NCH * 4].rearrange("p (j h) -> p j h", j=NCH)
              nbl_ps = psM[:, 96:96 + NCH * 4].rearrange("p (j h) -> p j h", j=NCH)
              TT("dve", ctm[:], gsb[:, :, 0:4], nb_ps, ALU.add, [gsb_b, psM_b], [ctm_b])
              ACT(ec[:], ctm[:], AF.Exp, [ctm_b], [ec_b])
              TT("dve", wst[:], ctm[:], nbl_ps, ALU.subtract, [ctm_b, psM_b], [wst_b])
              ACT(wst[:], wst[:], AF.Exp, [wst_b], [wst_b])
              ACT(dec[:], nbl_ps, AF.Exp, [psM_b], [dec_b], scale=-1.0)

              kbank = {}

              def ml_prep(j):
                  for h in range(4):
                      TS("pool", lftri[:, h, :], tri32[:], nlf[:, j, h:h + 1], -1.0, ALU.mult, ALU.mult,
                         [const_b, nlf_b], [lftri_b])
                  pbb_, pbbb_ = gen_bank()
                  MM(pbb_[:, 0:512], ones32[:], lftri[:].rearrange("p h t -> p (h t)"), True, True, [const_b, lftri_b], [pbbb_])
                  ACT(eb16[:].rearrange("p h t -> p (h t)"), pbb_[:, 0:512], AF.Exp, [pbbb_], [eb16_b], bias=-math.log(16.0))

              def ml_A1(idx):
                  j, h = idx // 4, idx % 4
                  ts_ = slice(j * 128, (j + 1) * 128)
                  if h == 0:
                      ml_prep(j)
                  q0 = 2 * h
                  k0 = 8 + 2 * h
                  for dc in range(2):
                      MM(psS[:, 0:128], qkv[:, k0 + dc, ts_], qkv[:, q0 + dc, ts_], dc == 0, dc == 1,
                         [qkv_b[k0 + dc], qkv_b[q0 + dc]], [psS_b], signal=(dc == 1))
                  for dc in range(2):
                      MM(psVK[:, 0:256], mlx[:, 2 * h + dc, 4 + j * 128:4 + (j + 1) * 128], w_v4[:, h, dc, :], dc == 0, dc == 1,
                         [mlx_b[2 * h + dc], swb], [psV_b], signal=(dc == 1))
                  for dc in range(2):
                      MM(psVK[:, 256:512], mxc[:, 2 * h + dc, ts_], w_k4[:, h, dc, :], dc == 0, dc == 1,
                         [mxc_b[2 * h + dc], swb], [psV_b], signal=(dc == 1))

              def ml_A2(idx):
                  j, h = idx // 4, idx % 4
                  ts_ = slice(j * 128, (j + 1) * 128)
                  s = idx % 2
                  q0 = 2 * h
                  STT(wT[s][:], eb16[:, h, :], ec[:, j, h:h + 1], tri32[:], ALU.mult, ALU.mult,
                      [eb16_b, ec_b, const_b], [wT_b[s]])
                  CP("act", vx[s][:, 0:256], psVK[:, 0:256], [psV_b], [vx_b[s]])
                  TT("pool", qs[s][:], qkv[:, q0:q0 + 2, ts_], eb16[:, h, :].unsqueeze(1).broadcast_to([128, 2, 128]),
                     ALU.mult, [qkv_b[q0], qkv_b[q0 + 1], eb16_b], [qs_b[s]])
                  TS("dve", kw[s][:], psVK[:, 256:512], wst[:, j, h:h + 1], None, ALU.mult, None, [psV_b, wst_b], [kw_b[s]])
                  TT("dve", pT[s][:], psS[:, 0:128], wT[s][:], ALU.mult, [psS_b, wT_b[s]], [pT_b[s]])
                  CP("pool", nbc[s][:], Cb[l][:, h, :, 256:257].broadcast_to([128, 2, 128]), [Cb_b[l][h]], [nbc_b[s]])

              def ml_B1(idx):
                  j, h = idx // 4, idx % 4
                  s = idx % 2
                  for vc in range(2):
                      o = vc * 128
                      MM(psND[:, o:o + 128], vx[s][:, vc * 128:(vc + 1) * 128], pT[s][:], True, False,
                         [vx_b[s], pT_b[s]], [psND_b], signal=False)
                      for dc in range(2):
                          MM(psND[:, o:o + 128], Cb[l][:, h, dc, vc * 128:(vc + 1) * 128], qs[s][:, dc, :], False, dc == 1,
                             [Cb_b[l][h], qs_b[s]], [psND_b], signal=False)
                  MM(psND[:, 256:384], ones_b[:], pT[s][:], True, False, [const_b, pT_b[s]], [psND_b], signal=False)
                  for dc in range(2):
                      MM(psND[:, 256:384], nbc[s][:, dc, :], qs[s][:, dc, :], False, dc == 1, [nbc_b[s], qs_b[s]], [psND_b],
                         signal=False)
                  for dc in range(2):
                      MM(psND[:, 384 + 2 * dc:386 + 2 * dc], kw[s][:, dc * 128:(dc + 1) * 128], vx[s][:, 256:258], True, True,
                         [kw_b[s], vx_b[s]], [psND_b], signal=(dc == 1))
                  for dc in range(2):
                      MM(psC[:, dc * 256:(dc + 1) * 256], kw[s][:, dc * 128:(dc + 1) * 128], vx[s][:, 0:256], True, True,
                         [kw_b[s], vx_b[s]], [psC_b], signal=(dc == 1))

              def ml_B2(idx):
                  j, h = idx // 4, idx % 4
                  ts_ = slice(j * 128, (j + 1) * 128)
                  s = idx % 2
                  ACT(rec[s][:], psND[:, 256:384], AF.Abs, [psND_b], [rec_b[s]])
                  STT(C32[l][:, h, :, 0:256], C32[l][:, h, :, 0:256], dec[:, j, h:h + 1],
                      psC[:, 0:512].rearrange("p (dc e) -> p dc e", dc=2), ALU.mult, ALU.add,
                      [C32_b[l][h], dec_b, psC_b], [C32_b[l][h]])
                  TS("dve", rec[s][:], rec[s][:], 1.0, None, ALU.max, None, [rec_b[s]], [rec_b[s]])
                  STT(C32[l][:, h, :, 256:258], C32[l][:, h, :, 256:258], dec[:, j, h:h + 1],
                      psND[:, 384:388].rearrange("p (dc e) -> p dc e", dc=2), ALU.mult, ALU.add,
                      [C32_b[l][h], dec_b, psND_b], [C32_b[l][h]])
                  RECIP(rec[s][:], rec[s][:], [rec_b[s]], [rec_b[s]])
                  CP("dve", Cb[l][:, h, :, :], C32[l][:, h, :, :], [C32_b[l][h]], [Cb_b[l][h]])
                  TT("dve", cell[:, 2 * h:2 * h + 2, ts_], psND[:, 0:256].rearrange("p (v t) -> p v t", v=2),
                     rec[s][:].unsqueeze(1).broadcast_to([128, 2, 128]), ALU.mult, [psND_b, rec_b[s]],
                     [cell_b[2 * h], cell_b[2 * h + 1]])

              NIT = NCH * 4
              ml_sched = []
              for i in range(NIT + 1):
                  if i < NIT:
                      ml_sched.append((ml_A1, i))
                  if i >= 1:
                      ml_sched.append((ml_B1, i - 1))
                  if i < NIT:
                      ml_sched.append((ml_A2, i))
                  if i >= 1:
                      ml_sched.append((ml_B2, i - 1))
              ml_pos = [0]

              def ml_emit(n):
                  for _ in range(n):
                      if ml_pos[0] < len(ml_sched):
                          f_, a_ = ml_sched[ml_pos[0]]
                          f_(a_)
                          ml_pos[0] += 1

              fill_pos = [0]

              def fill_pair():
                  k = fill_pos[0]
                  if k >= 8:
                      return
                  fill_pos[0] += 1
                  piece = 3 if k < 4 else 4
                  c0 = (k % 4) * 2
                  banks = []
                  for c in (c0, c0 + 1):
                      banks.append(win_chunk(piece, c))
                  outs = []
                  for ii, c in enumerate((c0, c0 + 1)):
                      pb, pbb = banks[ii]
                      if piece == 3:
                          o_, ob_ = sgo[:, c, :], sgo_b[c]
                      else:
                          o_, ob_ = tmp[ii][1][:], tmp_b[ii][1]
                      outs.append((o_, ob_))
                      ACT(o_, pb[:, 0:T], AF.Exp, [pbb], [ob_], scale=-1.0)
                  for (o_, ob_) in outs:
                      ACT(o_, o_, AF.Ln, [ob_], [ob_], bias=1.0)
                  for (o_, ob_) in outs:
                      ACT(o_, o_, AF.Exp, [ob_], [ob_], scale=-1.0)
                  if piece == 4:
                      for ii, c in enumerate((c0, c0 + 1)):
                          pb, pbb = banks[ii]
                          TT("dve", zsm[:, c, :], pb[:, 0:T], outs[ii][0], ALU.mult, [pbb, outs[ii][1]], [zsm_b[c]])
                  if k == 3:
                      release(sidx, 3)
                  if k == 7:
                      release(sidx, 4)

              per_gap = 1
              for bt in range(2):
                  cs = [bt * NB + ci for ci in range(NB)]
                  for ci, c in enumerate(cs):
                      diag_build(ci, c, O_RCW)
                  ml_emit(per_gap)
                  fill_pair()
                  for ci, c in enumerate(cs):
                      pb, pbb = conv_mm(ci, c, rgx, rgx_b)
                      ACT(bX[:, ci, :], pb[:, 0:T], AF.Identity, [pbb, par_b], [bX_b[ci]], bias=pcol(l, O_RCB)[:, c:c + 1])
                      ml_emit(1)
                  ml_emit(per_gap)
                  fill_pair()
                  for ci, c in enumerate(cs):
                      CP("dve", xcb[:, ci, :], bX[:, ci, :], [bX_b[ci]], [xcb_b[ci]])
                  gb = []
                  for ci, c in enumerate(cs):
                      pr, prb = gen_bank()
                      MM(pr[:, 0:T], w_a3[:, c, :], xcb[:, ci, :], True, True, [swb, xcb_b[ci]], [prb])
                      ACT(bR[:, ci, :], pr[:, 0:T], AF.Exp, [prb, par_b], [bR_b[ci]], scale=-1.0, bias=dv[:, l, c:c + 1])
                      pi_, pib = gen_bank()
                      MM(pi_[:, 0:T], w_x3[:, c, :], xcb[:, ci, :], True, True, [swb, xcb_b[ci]], [pib])
                      ACT(bI[:, ci, :], pi_[:, 0:T], AF.Exp, [pib, par_b], [bI_b[ci]], scale=-1.0, bias=dv[:, l, 8 + c:9 + c])
                      ml_emit(1)
                  ml_emit(per_gap)
                  fill_pair()
                  for hf in range(2):
                      sl = slice(2 * hf, 2 * hf + 2)
                      ACT(bR[:, sl, :], bR[:, sl, :], AF.Ln, bR_b[sl], bR_b[sl], bias=1.0)
                      ACT(bI[:, sl, :], bI[:, sl, :], AF.Ln, bI_b[sl], bI_b[sl], bias=1.0)
                  for hf in range(2):
                      sl = slice(2 * hf, 2 * hf + 2)
                      ACT(bR[:, sl, :], bR[:, sl, :], AF.Exp, bR_b[sl], bR_b[sl], scale=-1.0)
                      ACT(bI[:, sl, :], bI[:, sl, :], AF.Exp, bI_b[sl], bI_b[sl], scale=-1.0)
                  ml_emit(per_gap)
                  fill_pair()
                  for ci, c in enumerate(cs):
                      ACT(bA[:, ci, :], bR[:, ci, :], AF.Exp, [bR_b[ci], par_b], [bA_b[ci]], scale=dv[:, l, 24 + c:25 + c])
                      ACT(bM[:, ci, :], bR[:, ci, :], AF.Exp, [bR_b[ci], par_b], [bM_b[ci]], scale=dv[:, l, 32 + c:33 + c])
                      ml_emit(1)
                  for hf in range(2):
                      sl = slice(2 * hf, 2 * hf + 2)
                      TT("dve", bI[:, sl, :], bI[:, sl, :], bX[:, sl, :], ALU.mult, bI_b[sl] + bX_b[sl], bI_b[sl])
                  ml_emit(per_gap)
                  fill_pair()
                  for hf in range(2):
                      sl = slice(2 * hf, 2 * hf + 2)
                      ACT(bM[:, sl, :], bM[:, sl, :], AF.Ln, bM_b[sl], bM_b[sl], scale=-1.0, bias=1.0)
                  for hf in range(2):
                      sl = slice(2 * hf, 2 * hf + 2)
                      ACT(bM[:, sl, :], bM[:, sl, :], AF.Exp, bM_b[sl], bM_b[sl], scale=0.5)
                  ml_emit(per_gap)
                  fill_pair()
                  for hf in range(2):
                      sl = slice(2 * hf, 2 * hf + 2)
                      TT("dve", bI[:, sl, :], bI[:, sl, :], bM[:, sl, :], ALU.mult, bI_b[sl] + bM_b[sl], bI_b[sl])
                  for ci, c in enumerate(cs):
                      fw.op("dve", lambda e: e.tensor_tensor_scan(out=bM[:, ci, :], data0=bA[:, ci, :], data1=bI[:, ci, :],
                                                                  initial=rgh[l][:, c:c + 1], op0=ALU.mult, op1=ALU.add),
                            [bA_b[ci], bI_b[ci], rgh_b[l][c]], [bM_b[ci]])
                  ml_emit(per_gap)
                  fill_pair()
                  for ci, c in enumerate(cs):
                      CP("act", rgh[l][:, c:c + 1], bM[:, ci, T - 1:T], [bM_b[ci]], [rgh_b[l][c]])
                      TT("dve", yT[:, c, :], bM[:, ci, :], zsr[:, c, :], ALU.mult, [bM_b[ci], zsr_b[c]], [yT_b[c]])
              ml_emit(len(ml_sched))
              while fill_pos[0] < 8:
                  fill_pair()
              release(sidx, 5)
              if dbg == 3 and sidx == len(steps) - 1:
                  dump([(cell[:], yT[:, 0:8, :], yT_b[0:8])])
              if dbg == 6 and sidx == len(steps) - 1:
                  fw.dma("sp", "ostore", oT_d.rearrange("(c p) t -> p c t", p=128)[:, :, 0:T], cell[:], reads=cell_b, writes=[])
                  raise _Stop()
              for h in range(4):
                  hs_ = slice(2 * h, 2 * h + 2)
                  TT("dve", cell[:, hs_, :], cell[:, hs_, :], sgo[:, hs_, :], ALU.mult, cell_b[hs_] + sgo_b[hs_], cell_b[hs_])
                  ACT(sq[:, hs_, :], cell[:, hs_, :], AF.Square, cell_b[hs_], sq_b[hs_])
              for h in range(4):
                  pb, pbb = gen_bank()
                  for dc in range(2):
                      MM(pb[:, 0:T], inv256[:], sq[:, 2 * h + dc, :], dc == 0, dc == 1, [const_b, sq_b[2 * h + dc]], [pbb],
                         signal=(dc == 1))
                  s = h % 2
                  ln_, rs_ = tmp[s][0], tmp[s][1]
                  ACT(ln_[:], pb[:, 0:T], AF.Ln, [pbb], [tmp_b[s][0]], bias=EPS)
                  ACT(rs_[:], ln_[:], AF.Exp, [tmp_b[s][0]], [tmp_b[s][1]], scale=-0.5)
                  for dc in range(2):
                      c = 2 * h + dc
                      t1 = tmp[s][2 + dc]
                      STT(t1[:], cell[:, c, :], pcol(l, O_MNG)[:, c:c + 1], zsm[:, c, :], ALU.mult, ALU.mult,
                          [cell_b[c], par_b, zsm_b[c]], [tmp_b[s][2 + dc]])
                      TT("dve", yT[:, 8 + c, :], t1[:], rs_[:], ALU.mult, [tmp_b[s][2 + dc], tmp_b[s][1]], [yT_b[8 + c]])

              if dbg == 7 and sidx == len(steps) - 1:
                  dump([(cell[:], yT[:, 8:16, :], yT_b[8:16])])
              pieces_o = [get_piece(sidx, 6), get_piece(sidx, 7)]
              for c in range(8):
                  pb, pbb = gen_bank()
                  for kc in range(16):
                      r, rb = pieces_o[kc // 8]
                      w3 = r[:, 0:8192].rearrange("p (kc n) -> p kc n", kc=8)
                      MM(pb[:, 0:T], w3[:, kc % 8, c * 128:(c + 1) * 128], yT[:, kc, :], kc == 0, kc == 15,
                         [rb, yT_b[kc]], [pbb], signal=(kc == 15))
                  if c >= 1:
                      MM(psM[:, 128:128 + T], inv1024[:], sq[:, c - 1, :], c == 1, False, [const_b, sq_b[c - 1]], [psM_b], signal=False)
                  if l == nl - 1 and dbg != 8:
                      STT(cell[:, c, :], pb[:, 0:T], modT[:, l, 16 + c:17 + c], xT[:, c, :], ALU.mult, ALU.add,
                          [pbb, par_b, xT_b[c]], [cell_b[c]])
                      ACT(sq[:, c, :], cell[:, c, :], AF.Square, [cell_b[c]], [sq_b[c]])
                  else:
                      STT(xT[:, c, :], pb[:, 0:T], modT[:, l, 16 + c:17 + c], xT[:, c, :], ALU.mult, ALU.add,
                          [pbb, par_b, xT_b[c]], [xT_b[c]])
                      ACT(sq[:, c, :], xT[:, c, :], AF.Square, [xT_b[c]], [sq_b[c]])
              MM(psM[:, 128:128 + T], inv1024[:], sq[:, 7, :], False, True, [const_b, sq_b[7]], [psM_b])
              if l == nl - 1 and sidx + 1 < len(steps):
                  deferred_rel.extend([(sidx, 6), (sidx, 7)])
              else:
                  release(sidx, 6)
                  release(sidx, 7)

              if dbg == 8 and sidx == len(steps) - 1:
                  dump([(cell[:], xT[:], xT_b)])
              if l == nl - 1:
                  ACT(lnr[:], psM[:, 128:128 + T], AF.Ln, [psM_b], [lnr_b], bias=EPS)
                  ACT(rstd[:], lnr[:], AF.Exp, [lnr_b], [rstd_b], scale=-0.5)
                  for c in range(8):
                      STT(cell[:, c, :], cell[:, c, :], pv[:, O_FG + c:O_FG + c + 1], rstd[:], ALU.mult, ALU.mult,
                          [cell_b[c], par_b, rstd_b], [cell_b[c]])
                  fw.dma("sp", "ostore", oT_d.rearrange("(c p) t -> p c t", p=128)[:, :, t0:t0 + T], cell[:],
                         reads=cell_b, writes=[])
                  if dbg == 9 and sidx == 0:
                      raise _Stop()
        except _Stop:
            pass
        nc._fw_counts = dict(fw.cnt)
        for key_ in list(fw.cnt.keys()):
            if key_ not in fw.eng and fw.cnt[key_] > 0:
                fw._wait("sp", key_, fw.cnt[key_])
    return nc


_NC_CACHE = {}


def _feat(v):
    return np.ascontiguousarray(np.asarray(v, np.float32).reshape(8, 128).T)


def kernel(x, c, norm_g, w_ada, b_ada, w_in, rg_conv_w, rg_conv_b, rg_w_a, rg_b_a, rg_w_x, rg_b_x, rg_lambda,
           ml_conv_w, ml_conv_b, ml_w_q, ml_w_k, ml_w_v, ml_w_if, ml_b_if, ml_norm_g, w_out, final_g):
    f32 = lambda a: np.ascontiguousarray(np.asarray(a, np.float32))
    x = f32(x)
    B = x.shape[0]
    n_cores = 8
    if "nc" not in _NC_CACHE:
        _NC_CACHE["nc"] = build_nc()
    nc = _NC_CACHE["nc"]
    tri = np.triu(np.ones((128, 128), np.float32))
    in_maps = []
    for core in range(n_cores):
        b = core % B
        cols = []
        for l in range(NL):
            cols.append(_feat(norm_g[l]))
            for k in range(4):
                cols.append(_feat(rg_conv_w[l][k]))
            cols.append(_feat(rg_conv_b[l]))
            cols.append(_feat(rg_b_a[l]))
            cols.append(_feat(rg_b_x[l]))
            cols.append(_feat(rg_lambda[l]))
            for k in range(4):
                cols.append(_feat(ml_conv_w[l][k]))
            cols.append(_feat(ml_conv_b[l]))
            cols.append(_feat(ml_norm_g[l]))
            ba = np.asarray(b_ada[l], np.float32)
            for j in range(3):
                cols.append(_feat(ba[j * D:(j + 1) * D]))
        cols.append(_feat(final_g))
        cols.append(_feat(np.asarray(c, np.float32)[b]))
        pvh = np.ascontiguousarray(np.concatenate(cols, axis=1))
        assert pvh.shape == (128, NPV)
        in_maps.append({
            "xT": np.ascontiguousarray(x[b].T),
            "pv": pvh,
            "bif": f32(ml_b_if).reshape(1, NL * 8),
            "w_ada": f32(w_ada), "w_in": f32(w_in), "rg_w_a": f32(rg_w_a), "rg_w_x": f32(rg_w_x),
            "ml_w_q": f32(ml_w_q), "ml_w_k": f32(ml_w_k), "ml_w_v": f32(ml_w_v), "ml_w_if": f32(ml_w_if),
            "w_out": f32(w_out), "tri": tri,
        })
    res = run_bass_kernel_spmd(nc, in_maps, core_ids=list(range(n_cores)))
    out = np.empty((B, S, D), np.float32)
    for b in range(B):
        out[b] = res.results[b]["oT"].T
    return out
```

```python
import contextlib
import math
import numpy as np
import concourse.bass as bass
import concourse.mybir as mybir
from concourse.bass_utils import run_bass_kernel_spmd

F32 = mybir.dt.float32
BF16 = mybir.dt.bfloat16
ALU = mybir.AluOpType
AF = mybir.ActivationFunctionType

D = 1024
S = 4096
NL = 2
T = 256
NT = S // T
NCH = T // 128
EPS = 1e-6
LP = 144
NPV = NL * LP + 16
SLOT = 8448
NSLOT = 3


class Buf:
    __slots__ = ("name", "lw", "rd", "excl", "norec")

    def __init__(self, name, excl=False):
        self.name = name
        self.lw = None
        self.rd = {}
        self.excl = excl
        self.norec = False


class FW:
    def __init__(self, nc):
        self.nc = nc
        self.eng = {"pe": nc.tensor, "act": nc.scalar, "dve": nc.vector, "pool": nc.gpsimd, "sp": nc.sync}
        self.sems = {}
        self.cnt = {}
        self.seen = {e: {} for e in self.eng}
        for e in self.eng:
            self.sems[e] = nc.alloc_semaphore("s_" + e)
            self.cnt[e] = 0

    def _wait(self, e, key, val):
        if self.seen[e].get(key, 0) >= val:
            return
        self.eng[e].wait_ge(self.sems[key], val)
        self.seen[e][key] = val

    def deps(self, e, reads, writes, emit=True):
        pend = {}
        seen = self.seen[e]

        def need(k, v):
            if seen.get(k, 0) >= v:
                return
            if pend.get(k, 0) < v:
                pend[k] = v

        for b in reads:
            if b.lw is not None:
                k, v = b.lw
                if not (k == e and e == "pe"):
                    need(k, v)
        for b in writes:
            if b.lw is not None:
                k, v = b.lw
                if k != e:
                    need(k, v)
            for k, v in b.rd.items():
                if k != e:
                    need(k, v)
        waits = list(pend.items())
        if emit:
            for k, v in waits:
                self._wait(e, k, v)
            return []
        for k, v in waits:
            seen[k] = v
        return waits

    def op(self, e, ins_fn, reads=(), writes=(), signal=True):
        xr = [b for b in reads if b.excl]
        if xr:
            reads = [b for b in reads if not b.excl]
            writes = list(writes) + xr
        if e in ("act", "dve", "pool", "pe"):
            waits = self.deps(e, reads, writes, emit=False)
            for k, v in waits[:-1]:
                self.eng[e].wait_ge(self.sems[k], v)
            ins = ins_fn(self.eng[e])
            if waits:
                k, v = waits[-1]
                ins._wait_ge(self.sems[k], v)
        else:
            self.deps(e, reads, writes)
            ins = ins_fn(self.eng[e])
        n = self.cnt[e] + 1
        if signal:
            ins.then_inc(self.sems[e], 1)
            self.cnt[e] = n
        for b in writes:
            b.lw = (e, n)
            b.rd = {}
        for b in reads:
            if b.rd.get(e, 0) < n:
                b.rd[e] = n
        return ins

    def dma(self, q, key, out, in_, reads=(), writes=()):
        if key not in self.sems:
            self.sems[key] = self.nc.alloc_semaphore("d_" + key)
            self.cnt[key] = 0
        self.deps(q, reads, writes)
        ins = self.eng[q].dma_start(out=out, in_=in_)
        v = self.cnt[key] + 16
        ins.then_inc(self.sems[key], 16)
        self.cnt[key] = v
        for b in writes:
            b.lw = (key, v)
            b.rd = {}
        for b in reads:
            if not b.norec:
                b.rd[key] = v
        return ins


class _Stop(Exception):
    pass


def build_nc(nt=NT, nl=NL, dbg=None):
    nc = bass.Bass("TRN2", target_bir_lowering=False)
    dt_in = lambda name, shape: nc.dram_tensor(name, shape, F32, kind="ExternalInput").ap()
    xT_d = dt_in("xT", [D, S])
    pv_d = dt_in("pv", [128, NPV])
    bif_d = dt_in("bif", [1, NL * 8])
    w_ada_d = dt_in("w_ada", [NL, D, 3 * D])
    w_in_d = dt_in("w_in", [NL, D, 5 * D])
    w_a_d = dt_in("rg_w_a", [NL, 8, 128, 128])
    w_x_d = dt_in("rg_w_x", [NL, 8, 128, 128])
    w_q_d = dt_in("ml_w_q", [NL, 4, 256, 256])
    w_k_d = dt_in("ml_w_k", [NL, 4, 256, 256])
    w_v_d = dt_in("ml_w_v", [NL, 4, 256, 256])
    w_if_d = dt_in("ml_w_if", [NL, 3 * D, 8])
    w_out_d = dt_in("w_out", [NL, 2 * D, D])
    tri_d = dt_in("tri", [128, 128])
    oT_d = nc.dram_tensor("oT", [D, S], F32, kind="ExternalOutput").ap()

    fw = FW(nc)
    with contextlib.ExitStack() as es:
        def SB(name, shape, dt):
            return es.enter_context(nc.sbuf_tensor(name, shape, dt))

        def PS(name):
            return es.enter_context(nc.psum_tensor(name, [128, 512], F32))

        ring = [SB("ring%d" % i, [128, SLOT], BF16) for i in range(NSLOT)]
        ring_b = [Buf("ring%d" % i) for i in range(NSLOT)]
        xT = SB("xT_sb", [128, 8, T], F32)
        xT_b = [Buf("xT%d" % c) for c in range(8)]
        sq = SB("sq", [128, 8, T], BF16)
        sq_b = [Buf("sq%d" % c) for c in range(8)]
        hb = SB("hb", [128, 8, T], BF16)
        hb_b = [Buf("hb%d" % c) for c in range(8)]
        rstd = SB("rstd", [128, T], F32)
        rstd_b = Buf("rstd")
        lnr = SB("lnr", [128, T], F32)
        lnr_b = Buf("lnr")
        rgx = SB("rgx", [128, 8, 4 + T], BF16)
        rgx_b = [Buf("rgx%d" % c) for c in range(8)]
        mlx = SB("mlx", [128, 8, 4 + T], BF16)
        mlx_b = [Buf("mlx%d" % c) for c in range(8)]
        yT = SB("yT", [128, 16, T], BF16)
        yT_b = [Buf("yT%d" % c) for c in range(16)]
        mxc = SB("mxc", [128, 8, T], BF16)
        mxc_b = [Buf("mxc%d" % c) for c in range(8)]
        qkv = SB("qkv", [128, 24, T], BF16)
        qkv_b = [Buf("qkv%d" % c) for c in range(24)]
        sgo = SB("sgo", [128, 8, T], F32)
        sgo_b = [Buf("sgo%d" % c) for c in range(8)]
        zsm = SB("zsm", [128, 8, T], F32)
        zsm_b = [Buf("zsm%d" % c) for c in range(8)]
        zsr = SB("zsr", [128, 8, T], F32)
        zsr_b = [Buf("zsr%d" % c) for c in range(8)]
        cell = SB("cell", [128, 8, T], F32)
        cell_b = [Buf("cell%d" % c) for c in range(8)]
        NTMP = 4
        tmp = [[SB("tmp%d_%d" % (s, i), [128, T], F32) for i in range(NTMP)] for s in range(2)]
        tmp_b = [[Buf("tmp%d_%d" % (s, i)) for i in range(NTMP)] for s in range(2)]
        NB = 4
        bX = SB("bX", [128, NB, T], F32)
        bR = SB("bR", [128, NB, T], F32)
        bI = SB("bI", [128, NB, T], F32)
        bA = SB("bA", [128, NB, T], F32)
        bM = SB("bM", [128, NB, T], F32)
        bX_b = [Buf("bX%d" % i) for i in range(NB)]
        bR_b = [Buf("bR%d" % i) for i in range(NB)]
        bI_b = [Buf("bI%d" % i) for i in range(NB)]
        bA_b = [Buf("bA%d" % i) for i in range(NB)]
        bM_b = [Buf("bM%d" % i) for i in range(NB)]
        xcb = SB("xcb", [128, NB, T], BF16)
        xcb_b = [Buf("xcb%d" % i) for i in range(NB)]
        dg = [SB("dg%d" % s, [128, 4, 128], BF16) for s in range(NB)]
        dg_b = [Buf("dg%d" % s) for s in range(NB)]
        eb16 = SB("eb16", [128, 4, 128], F32)
        eb16_b = Buf("eb16")
        lftri = SB("lftri", [128, 4, 128], F32)
        lftri_b = Buf("lftri")
        wT = [SB("wT%d" % s, [128, 128], F32) for s in range(2)]
        wT_b = [Buf("wT%d" % s) for s in range(2)]
        pT = [SB("pT%d" % s, [128, 128], BF16) for s in range(2)]
        pT_b = [Buf("pT%d" % s) for s in range(2)]
        qs = [SB("qs%d" % s, [128, 2, 128], BF16) for s in range(2)]
        qs_b = [Buf("qs%d" % s) for s in range(2)]
        vx = [SB("vx%d" % s, [128, 258], BF16) for s in range(2)]
        vx_b = [Buf("vx%d" % s) for s in range(2)]
        kw = [SB("kw%d" % s, [128, 256], BF16) for s in range(2)]
        kw_b = [Buf("kw%d" % s) for s in range(2)]
        rec = [SB("rec%d" % s, [128, 128], F32) for s in range(2)]
        rec_b = [Buf("rec%d" % s) for s in range(2)]
        nbc = [SB("nbc%d" % s, [128, 2, 128], BF16) for s in range(2)]
        nbc_b = [Buf("nbc%d" % s) for s in range(2)]
        gsb = SB("gsb", [128, NCH, 8], F32)
        gsb_b = Buf("gsb")
        nlf = SB("nlf", [128, NCH, 4], F32)
        nlf_b = Buf("nlf")
        ctm = SB("ctm", [128, NCH, 4], F32)
        ctm_b = Buf("ctm")
        ec = SB("ec", [128, NCH, 4], F32)
        ec_b = Buf("ec")
        wst = SB("wst", [128, NCH, 4], F32)
        wst_b = Buf("wst")
        dec = SB("dec", [128, NCH, 4], F32)
        dec_b = Buf("dec")
        halo_rg = [SB("halo_rg%d" % l, [128, 8, 3], BF16) for l in range(NL)]
        halo_ml = [SB("halo_ml%d" % l, [128, 8, 3], BF16) for l in range(NL)]
        halo_rg_b = [Buf("halo_rg%d" % l) for l in range(NL)]
        halo_ml_b = [Buf("halo_ml%d" % l) for l in range(NL)]
        rgh = [SB("rgh%d" % l, [128, 8], F32) for l in range(NL)]
        rgh_b = [[Buf("rgh%d_%d" % (l, c)) for c in range(8)] for l in range(NL)]
        C32 = [SB("C32_%d" % l, [128, 4, 2, 258], F32) for l in range(NL)]
        Cb = [SB("Cb_%d" % l, [128, 4, 2, 258], BF16) for l in range(NL)]
        C32_b = [[Buf("C32_%d_%d" % (l, h)) for h in range(4)] for l in range(NL)]
        Cb_b = [[Buf("Cb_%d_%d" % (l, h)) for h in range(4)] for l in range(NL)]
        ident = SB("ident", [128, 128], BF16)
        ones_b = SB("ones_b", [128, 128], BF16)
        inv1024 = SB("inv1024", [128, 128], BF16)
        inv256 = SB("inv256", [128, 128], BF16)
        tri32 = SB("tri_sb", [128, 128], F32)
        ones32 = SB("ones32", [128, 128], F32)
        pv = SB("pv_sb", [128, NPV], F32)
        dv = SB("dv", [128, NL, 64], F32)
        modT = SB("modT", [128, NL, 24], F32)
        bifb = SB("bif_sb", [128, NL * 8], F32)
        cact = SB("cact", [128, 8], F32)
        ctmp = SB("ctmp", [128, 8], F32)
        const_b = Buf("consts")
        par_b = Buf("params")
        psg = [PS("psg%d" % i) for i in range(3)]
        psg_b = [Buf("psg%d" % i, True) for i in range(3)]
        psS = PS("psS")
        psS_b = Buf("psS_S", True)
        psND = PS("psND")
        psND_b = Buf("psND", True)
        psVK = PS("psVK")
        psV_b = Buf("psV", True)
        psC = PS("psC")
        psC_b = Buf("psC", True)
        psM = PS("psM")
        psM_b = Buf("psM", True)

        nc._sbuf_left = nc.sbuf_bytes_remaining
        gctr = [0]

        def gen_bank():
            i = gctr[0] % 3
            gctr[0] += 1
            return psg[i], psg_b[i]

        def ACT(out, in_, func, reads, writes, bias=None, scale=None):
            kw_ = {}
            if bias is not None:
                kw_["bias"] = bias
            if scale is not None:
                kw_["scale"] = scale
            return fw.op("act", lambda e: e.activation(out=out, in_=in_, func=func, **kw_), reads, writes)

        def TT(eng, out, in0, in1, op, reads, writes):
            return fw.op(eng, lambda e: e.tensor_tensor(out=out, in0=in0, in1=in1, op=op), reads, writes)

        def TS(eng, out, in0, s1, s2, op0, op1, reads, writes):
            if s2 is None:
                return fw.op(eng, lambda e: e.tensor_scalar(out=out, in0=in0, scalar1=s1, scalar2=None, op0=op0), reads, writes)
            return fw.op(eng, lambda e: e.tensor_scalar(out=out, in0=in0, scalar1=s1, scalar2=s2, op0=op0, op1=op1), reads, writes)

        def STT(out, in0, scalar, in1, op0, op1, reads, writes):
            return fw.op("dve", lambda e: e.scalar_tensor_tensor(out=out, in0=in0, scalar=scalar, in1=in1, op0=op0, op1=op1), reads, writes)

        def CP(eng, out, in_, reads, writes):
            if eng == "act":
                return fw.op("act", lambda e: e.copy(out=out, in_=in_), reads, writes)
            return fw.op(eng, lambda e: e.tensor_copy(out=out, in_=in_), reads, writes)

        def RECIP(out, in_, reads, writes):
            return fw.op("dve", lambda e: e.reciprocal(out=out, in_=in_), reads, writes)

        def MM(out, lhsT, rhs, start, stop, reads, writes, signal=True):
            return fw.op("pe", lambda e: e.matmul(out, lhsT=lhsT, rhs=rhs, start=start, stop=stop), reads, writes, signal=signal)

        def MSET(eng, ap, val, writes):
            return fw.op(eng, lambda e: e.memset(ap, val), (), writes)

        MSET("pool", ones_b[:], 1.0, [const_b])
        MSET("pool", inv1024[:], 1.0 / 1024.0, [const_b])
        MSET("pool", inv256[:], 1.0 / 256.0, [const_b])
        MSET("pool", ones32[:], 1.0, [const_b])
        for s in range(2):
            MSET("pool", vx[s][:], 1.0, [vx_b[s]])
        for l in range(NL):
            MSET("pool", halo_rg[l][:], 0.0, [halo_rg_b[l]])
            MSET("pool", halo_ml[l][:], 0.0, [halo_ml_b[l]])
            MSET("pool", rgh[l][:], 0.0, rgh_b[l])
            MSET("pool", C32[l][:], 0.0, C32_b[l])
            MSET("pool", Cb[l][:], 0.0, Cb_b[l])
        fw.dma("sp", "c_tri", tri32[:], tri_d, writes=[const_b])
        fw.dma("sp", "c_pv", pv[:], pv_d, writes=[par_b])
        fw.dma("sp", "c_bif", bifb[:], bif_d.partition_broadcast(128), writes=[par_b])
        CP("dve", ident[:, 0:1], tri32[:, 0:1], [const_b], [const_b])
        TT("dve", ident[:, 1:128], tri32[:, 1:128], tri32[:, 0:127], ALU.subtract, [const_b], [const_b])

        def pcol(l, off, n=8):
            b = l * LP + off
            return pv[:, b:b + n]

        O_NG, O_RCW, O_RCB, O_BA, O_BX, O_LAM, O_MCW, O_MCB, O_MNG, O_BADA = 0, 8, 40, 48, 56, 64, 72, 104, 112, 120
        O_FG = NL * LP
        O_CT = NL * LP + 8
        for l in range(NL):
            TS("dve", dv[:, l, 0:8], pcol(l, O_BA), -1.0, None, ALU.mult, None, [par_b], [par_b])
            TS("dve", dv[:, l, 8:16], pcol(l, O_BX), -1.0, None, ALU.mult, None, [par_b], [par_b])
            TS("dve", dv[:, l, 16:24], pcol(l, O_MCB), -1.0, None, ALU.mult, None, [par_b], [par_b])
            ACT(dv[:, l, 48:56], pcol(l, O_LAM), AF.Exp, [par_b], [par_b], scale=-1.0)
            ACT(dv[:, l, 48:56], dv[:, l, 48:56], AF.Ln, [par_b], [par_b], bias=1.0)
            TS("dve", dv[:, l, 24:32], dv[:, l, 48:56], -8.0, None, ALU.mult, None, [par_b], [par_b])
            TS("dve", dv[:, l, 32:40], dv[:, l, 48:56], -16.0, None, ALU.mult, None, [par_b], [par_b])
        ACT(ctmp[:], pv[:, O_CT:O_CT + 8], AF.Exp, [par_b], [par_b], scale=-1.0)
        TS("dve", ctmp[:], ctmp[:], 1.0, None, ALU.add, None, [par_b], [par_b])
        RECIP(ctmp[:], ctmp[:], [par_b], [par_b])
        TT("dve", cact[:], pv[:, O_CT:O_CT + 8], ctmp[:], ALU.mult, [par_b], [par_b])
        slot_ctr = [0]

        def next_slot():
            i = slot_ctr[0] % NSLOT
            slot_ctr[0] += 1
            return i

        for l in range(NL):
            for pi in range(6):
                si = next_slot()
                r32 = ring[si][:].bitcast(F32)
                dst = r32[:, 0:4096].rearrange("p (kc n) -> p kc n", kc=8)
                src = w_ada_d[l].rearrange("(kc p) n -> p kc n", p=128)[:, :, pi * 512:(pi + 1) * 512]
                fw.dma("sp", "ada%d" % si, dst, src, writes=[ring_b[si]])
                for mm_ in range(4):
                    m = pi * 4 + mm_
                    for kc in range(8):
                        MM(psM[:, m:m + 1], dst[:, kc, mm_ * 128:(mm_ + 1) * 128], cact[:, kc:kc + 1], kc == 0, kc == 7,
                           [ring_b[si], par_b], [psM_b], signal=(kc == 7))
            TT("dve", modT[:, l, :], psM[:, 0:24], pcol(l, O_BADA, 24), ALU.add, [psM_b, par_b], [par_b])
            STT(dv[:, l, 40:48], modT[:, l, 8:16], 1.0, pcol(l, O_NG), ALU.add, ALU.mult, [par_b], [par_b])

        def load_piece(si, l, piece):
            key = "ring%d" % si
            r = ring[si]
            if piece < 5:
                dst = r[:, 0:8192].rearrange("p (kc n) -> p kc n", kc=8)
                src = w_in_d[l].rearrange("(kc p) n -> p kc n", p=128)[:, :, piece * 1024:(piece + 1) * 1024]
                fw.dma("pool", key, dst, src, reads=[xl_b], writes=[ring_b[si]])
            elif piece == 5:
                fw.dma("pool", key, r[:, 0:1024].rearrange("p (h e) -> p h e", h=8),
                       w_a_d[l].rearrange("h d e -> d h e"), reads=[xl_b], writes=[ring_b[si]])
                fw.dma("pool", key, r[:, 1024:2048].rearrange("p (h e) -> p h e", h=8),
                       w_x_d[l].rearrange("h d e -> d h e"), writes=[])
                for wi, wd in enumerate((w_q_d, w_k_d, w_v_d)):
                    for h in range(4):
                        o = 2048 + wi * 2048 + h * 512
                        fw.dma("pool", key, r[:, o:o + 512].rearrange("p (dc e) -> p dc e", dc=2),
                               wd[l, h].rearrange("(dc p) e -> p dc e", p=128), writes=[])
                fw.dma("pool", key, r[:, 8192:8384].rearrange("p (kc g) -> p kc g", kc=24),
                       w_if_d[l].rearrange("(kc p) g -> p kc g", p=128), writes=[])
                ring_b[si].lw = (key, fw.cnt[key])
            else:
                hf = piece - 6
                dst = r[:, 0:8192].rearrange("p (kc n) -> p kc n", kc=8)
                src = w_out_d[l].rearrange("(kc p) n -> p kc n", p=128)[:, hf * 8:(hf + 1) * 8, :]
                fw.dma("pool", key, dst, src, reads=[xl_b], writes=[ring_b[si]])

        steps = [(ti, l) for ti in range(nt) for l in range(nl)]
        ORDER = [0, 1, 2, 5, 3, 4, 6, 7]
        piece_seq = [(si_, p) for si_ in range(len(steps)) for p in ORDER]
        slot_of = {}
        free_slots = list(range(NSLOT))
        next_load = [0]

        xl_b = Buf("xload_done")
        xl_b.norec = True
        deferred_rel = []

        def pump():
            while next_load[0] < len(piece_seq) and free_slots:
                key_ = piece_seq[next_load[0]]
                si = free_slots.pop(0)
                load_piece(si, steps[key_[0]][1], key_[1])
                slot_of[key_] = si
                next_load[0] += 1

        def get_piece(sidx, p):
            pump()
            assert (sidx, p) in slot_of, ("ring deadlock", sidx, p)
            si = slot_of[(sidx, p)]
            return ring[si], ring_b[si]

        def release(sidx, p):
            si = slot_of.pop((sidx, p))
            free_slots.append(si)
            pump()

        def dump(srcs):
            MSET("dve", cell[:], 0.0, cell_b)
            for (dst, ap_, bufs) in srcs:
                CP("dve", dst, ap_, bufs, cell_b)
            fw.dma("sp", "ostore", oT_d.rearrange("(c p) t -> p c t", p=128)[:, :, 0:T], cell[:], reads=cell_b, writes=[])
            raise _Stop()

        try:
          if dbg == 0:
            dump([(cell[:, 0, 0:64], dv[:, 0, :], [par_b]), (cell[:, 1, 0:24], modT[:, 0, :], [par_b]),
                  (cell[:, 2, 0:128], ident[:], [const_b]), (cell[:, 3, 0:16], bifb[:], [par_b])])
          out_sem_keys = []
          for sidx, (ti, l) in enumerate(steps):
              t0 = ti * T
              if l == 0:
                  xsrc = xT_d.rearrange("(c p) t -> p c t", p=128)
                  fw.dma("sp", "xloadA", xT[:, 0:4, :], xsrc[:, 0:4, t0:t0 + T], reads=[], writes=xT_b[0:4] + [xl_b])
                  fw.dma("sp", "xloadB", xT[:, 4:8, :], xsrc[:, 4:8, t0:t0 + T], reads=[], writes=xT_b[4:8] + [xl_b])
              for (rs_, rp_) in deferred_rel:
                  release(rs_, rp_)
              del deferred_rel[:]
              if l == 0:
                  for kc in range(8):
                      ACT(sq[:, kc, :], xT[:, kc, :], AF.Square, [xT_b[kc]], [sq_b[kc]])
                      MM(psM[:, 128:128 + T], inv1024[:], sq[:, kc, :], kc == 0, kc == 7, [const_b, sq_b[kc]], [psM_b], signal=(kc == 7))
              ACT(lnr[:], psM[:, 128:128 + T], AF.Ln, [psM_b], [lnr_b], bias=EPS)
              ACT(rstd[:], lnr[:], AF.Exp, [lnr_b], [rstd_b], scale=-0.5)
              for c in range(8):
                  s = c % 2
                  t_ = tmp[s][0]
                  STT(t_[:], xT[:, c, :], dv[:, l, 40 + c:41 + c], rstd[:], ALU.mult, ALU.mult,
                      [xT_b[c], rstd_b, par_b], [tmp_b[s][0]])
                  ACT(hb[:, c, :], t_[:], AF.Identity, [tmp_b[s][0], par_b], [hb_b[c]], bias=modT[:, l, c:c + 1])

              if dbg == 1 and sidx == len(steps) - 1:
                  dump([(cell[:], hb[:], hb_b)])
              def win_chunk(piece, c):
                  r, rb = get_piece(sidx, piece)
                  w3 = r[:, 0:8192].rearrange("p (kc n) -> p kc n", kc=8)
                  pb_, pbb_ = gen_bank()
                  for kc in range(8):
                      MM(pb_[:, 0:T], w3[:, kc, c * 128:(c + 1) * 128], hb[:, kc, :], kc == 0, kc == 7,
                         [rb, hb_b[kc]], [pbb_], signal=(kc == 7))
                  return pb_, pbb_

              def sigmoid_from(out, src, reads, wbuf, nbias=None):
                  if nbias is None:
                      ACT(out, src, AF.Exp, reads, [wbuf], scale=-1.0)
                  else:
                      ACT(out, src, AF.Exp, reads + [par_b], [wbuf], scale=-1.0, bias=nbias)
                  ACT(out, out, AF.Ln, [wbuf], [wbuf], bias=1.0)
                  ACT(out, out, AF.Exp, [wbuf], [wbuf], scale=-1.0)

              CP("pool", rgx[:, :, 1:4], halo_rg[l][:], [halo_rg_b[l]], rgx_b)
              CP("pool", mlx[:, :, 1:4], halo_ml[l][:], [halo_ml_b[l]], mlx_b)

              smallw = [None]

              def get_small():
                  if smallw[0] is None:
                      smallw[0] = get_piece(sidx, 5)
                  return smallw[0]

              for c in range(8):
                  pb, pbb = win_chunk(0, c)
                  CP("act", rgx[:, c, 4:4 + T], pb[:, 0:T], [pbb], [rgx_b[c]])
              CP("pool", halo_rg[l][:], rgx[:, :, T + 1:T + 4], rgx_b, [halo_rg_b[l]])
              release(sidx, 0)
              if dbg == 2 and sidx == len(steps) - 1:
                  dump([(cell[:], rgx[:, :, 4:4 + T], rgx_b)])
              for c in range(8):
                  pb, pbb = win_chunk(1, c)
                  s = c % 2
                  sz = tmp[s][1]
                  sigmoid_from(sz[:], pb[:, 0:T], [pbb], tmp_b[s][1])
                  TT("dve", zsr[:, c, :], pb[:, 0:T], sz[:], ALU.mult, [pbb, tmp_b[s][1]], [zsr_b[c]])
              release(sidx, 1)
              for c in range(8):
                  pb, pbb = win_chunk(2, c)
                  CP("act", mlx[:, c, 4:4 + T], pb[:, 0:T], [pbb], [mlx_b[c]])
              CP("pool", halo_ml[l][:], mlx[:, :, T + 1:T + 4], mlx_b, [halo_ml_b[l]])
              release(sidx, 2)

              sw, swb = get_small()
              w_a3 = sw[:, 0:1024].rearrange("p (h e) -> p h e", h=8)
              w_x3 = sw[:, 1024:2048].rearrange("p (h e) -> p h e", h=8)
              w_q4 = sw[:, 2048:4096].rearrange("p (h dc e) -> p h dc e", h=4, dc=2)
              w_k4 = sw[:, 4096:6144].rearrange("p (h dc e) -> p h dc e", h=4, dc=2)
              w_v4 = sw[:, 6144:8192].rearrange("p (h dc e) -> p h dc e", h=4, dc=2)
              w_if3 = sw[:, 8192:8384].rearrange("p (kc g) -> p kc g", kc=24)

              def diag_build(ci, c, wcol_off):
                  for k in range(4):
                      col = l * LP + wcol_off + k * 8 + c
                      if k < 2:
                          ACT(dg[ci][:, k, :], ident[:], AF.Identity, [const_b, par_b], [dg_b[ci]], scale=pv[:, col:col + 1])
                      else:
                          TS("dve", dg[ci][:, k, :], ident[:], pv[:, col:col + 1], None, ALU.mult, None,
                             [const_b, par_b], [dg_b[ci]])

              def conv_mm(ci, c, src, src_b):
                  pb_, pbb_ = gen_bank()
                  for k in range(4):
                      MM(pb_[:, 0:T], dg[ci][:, k, :], src[:, c, 1 + k:1 + k + T], k == 0, k == 3, [dg_b[ci], src_b[c]], [pbb_],
                         signal=(k == 3))
                  return pb_, pbb_

              for bt in range(2):
                  cs = [bt * NB + ci for ci in range(NB)]
                  for ci, c in enumerate(cs):
                      diag_build(ci, c, O_MCW)
                  for ci, c in enumerate(cs):
                      pb, pbb = conv_mm(ci, c, mlx, mlx_b)
                      TS("dve", bR[:, ci, :], pb[:, 0:T], pcol(l, O_MCB)[:, c:c + 1], None, ALU.add, None, [pbb, par_b], [bR_b[ci]])
                  for hf in range(2):
                      sl = slice(2 * hf, 2 * hf + 2)
                      ACT(bX[:, sl, :], bR[:, sl, :], AF.Exp, bR_b[sl], bX_b[sl], scale=-1.0)
                  for hf in range(2):
                      sl = slice(2 * hf, 2 * hf + 2)
                      ACT(bX[:, sl, :], bX[:, sl, :], AF.Ln, bX_b[sl], bX_b[sl], bias=1.0)
                  for hf in range(2):
                      sl = slice(2 * hf, 2 * hf + 2)
                      ACT(bX[:, sl, :], bX[:, sl, :], AF.Exp, bX_b[sl], bX_b[sl], scale=-1.0)
                  for hf in range(2):
                      sl = slice(2 * hf, 2 * hf + 2)
                      c0_ = bt * NB + 2 * hf
                      TT("dve", mxc[:, c0_:c0_ + 2, :], bR[:, sl, :], bX[:, sl, :], ALU.mult, bR_b[sl] + bX_b[sl], mxc_b[c0_:c0_ + 2])
              if dbg == 4 and sidx == len(steps) - 1:
                  dump([(cell[:], mxc[:], mxc_b)])
              for wi, (w4, src, src_b, off) in enumerate(((w_q4, mxc, mxc_b, 0), (w_k4, mxc, mxc_b, 0), (w_v4, mlx, mlx_b, 4))):
                  for h in range(4):
                      for ecx in range(2):
                          pb, pbb = gen_bank()
                          for dc in range(2):
                              MM(pb[:, 0:T], w4[:, h, dc, ecx * 128:(ecx + 1) * 128], src[:, 2 * h + dc, off:off + T],
                                 dc == 0, dc == 1, [swb, src_b[2 * h + dc]], [pbb], signal=(dc == 1))
                          oc = wi * 8 + 2 * h + ecx
                          eng = "act" if (oc % 2 == 0) else "dve"
                          CP(eng, qkv[:, oc, :], pb[:, 0:T], [pbb], [qkv_b[oc]])
              for j in range(NCH):
                  for kc in range(24):
                      MM(psM[:, 32 + j * 8:32 + (j + 1) * 8], qkv[:, kc, j * 128:(j + 1) * 128], w_if3[:, kc, :], kc == 0, kc == 23,
                         [qkv_b[kc], swb], [psM_b], signal=(kc == 23))
              g_ps = psM[:, 32:32 + NCH * 8].rearrange("p (j g) -> p j g", j=NCH)
              TT("dve", gsb[:], g_ps, bifb[:, l * 8:(l + 1) * 8].unsqueeze(1).broadcast_to([128, NCH, 8]), ALU.add,
                 [psM_b, par_b], [gsb_b])
              if dbg == 5 and sidx == len(steps) - 1:
                  dump([(cell[:, 0, 0:NCH * 8], gsb[:].rearrange("p j g -> p (j g)"), [gsb_b])])
              ACT(nlf[:], gsb[:, :, 4:8], AF.Exp, [gsb_b], [nlf_b], scale=-1.0)
              ACT(nlf[:], nlf[:], AF.Ln, [nlf_b], [nlf_b], bias=1.0)
              nlf2 = nlf[:].rearrange("p j h -> p (j h)")
              MM(psM[:, 64:64 + NCH * 4], tri32[:], nlf2, True, True, [const_b, nlf_b], [psM_b])
              MM(psM[:, 96:96 + NCH * 4], ones32[:], nlf2, True, True, [const_b, nlf_b], [psM_b])
              nb_ps = psM[:, 64:64 + NCH * 4].rearrange("p (j h) -> p j h", j=NCH)
              nbl_ps = psM[:, 96:96 + NCH * 4].rearrange("p (j h) -> p j h", j=NCH)
              TT("dve", ctm[:], gsb[:, :, 0:4], nb_ps, ALU.add, [gsb_b, psM_b], [ctm_b])
              ACT(ec[:], ctm[:], AF.Exp, [ctm_b], [ec_b])
              TT("dve", wst[:], ctm[:], nbl_ps, ALU.subtract, [ctm_b, psM_b], [wst_b])
              ACT(wst[:], wst[:], AF.Exp, [wst_b], [wst_b])
              ACT(dec[:], nbl_ps, AF.Exp, [psM_b], [dec_b], scale=-1.0)

              kbank = {}

              def ml_prep(j):
                  for h in range(4):
                      TS("pool", lftri[:, h, :], tri32[:], nlf[:, j, h:h + 1], -1.0, ALU.mult, ALU.mult,
                         [const_b, nlf_b], [lftri_b])
                  pbb_, pbbb_ = gen_bank()
                  MM(pbb_[:, 0:512], ones32[:], lftri[:].rearrange("p h t -> p (h t)"), True, True, [const_b, lftri_b], [pbbb_])
                  ACT(eb16[:].rearrange("p h t -> p (h t)"), pbb_[:, 0:512], AF.Exp, [pbbb_], [eb16_b], bias=-math.log(16.0))

              def ml_A1(idx):
                  j, h = idx // 4, idx % 4
                  ts_ = slice(j * 128, (j + 1) * 128)
                  if h == 0:
                      ml_prep(j)
                  q0 = 2 * h
                  k0 = 8 + 2 * h
                  for dc in range(2):
                      MM(psS[:, 0:128], qkv[:, k0 + dc, ts_], qkv[:, q0 + dc, ts_], dc == 0, dc == 1,
                         [qkv_b[k0 + dc], qkv_b[q0 + dc]], [psS_b], signal=(dc == 1))
                  for dc in range(2):
                      MM(psVK[:, 0:256], mlx[:, 2 * h + dc, 4 + j * 128:4 + (j + 1) * 128], w_v4[:, h, dc, :], dc == 0, dc == 1,
                         [mlx_b[2 * h + dc], swb], [psV_b], signal=(dc == 1))
                  for dc in range(2):
                      MM(psVK[:, 256:512], mxc[:, 2 * h + dc, ts_], w_k4[:, h, dc, :], dc == 0, dc == 1,
                         [mxc_b[2 * h + dc], swb], [psV_b], signal=(dc == 1))

              def ml_A2(idx):
                  j, h = idx // 4, idx % 4
                  ts_ = slice(j * 128, (j + 1) * 128)
                  s = idx % 2
                  q0 = 2 * h
                  STT(wT[s][:], eb16[:, h, :], ec[:, j, h:h + 1], tri32[:], ALU.mult, ALU.mult,
                      [eb16_b, ec_b, const_b], [wT_b[s]])
                  CP("act", vx[s][:, 0:256], psVK[:, 0:256], [psV_b], [vx_b[s]])
                  TT("pool", qs[s][:], qkv[:, q0:q0 + 2, ts_], eb16[:, h, :].unsqueeze(1).broadcast_to([128, 2, 128]),
                     ALU.mult, [qkv_b[q0], qkv_b[q0 + 1], eb16_b], [qs_b[s]])
                  TS("dve", kw[s][:], psVK[:, 256:512], wst[:, j, h:h + 1], None, ALU.mult, None, [psV_b, wst_b], [kw_b[s]])
                  TT("dve", pT[s][:], psS[:, 0:128], wT[s][:], ALU.mult, [psS_b, wT_b[s]], [pT_b[s]])
                  CP("pool", nbc[s][:], Cb[l][:, h, :, 256:257].broadcast_to([128, 2, 128]), [Cb_b[l][h]], [nbc_b[s]])

              def ml_B1(idx):
                  j, h = idx // 4, idx % 4
                  s = idx % 2
                  for vc in range(2):
                      o = vc * 128
                      MM(psND[:, o:o + 128], vx[s][:, vc * 128:(vc + 1) * 128], pT[s][:], True, False,
                         [vx_b[s], pT_b[s]], [psND_b], signal=False)
                      for dc in range(2):
                          MM(psND[:, o:o + 128], Cb[l][:, h, dc, vc * 128:(vc + 1) * 128], qs[s][:, dc, :], False, dc == 1,
                             [Cb_b[l][h], qs_b[s]], [psND_b], signal=False)
                  MM(psND[:, 256:384], ones_b[:], pT[s][:], True, False, [const_b, pT_b[s]], [psND_b], signal=False)
                  for dc in range(2):
                      MM(psND[:, 256:384], nbc[s][:, dc, :], qs[s][:, dc, :], False, dc == 1, [nbc_b[s], qs_b[s]], [psND_b],
                         signal=False)
                  for dc in range(2):
                      MM(psND[:, 384 + 2 * dc:386 + 2 * dc], kw[s][:, dc * 128:(dc + 1) * 128], vx[s][:, 256:258], True, True,
                         [kw_b[s], vx_b[s]], [psND_b], signal=(dc == 1))
                  for dc in range(2):
                      MM(psC[:, dc * 256:(dc + 1) * 256], kw[s][:, dc * 128:(dc + 1) * 128], vx[s][:, 0:256], True, True,
                         [kw_b[s], vx_b[s]], [psC_b], signal=(dc == 1))

              def ml_B2(idx):
                  j, h = idx // 4, idx % 4
                  ts_ = slice(j * 128, (j + 1) * 128)
                  s = idx % 2
                  ACT(rec[s][:], psND[:, 256:384], AF.Abs, [psND_b], [rec_b[s]])
                  STT(C32[l][:, h, :, 0:256], C32[l][:, h, :, 0:256], dec[:, j, h:h + 1],
                      psC[:, 0:512].rearrange("p (dc e) -> p dc e", dc=2), ALU.mult, ALU.add,
                      [C32_b[l][h], dec_b, psC_b], [C32_b[l][h]])
                  TS("dve", rec[s][:], rec[s][:], 1.0, None, ALU.max, None, [rec_b[s]], [rec_b[s]])
                  STT(C32[l][:, h, :, 256:258], C32[l][:, h, :, 256:258], dec[:, j, h:h + 1],
                      psND[:, 384:388].rearrange("p (dc e) -> p dc e", dc=2), ALU.mult, ALU.add,
                      [C32_b[l][h], dec_b, psND_b], [C32_b[l][h]])
                  RECIP(rec[s][:], rec[s][:], [rec_b[s]], [rec_b[s]])
                  CP("dve", Cb[l][:, h, :, :], C32[l][:, h, :, :], [C32_b[l][h]], [Cb_b[l][h]])
                  TT("dve", cell[:, 2 * h:2 * h + 2, ts_], psND[:, 0:256].rearrange("p (v t) -> p v t", v=2),
                     rec[s][:].unsqueeze(1).broadcast_to([128, 2, 128]), ALU.mult, [psND_b, rec_b[s]],
                     [cell_b[2 * h], cell_b[2 * h + 1]])

              NIT = NCH * 4
              ml_sched = []
              for i in range(NIT + 1):
                  if i < NIT:
                      ml_sched.append((ml_A1, i))
                  if i >= 1:
                      ml_sched.append((ml_B1, i - 1))
                  if i < NIT:
                      ml_sched.append((ml_A2, i))
                  if i >= 1:
                      ml_sched.append((ml_B2, i - 1))
              ml_pos = [0]

              def ml_emit(n):
                  for _ in range(n):
                      if ml_pos[0] < len(ml_sched):
                          f_, a_ = ml_sched[ml_pos[0]]
                          f_(a_)
                          ml_pos[0] += 1

              fill_pos = [0]

              def fill_pair():
                  k = fill_pos[0]
                  if k >= 8:
                      return
                  fill_pos[0] += 1
                  piece = 3 if k < 4 else 4
                  c0 = (k % 4) * 2
                  banks = []
                  for c in (c0, c0 + 1):
                      banks.append(win_chunk(piece, c))
                  outs = []
                  for ii, c in enumerate((c0, c0 + 1)):
                      pb, pbb = banks[ii]
                      if piece == 3:
                          o_, ob_ = sgo[:, c, :], sgo_b[c]
                      else:
                          o_, ob_ = tmp[ii][1][:], tmp_b[ii][1]
                      outs.append((o_, ob_))
                      ACT(o_, pb[:, 0:T], AF.Exp, [pbb], [ob_], scale=-1.0)
                  for (o_, ob_) in outs:
                      ACT(o_, o_, AF.Ln, [ob_], [ob_], bias=1.0)
                  for (o_, ob_) in outs:
                      ACT(o_, o_, AF.Exp, [ob_], [ob_], scale=-1.0)
                  if piece == 4:
                      for ii, c in enumerate((c0, c0 + 1)):
                          pb, pbb = banks[ii]
                          TT("dve", zsm[:, c, :], pb[:, 0:T], outs[ii][0], ALU.mult, [pbb, outs[ii][1]], [zsm_b[c]])
                  if k == 3:
                      release(sidx, 3)
                  if k == 7:
                      release(sidx, 4)

              per_gap = 1
              for bt in range(2):
                  cs = [bt * NB + ci for ci in range(NB)]
                  for ci, c in enumerate(cs):
                      diag_build(ci, c, O_RCW)
                  ml_emit(per_gap)
                  fill_pair()
                  for ci, c in enumerate(cs):
                      pb, pbb = conv_mm(ci, c, rgx, rgx_b)
                      ACT(bX[:, ci, :], pb[:, 0:T], AF.Identity, [pbb, par_b], [bX_b[ci]], bias=pcol(l, O_RCB)[:, c:c + 1])
                      ml_emit(1)
                  ml_emit(per_gap)
                  fill_pair()
                  for ci, c in enumerate(cs):
                      CP("dve", xcb[:, ci, :], bX[:, ci, :], [bX_b[ci]], [xcb_b[ci]])
                  gb = []
                  for ci, c in enumerate(cs):
                      pr, prb = gen_bank()
                      MM(pr[:, 0:T], w_a3[:, c, :], xcb[:, ci, :], True, True, [swb, xcb_b[ci]], [prb])
                      ACT(bR[:, ci, :], pr[:, 0:T], AF.Exp, [prb, par_b], [bR_b[ci]], scale=-1.0, bias=dv[:, l, c:c + 1])
                      pi_, pib = gen_bank()
                      MM(pi_[:, 0:T], w_x3[:, c, :], xcb[:, ci, :], True, True, [swb, xcb_b[ci]], [pib])
                      ACT(bI[:, ci, :], pi_[:, 0:T], AF.Exp, [pib, par_b], [bI_b[ci]], scale=-1.0, bias=dv[:, l, 8 + c:9 + c])
                      ml_emit(1)
                  ml_emit(per_gap)
                  fill_pair()
                  for hf in range(2):
                      sl = slice(2 * hf, 2 * hf + 2)
                      ACT(bR[:, sl, :], bR[:, sl, :], AF.Ln, bR_b[sl], bR_b[sl], bias=1.0)
                      ACT(bI[:, sl, :], bI[:, sl, :], AF.Ln, bI_b[sl], bI_b[sl], bias=1.0)
                  for hf in range(2):
                      sl = slice(2 * hf, 2 * hf + 2)
                      ACT(bR[:, sl, :], bR[:, sl, :], AF.Exp, bR_b[sl], bR_b[sl], scale=-1.0)
                      ACT(bI[:, sl, :], bI[:, sl, :], AF.Exp, bI_b[sl], bI_b[sl], scale=-1.0)
                  ml_emit(per_gap)
                  fill_pair()
                  for ci, c in enumerate(cs):
                      ACT(bA[:, ci, :], bR[:, ci, :], AF.Exp, [bR_b[ci], par_b], [bA_b[ci]], scale=dv[:, l, 24 + c:25 + c])
                      ACT(bM[:, ci, :], bR[:, ci, :], AF.Exp, [bR_b[ci], par_b], [bM_b[ci]], scale=dv[:, l, 32 + c:33 + c])
                      ml_emit(1)
                  for hf in range(2):
                      sl = slice(2 * hf, 2 * hf + 2)
                      TT("dve", bI[:, sl, :], bI[:, sl, :], bX[:, sl, :], ALU.mult, bI_b[sl] + bX_b[sl], bI_b[sl])
                  ml_emit(per_gap)
                  fill_pair()
                  for hf in range(2):
                      sl = slice(2 * hf, 2 * hf + 2)
                      ACT(bM[:, sl, :], bM[:, sl, :], AF.Ln, bM_b[sl], bM_b[sl], scale=-1.0, bias=1.0)
                  for hf in range(2):
                      sl = slice(2 * hf, 2 * hf + 2)
                      ACT(bM[:, sl, :], bM[:, sl, :], AF.Exp, bM_b[sl], bM_b[sl], scale=0.5)
                  ml_emit(per_gap)
                  fill_pair()
                  for hf in range(2):
                      sl = slice(2 * hf, 2 * hf + 2)
                      TT("dve", bI[:, sl, :], bI[:, sl, :], bM[:, sl, :], ALU.mult, bI_b[sl] + bM_b[sl], bI_b[sl])
                  for ci, c in enumerate(cs):
                      fw.op("dve", lambda e: e.tensor_tensor_scan(out=bM[:, ci, :], data0=bA[:, ci, :], data1=bI[:, ci, :],
                                                                  initial=rgh[l][:, c:c + 1], op0=ALU.mult, op1=ALU.add),
                            [bA_b[ci], bI_b[ci], rgh_b[l][c]], [bM_b[ci]])
                  ml_emit(per_gap)
                  fill_pair()
                  for ci, c in enumerate(cs):
                      CP("act", rgh[l][:, c:c + 1], bM[:, ci, T - 1:T], [bM_b[ci]], [rgh_b[l][c]])
                      TT("dve", yT[:, c, :], bM[:, ci, :], zsr[:, c, :], ALU.mult, [bM_b[ci], zsr_b[c]], [yT_b[c]])
              ml_emit(len(ml_sched))
              while fill_pos[0] < 8:
                  fill_pair()
              release(sidx, 5)
              if dbg == 3 and sidx == len(steps) - 1:
                  dump([(cell[:], yT[:, 0:8, :], yT_b[0:8])])
              if dbg == 6 and sidx == len(steps) - 1:
                  fw.dma("sp", "ostore", oT_d.rearrange("(c p) t -> p c t", p=128)[:, :, 0:T], cell[:], reads=cell_b, writes=[])
                  raise _Stop()
              for h in range(4):
                  hs_ = slice(2 * h, 2 * h + 2)
                  TT("dve", cell[:, hs_, :], cell[:, hs_, :], sgo[:, hs_, :], ALU.mult, cell_b[hs_] + sgo_b[hs_], cell_b[hs_])
                  ACT(sq[:, hs_, :], cell[:, hs_, :], AF.Square, cell_b[hs_], sq_b[hs_])
              for h in range(4):
                  pb, pbb = gen_bank()
                  for dc in range(2):
                      MM(pb[:, 0:T], inv256[:], sq[:, 2 * h + dc, :], dc == 0, dc == 1, [const_b, sq_b[2 * h + dc]], [pbb],
                         signal=(dc == 1))
                  s = h % 2
                  ln_, rs_ = tmp[s][0], tmp[s][1]
                  ACT(ln_[:], pb[:, 0:T], AF.Ln, [pbb], [tmp_b[s][0]], bias=EPS)
                  ACT(rs_[:], ln_[:], AF.Exp, [tmp_b[s][0]], [tmp_b[s][1]], scale=-0.5)
                  for dc in range(2):
                      c = 2 * h + dc
                      t1 = tmp[s][2 + dc]
                      STT(t1[:], cell[:, c, :], pcol(l, O_MNG)[:, c:c + 1], zsm[:, c, :], ALU.mult, ALU.mult,
                          [cell_b[c], par_b, zsm_b[c]], [tmp_b[s][2 + dc]])
                      TT("dve", yT[:, 8 + c, :], t1[:], rs_[:], ALU.mult, [tmp_b[s][2 + dc], tmp_b[s][1]], [yT_b[8 + c]])

              if dbg == 7 and sidx == len(steps) - 1:
                  dump([(cell[:], yT[:, 8:16, :], yT_b[8:16])])
              pieces_o = [get_piece(sidx, 6), get_piece(sidx, 7)]
              for c in range(8):
                  pb, pbb = gen_bank()
                  for kc in range(16):
                      r, rb = pieces_o[kc // 8]
                      w3 = r[:, 0:8192].rearrange("p (kc n) -> p kc n", kc=8)
                      MM(pb[:, 0:T], w3[:, kc % 8, c * 128:(c + 1) * 128], yT[:, kc, :], kc == 0, kc == 15,
                         [rb, yT_b[kc]], [pbb], signal=(kc == 15))
                  if c >= 1:
                      MM(psM[:, 128:128 + T], inv1024[:], sq[:, c - 1, :], c == 1, False, [const_b, sq_b[c - 1]], [psM_b], signal=False)
                  if l == nl - 1 and dbg != 8:
                      STT(cell[:, c, :], pb[:, 0:T], modT[:, l, 16 + c:17 + c], xT[:, c, :], ALU.mult, ALU.add,
                          [pbb, par_b, xT_b[c]], [cell_b[c]])
                      ACT(sq[:, c, :], cell[:, c, :], AF.Square, [cell_b[c]], [sq_b[c]])
                  else:
                      STT(xT[:, c, :], pb[:, 0:T], modT[:, l, 16 + c:17 + c], xT[:, c, :], ALU.mult, ALU.add,
                          [pbb, par_b, xT_b[c]], [xT_b[c]])
                      ACT(sq[:, c, :], xT[:, c, :], AF.Square, [xT_b[c]], [sq_b[c]])
              MM(psM[:, 128:128 + T], inv1024[:], sq[:, 7, :], False, True, [const_b, sq_b[7]], [psM_b])
              if l == nl - 1 and sidx + 1 < len(steps):
                  deferred_rel.extend([(sidx, 6), (sidx, 7)])
              else:
                  release(sidx, 6)
                  release(sidx, 7)

              if dbg == 8 and sidx == len(steps) - 1:
                  dump([(cell[:], xT[:], xT_b)])
              if l == nl - 1:
                  ACT(lnr[:], psM[:, 128:128 + T], AF.Ln, [psM_b], [lnr_b], bias=EPS)
                  ACT(rstd[:], lnr[:], AF.Exp, [lnr_b], [rstd_b], scale=-0.5)
                  for c in range(8):
                      STT(cell[:, c, :], cell[:, c, :], pv[:, O_FG + c:O_FG + c + 1], rstd[:], ALU.mult, ALU.mult,
                          [cell_b[c], par_b, rstd_b], [cell_b[c]])
                  fw.dma("sp", "ostore", oT_d.rearrange("(c p) t -> p c t", p=128)[:, :, t0:t0 + T], cell[:],
                         reads=cell_b, writes=[])
                  if dbg == 9 and sidx == 0:
                      raise _Stop()
        except _Stop:
            pass
        nc._fw_counts = dict(fw.cnt)
        for key_ in list(fw.cnt.keys()):
            if key_ not in fw.eng and fw.cnt[key_] > 0:
                fw._wait("sp", key_, fw.cnt[key_])
    return nc


_NC_CACHE = {}


def _feat(v):
    return np.ascontiguousarray(np.asarray(v, np.float32).reshape(8, 128).T)


def kernel(x, c, norm_g, w_ada, b_ada, w_in, rg_conv_w, rg_conv_b, rg_w_a, rg_b_a, rg_w_x, rg_b_x, rg_lambda,
           ml_conv_w, ml_conv_b, ml_w_q, ml_w_k, ml_w_v, ml_w_if, ml_b_if, ml_norm_g, w_out, final_g):
    f32 = lambda a: np.ascontiguousarray(np.asarray(a, np.float32))
    x = f32(x)
    B = x.shape[0]
    n_cores = 8
    if "nc" not in _NC_CACHE:
        _NC_CACHE["nc"] = build_nc()
    nc = _NC_CACHE["nc"]
    tri = np.triu(np.ones((128, 128), np.float32))
    in_maps = []
    for core in range(n_cores):
        b = core % B
        cols = []
        for l in range(NL):
            cols.append(_feat(norm_g[l]))
            for k in range(4):
                cols.append(_feat(rg_conv_w[l][k]))
            cols.append(_feat(rg_conv_b[l]))
            cols.append(_feat(rg_b_a[l]))
            cols.append(_feat(rg_b_x[l]))
            cols.append(_feat(rg_lambda[l]))
            for k in range(4):
                cols.append(_feat(ml_conv_w[l][k]))
            cols.append(_feat(ml_conv_b[l]))
            cols.append(_feat(ml_norm_g[l]))
            ba = np.asarray(b_ada[l], np.float32)
            for j in range(3):
                cols.append(_feat(ba[j * D:(j + 1) * D]))
        cols.append(_feat(final_g))
        cols.append(_feat(np.asarray(c, np.float32)[b]))
        pvh = np.ascontiguousarray(np.concatenate(cols, axis=1))
        assert pvh.shape == (128, NPV)
        in_maps.append({
            "xT": np.ascontiguousarray(x[b].T),
            "pv": pvh,
            "bif": f32(ml_b_if).reshape(1, NL * 8),
            "w_ada": f32(w_ada), "w_in": f32(w_in), "rg_w_a": f32(rg_w_a), "rg_w_x": f32(rg_w_x),
            "ml_w_q": f32(ml_w_q), "ml_w_k": f32(ml_w_k), "ml_w_v": f32(ml_w_v), "ml_w_if": f32(ml_w_if),
            "w_out": f32(w_out), "tri": tri,
        })
    res = run_bass_kernel_spmd(nc, in_maps, core_ids=list(range(n_cores)))
    out = np.empty((B, S, D), np.float32)
    for b in range(B):
        out[b] = res.results[b]["oT"].T
    return out
```

```python
import contextlib
import math
import numpy as np
import concourse.bass as bass
import concourse.mybir as mybir
from concourse.bass_utils import run_bass_kernel_spmd

F32 = mybir.dt.float32
BF16 = mybir.dt.bfloat16
ALU = mybir.AluOpType
AF = mybir.ActivationFunctionType

D = 1024
S = 4096
NL = 2
T = 256
NT = S // T
NCH = T // 128
EPS = 1e-6
LP = 144
NPV = NL * LP + 16
SLOT = 8448
NSLOT = 3


class Buf:
    __slots__ = ("name", "lw", "rd", "excl", "norec")

    def __init__(self, name, excl=False):
        self.name = name
        self.lw = None
        self.rd = {}
        self.excl = excl
        self.norec = False


class FW:
    def __init__(self, nc):
        self.nc = nc
        self.eng = {"pe": nc.tensor, "act": nc.scalar, "dve": nc.vector, "pool": nc.gpsimd, "sp": nc.sync}
        self.sems = {}
        self.cnt = {}
        self.seen = {e: {} for e in self.eng}
        self.hist = {e: {} for e in self.eng}
        for e in self.eng:
            self.sems[e] = nc.alloc_semaphore("s_" + e)
            self.cnt[e] = 0

    def _wait(self, e, key, val):
        if self.seen[e].get(key, 0) >= val:
            return
        self.eng[e].wait_ge(self.sems[key], val)
        self.seen[e][key] = val

    def deps(self, e, reads, writes, emit=True):
        pend = {}
        seen = self.seen[e]

        cand = []

        def need(k, v):
            cand.append((k, v))

        for b in reads:
            if b.lw is not None:
                k, v = b.lw
                if not (k == e and e == "pe"):
                    need(k, v)
        for b in writes:
            if b.lw is not None:
                k, v = b.lw
                if k != e:
                    need(k, v)
            for k, v in b.rd.items():
                if k != e:
                    need(k, v)
        cand.sort(key=lambda kv: (kv[0] not in self.hist, -kv[1]))
        for k, v in cand:
            if seen.get(k, 0) >= v or pend.get(k, 0) >= v:
                continue
            pend[k] = v
            h = self.hist.get(k)
            if h is not None and v in h:
                for k2, v2 in h[v].items():
                    if k2 != e and seen.get(k2, 0) < v2:
                        seen[k2] = v2
        waits = [(k, v) for k, v in pend.items() if seen.get(k, 0) < v]
        if emit:
            for k, v in waits:
                self._wait(e, k, v)
            return []
        for k, v in waits:
            seen[k] = v
        return waits

    def op(self, e, ins_fn, reads=(), writes=(), signal=True):
        xr = [b for b in reads if b.excl]
        if xr:
            reads = [b for b in reads if not b.excl]
            writes = list(writes) + xr
        if e in ("act", "dve", "pool", "pe"):
            waits = self.deps(e, reads, writes, emit=False)
            for k, v in waits[:-1]:
                self.eng[e].wait_ge(self.sems[k], v)
            ins = ins_fn(self.eng[e])
            if waits:
                k, v = waits[-1]
                ins._wait_ge(self.sems[k], v)
        else:
            self.deps(e, reads, writes)
            ins = ins_fn(self.eng[e])
        n = self.cnt[e] + 1
        if signal:
            ins.then_inc(self.sems[e], 1)
            self.cnt[e] = n
            self.hist[e][n] = dict(self.seen[e])
        for b in writes:
            b.lw = (e, n)
            b.rd = {}
        for b in reads:
            if b.rd.get(e, 0) < n:
                b.rd[e] = n
        return ins

    def dma(self, q, key, out, in_, reads=(), writes=()):
        if key not in self.sems:
            self.sems[key] = self.nc.alloc_semaphore("d_" + key)
            self.cnt[key] = 0
        self.deps(q, reads, writes)
        ins = self.eng[q].dma_start(out=out, in_=in_)
        v = self.cnt[key] + 16
        ins.then_inc(self.sems[key], 16)
        self.cnt[key] = v
        for b in writes:
            b.lw = (key, v)
            b.rd = {}
        for b in reads:
            if not b.norec:
                b.rd[key] = v
        return ins


class _Stop(Exception):
    pass


def build_nc(nt=NT, nl=NL, dbg=None):
    nc = bass.Bass("TRN2", target_bir_lowering=False)
    dt_in = lambda name, shape: nc.dram_tensor(name, shape, F32, kind="ExternalInput").ap()
    xT_d = dt_in("xT", [D, S])
    pv_d = dt_in("pv", [128, NPV])
    bif_d = dt_in("bif", [1, NL * 8])
    w_ada_d = dt_in("w_ada", [NL, D, 3 * D])
    w_in_d = dt_in("w_in", [NL, D, 5 * D])
    w_a_d = dt_in("rg_w_a", [NL, 8, 128, 128])
    w_x_d = dt_in("rg_w_x", [NL, 8, 128, 128])
    w_q_d = dt_in("ml_w_q", [NL, 4, 256, 256])
    w_k_d = dt_in("ml_w_k", [NL, 4, 256, 256])
    w_v_d = dt_in("ml_w_v", [NL, 4, 256, 256])
    w_if_d = dt_in("ml_w_if", [NL, 3 * D, 8])
    w_out_d = dt_in("w_out", [NL, 2 * D, D])
    tri_d = dt_in("tri", [128, 128])
    oT_d = nc.dram_tensor("oT", [D, S], F32, kind="ExternalOutput").ap()

    fw = FW(nc)
    with contextlib.ExitStack() as es:
        def SB(name, shape, dt):
            return es.enter_context(nc.sbuf_tensor(name, shape, dt))

        def PS(name):
            return es.enter_context(nc.psum_tensor(name, [128, 512], F32))

        ring = [SB("ring%d" % i, [128, SLOT], BF16) for i in range(NSLOT)]
        ring_b = [Buf("ring%d" % i) for i in range(NSLOT)]
        xT = SB("xT_sb", [128, 8, T], F32)
        xT_b = [Buf("xT%d" % c) for c in range(8)]
        sq = SB("sq", [128, 8, T], BF16)
        sq_b = [Buf("sq%d" % c) for c in range(8)]
        hb = SB("hb", [128, 8, T], BF16)
        hb_b = [Buf("hb%d" % c) for c in range(8)]
        rstd = SB("rstd", [128, T], F32)
        rstd_b = Buf("rstd")
        lnr = SB("lnr", [128, T], F32)
        lnr_b = Buf("lnr")
        rgx = SB("rgx", [128, 8, 4 + T], BF16)
        rgx_b = [Buf("rgx%d" % c) for c in range(8)]
        mlx = SB("mlx", [128, 8, 4 + T], BF16)
        mlx_b = [Buf("mlx%d" % c) for c in range(8)]
        yT = SB("yT", [128, 16, T], BF16)
        yT_b = [Buf("yT%d" % c) for c in range(16)]
        mxc = SB("mxc", [128, 8, T], BF16)
        mxc_b = [Buf("mxc%d" % c) for c in range(8)]
        qkv = SB("qkv", [128, 24, T], BF16)
        qkv_b = [Buf("qkv%d" % c) for c in range(24)]
        sgo = SB("sgo", [128, 8, T], F32)
        sgo_b = [Buf("sgo%d" % c) for c in range(8)]
        zsm = SB("zsm", [128, 8, T], F32)
        zsm_b = [Buf("zsm%d" % c) for c in range(8)]
        zsr = SB("zsr", [128, 8, T], F32)
        zsr_b = [Buf("zsr%d" % c) for c in range(8)]
        cell = SB("cell", [128, 8, T], F32)
        cell_b = [Buf("cell%d" % c) for c in range(8)]
        NTMP = 4
        tmp = [[SB("tmp%d_%d" % (s, i), [128, T], F32) for i in range(NTMP)] for s in range(2)]
        tmp_b = [[Buf("tmp%d_%d" % (s, i)) for i in range(NTMP)] for s in range(2)]
        NB = 4
        bX = SB("bX", [128, NB, T], F32)
        bR = SB("bR", [128, NB, T], F32)
        bI = SB("bI", [128, NB, T], F32)
        bA = SB("bA", [128, NB, T], F32)
        bM = SB("bM", [128, NB, T], F32)
        bX_b = [Buf("bX%d" % i) for i in range(NB)]
        bR_b = [Buf("bR%d" % i) for i in range(NB)]
        bI_b = [Buf("bI%d" % i) for i in range(NB)]
        bA_b = [Buf("bA%d" % i) for i in range(NB)]
        bM_b = [Buf("bM%d" % i) for i in range(NB)]
        xcb = SB("xcb", [128, NB, T], BF16)
        xcb_b = [Buf("xcb%d" % i) for i in range(NB)]
        dg = [SB("dg%d" % s, [128, 4, 128], BF16) for s in range(NB)]
        dg_b = [Buf("dg%d" % s) for s in range(NB)]
        eb16 = SB("eb16", [128, 4, 128], F32)
        eb16_b = Buf("eb16")
        lftri = SB("lftri", [128, 4, 128], F32)
        lftri_b = Buf("lftri")
        wT = [SB("wT%d" % s, [128, 128], F32) for s in range(2)]
        wT_b = [Buf("wT%d" % s) for s in range(2)]
        pT = [SB("pT%d" % s, [128, 128], BF16) for s in range(2)]
        pT_b = [Buf("pT%d" % s) for s in range(2)]
        qs = [SB("qs%d" % s, [128, 2, 128], BF16) for s in range(2)]
        qs_b = [Buf("qs%d" % s) for s in range(2)]
        vx = [SB("vx%d" % s, [128, 258], BF16) for s in range(2)]
        vx_b = [Buf("vx%d" % s) for s in range(2)]
        kw = [SB("kw%d" % s, [128, 256], BF16) for s in range(2)]
        kw_b = [Buf("kw%d" % s) for s in range(2)]
        rec = [SB("rec%d" % s, [128, 128], F32) for s in range(2)]
        rec_b = [Buf("rec%d" % s) for s in range(2)]
        nbc = [SB("nbc%d" % s, [128, 2, 128], BF16) for s in range(2)]
        nbc_b = [Buf("nbc%d" % s) for s in range(2)]
        gsb = SB("gsb", [128, NCH, 8], F32)
        gsb_b = Buf("gsb")
        nlf = SB("nlf", [128, NCH, 4], F32)
        nlf_b = Buf("nlf")
        ctm = SB("ctm", [128, NCH, 4], F32)
        ctm_b = Buf("ctm")
        ec = SB("ec", [128, NCH, 4], F32)
        ec_b = Buf("ec")
        wst = SB("wst", [128, NCH, 4], F32)
        wst_b = Buf("wst")
        dec = SB("dec", [128, NCH, 4], F32)
        dec_b = Buf("dec")
        halo_rg = [SB("halo_rg%d" % l, [128, 8, 3], BF16) for l in range(NL)]
        halo_ml = [SB("halo_ml%d" % l, [128, 8, 3], BF16) for l in range(NL)]
        halo_rg_b = [Buf("halo_rg%d" % l) for l in range(NL)]
        halo_ml_b = [Buf("halo_ml%d" % l) for l in range(NL)]
        rgh = [SB("rgh%d" % l, [128, 8], F32) for l in range(NL)]
        rgh_b = [[Buf("rgh%d_%d" % (l, c)) for c in range(8)] for l in range(NL)]
        C32 = [SB("C32_%d" % l, [128, 4, 2, 258], F32) for l in range(NL)]
        Cb = [SB("Cb_%d" % l, [128, 4, 2, 258], BF16) for l in range(NL)]
        C32_b = [[Buf("C32_%d_%d" % (l, h)) for h in range(4)] for l in range(NL)]
        Cb_b = [[Buf("Cb_%d_%d" % (l, h)) for h in range(4)] for l in range(NL)]
        ident = SB("ident", [128, 128], BF16)
        ones_b = SB("ones_b", [128, 128], BF16)
        inv1024 = SB("inv1024", [128, 128], BF16)
        inv256 = SB("inv256", [128, 128], BF16)
        tri32 = SB("tri_sb", [128, 128], F32)
        ones32 = SB("ones32", [128, 128], F32)
        pv = SB("pv_sb", [128, NPV], F32)
        dv = SB("dv", [128, NL, 64], F32)
        modT = SB("modT", [128, NL, 24], F32)
        bifb = SB("bif_sb", [128, NL * 8], F32)
        cact = SB("cact", [128, 8], F32)
        ctmp = SB("ctmp", [128, 8], F32)
        const_b = Buf("consts")
        par_b = Buf("params")
        psg = [PS("psg%d" % i) for i in range(3)]
        psg_b = [Buf("psg%d" % i, True) for i in range(3)]
        psS = PS("psS")
        psS_b = Buf("psS_S", True)
        psND = PS("psND")
        psND_b = Buf("psND", True)
        psVK = PS("psVK")
        psV_b = Buf("psV", True)
        psC = PS("psC")
        psC_b = Buf("psC", True)
        psM = PS("psM")
        psM_b = Buf("psM", True)

        nc._sbuf_left = nc.sbuf_bytes_remaining
        gctr = [0]

        def gen_bank():
            i = gctr[0] % 3
            gctr[0] += 1
            return psg[i], psg_b[i]

        def ACT(out, in_, func, reads, writes, bias=None, scale=None):
            kw_ = {}
            if bias is not None:
                kw_["bias"] = bias
            if scale is not None:
                kw_["scale"] = scale
            return fw.op("act", lambda e: e.activation(out=out, in_=in_, func=func, **kw_), reads, writes)

        def TT(eng, out, in0, in1, op, reads, writes):
            return fw.op(eng, lambda e: e.tensor_tensor(out=out, in0=in0, in1=in1, op=op), reads, writes)

        def TS(eng, out, in0, s1, s2, op0, op1, reads, writes):
            if s2 is None:
                return fw.op(eng, lambda e: e.tensor_scalar(out=out, in0=in0, scalar1=s1, scalar2=None, op0=op0), reads, writes)
            return fw.op(eng, lambda e: e.tensor_scalar(out=out, in0=in0, scalar1=s1, scalar2=s2, op0=op0, op1=op1), reads, writes)

        def STT(out, in0, scalar, in1, op0, op1, reads, writes):
            return fw.op("dve", lambda e: e.scalar_tensor_tensor(out=out, in0=in0, scalar=scalar, in1=in1, op0=op0, op1=op1), reads, writes)

        def CP(eng, out, in_, reads, writes):
            if eng == "act":
                return fw.op("act", lambda e: e.copy(out=out, in_=in_), reads, writes)
            return fw.op(eng, lambda e: e.tensor_copy(out=out, in_=in_), reads, writes)

        def RECIP(out, in_, reads, writes):
            return fw.op("dve", lambda e: e.reciprocal(out=out, in_=in_), reads, writes)

        def MM(out, lhsT, rhs, start, stop, reads, writes, signal=True):
            return fw.op("pe", lambda e: e.matmul(out, lhsT=lhsT, rhs=rhs, start=start, stop=stop), reads, writes, signal=signal)

        def MSET(eng, ap, val, writes):
            return fw.op(eng, lambda e: e.memset(ap, val), (), writes)

        MSET("pool", ones_b[:], 1.0, [const_b])
        MSET("pool", inv1024[:], 1.0 / 1024.0, [const_b])
        MSET("pool", inv256[:], 1.0 / 256.0, [const_b])
        MSET("pool", ones32[:], 1.0, [const_b])
        for s in range(2):
            MSET("pool", vx[s][:], 1.0, [vx_b[s]])
        for l in range(NL):
            MSET("pool", halo_rg[l][:], 0.0, [halo_rg_b[l]])
            MSET("pool", halo_ml[l][:], 0.0, [halo_ml_b[l]])
            MSET("pool", rgh[l][:], 0.0, rgh_b[l])
            MSET("pool", C32[l][:], 0.0, C32_b[l])
            MSET("pool", Cb[l][:], 0.0, Cb_b[l])
        fw.dma("sp", "c_tri", tri32[:], tri_d, writes=[const_b])
        fw.dma("sp", "c_pv", pv[:], pv_d, writes=[par_b])
        fw.dma("sp", "c_bif", bifb[:], bif_d.partition_broadcast(128), writes=[par_b])
        CP("dve", ident[:, 0:1], tri32[:, 0:1], [const_b], [const_b])
        TT("dve", ident[:, 1:128], tri32[:, 1:128], tri32[:, 0:127], ALU.subtract, [const_b], [const_b])

        def pcol(l, off, n=8):
            b = l * LP + off
            return pv[:, b:b + n]

        O_NG, O_RCW, O_RCB, O_BA, O_BX, O_LAM, O_MCW, O_MCB, O_MNG, O_BADA = 0, 8, 40, 48, 56, 64, 72, 104, 112, 120
        O_FG = NL * LP
        O_CT = NL * LP + 8
        for l in range(NL):
            TS("dve", dv[:, l, 0:8], pcol(l, O_BA), -1.0, None, ALU.mult, None, [par_b], [par_b])
            TS("dve", dv[:, l, 8:16], pcol(l, O_BX), -1.0, None, ALU.mult, None, [par_b], [par_b])
            TS("dve", dv[:, l, 16:24], pcol(l, O_MCB), -1.0, None, ALU.mult, None, [par_b], [par_b])
            ACT(dv[:, l, 48:56], pcol(l, O_LAM), AF.Exp, [par_b], [par_b], scale=-1.0)
            ACT(dv[:, l, 48:56], dv[:, l, 48:56], AF.Ln, [par_b], [par_b], bias=1.0)
            TS("dve", dv[:, l, 24:32], dv[:, l, 48:56], -8.0, None, ALU.mult, None, [par_b], [par_b])
            TS("dve", dv[:, l, 32:40], dv[:, l, 48:56], -16.0, None, ALU.mult, None, [par_b], [par_b])
        ACT(ctmp[:], pv[:, O_CT:O_CT + 8], AF.Exp, [par_b], [par_b], scale=-1.0)
        TS("dve", ctmp[:], ctmp[:], 1.0, None, ALU.add, None, [par_b], [par_b])
        RECIP(ctmp[:], ctmp[:], [par_b], [par_b])
        TT("dve", cact[:], pv[:, O_CT:O_CT + 8], ctmp[:], ALU.mult, [par_b], [par_b])
        slot_ctr = [0]

        def next_slot():
            i = slot_ctr[0] % NSLOT
            slot_ctr[0] += 1
            return i

        for l in range(NL):
            for pi in range(6):
                si = next_slot()
                r32 = ring[si][:].bitcast(F32)
                dst = r32[:, 0:4096].rearrange("p (kc n) -> p kc n", kc=8)
                src = w_ada_d[l].rearrange("(kc p) n -> p kc n", p=128)[:, :, pi * 512:(pi + 1) * 512]
                fw.dma("sp", "ada%d" % si, dst, src, writes=[ring_b[si]])
                for mm_ in range(4):
                    m = pi * 4 + mm_
                    for kc in range(8):
                        MM(psM[:, m:m + 1], dst[:, kc, mm_ * 128:(mm_ + 1) * 128], cact[:, kc:kc + 1], kc == 0, kc == 7,
                           [ring_b[si], par_b], [psM_b], signal=(kc == 7))
            TT("dve", modT[:, l, :], psM[:, 0:24], pcol(l, O_BADA, 24), ALU.add, [psM_b, par_b], [par_b])
            STT(dv[:, l, 40:48], modT[:, l, 8:16], 1.0, pcol(l, O_NG), ALU.add, ALU.mult, [par_b], [par_b])

        def load_piece(si, l, piece):
            key = "ring%d" % si
            r = ring[si]
            if piece < 5:
                dst = r[:, 0:8192].rearrange("p (kc n) -> p kc n", kc=8)
                src = w_in_d[l].rearrange("(kc p) n -> p kc n", p=128)[:, :, piece * 1024:(piece + 1) * 1024]
                fw.dma("pool", key, dst, src, reads=[xl_b], writes=[ring_b[si]])
            elif piece == 5:
                fw.dma("pool", key, r[:, 0:1024].rearrange("p (h e) -> p h e", h=8),
                       w_a_d[l].rearrange("h d e -> d h e"), reads=[xl_b], writes=[ring_b[si]])
                fw.dma("pool", key, r[:, 1024:2048].rearrange("p (h e) -> p h e", h=8),
                       w_x_d[l].rearrange("h d e -> d h e"), writes=[])
                for wi, wd in enumerate((w_q_d, w_k_d, w_v_d)):
                    for h in range(4):
                        o = 2048 + wi * 2048 + h * 512
                        fw.dma("pool", key, r[:, o:o + 512].rearrange("p (dc e) -> p dc e", dc=2),
                               wd[l, h].rearrange("(dc p) e -> p dc e", p=128), writes=[])
                fw.dma("pool", key, r[:, 8192:8384].rearrange("p (kc g) -> p kc g", kc=24),
                       w_if_d[l].rearrange("(kc p) g -> p kc g", p=128), writes=[])
                ring_b[si].lw = (key, fw.cnt[key])
            else:
                hf = piece - 6
                dst = r[:, 0:8192].rearrange("p (kc n) -> p kc n", kc=8)
                src = w_out_d[l].rearrange("(kc p) n -> p kc n", p=128)[:, hf * 8:(hf + 1) * 8, :]
                fw.dma("pool", key, dst, src, reads=[xl_b], writes=[ring_b[si]])

        steps = [(ti, l) for ti in range(nt) for l in range(nl)]
        ORDER = [0, 1, 2, 5, 3, 4, 6, 7]
        piece_seq = [(si_, p) for si_ in range(len(steps)) for p in ORDER]
        slot_of = {}
        free_slots = list(range(NSLOT))
        next_load = [0]

        xl_b = Buf("xload_done")
        xl_b.norec = True
        deferred_rel = []

        def pump():
            while next_load[0] < len(piece_seq) and free_slots:
                key_ = piece_seq[next_load[0]]
                si = free_slots.pop(0)
                load_piece(si, steps[key_[0]][1], key_[1])
                slot_of[key_] = si
                next_load[0] += 1

        def get_piece(sidx, p):
            pump()
            assert (sidx, p) in slot_of, ("ring deadlock", sidx, p)
            si = slot_of[(sidx, p)]
            return ring[si], ring_b[si]

        def release(sidx, p):
            si = slot_of.pop((sidx, p))
            free_slots.append(si)
            pump()

        def dump(srcs):
            MSET("dve", cell[:], 0.0, cell_b)
            for (dst, ap_, bufs) in srcs:
                CP("dve", dst, ap_, bufs, cell_b)
            fw.dma("sp", "ostore", oT_d.rearrange("(c p) t -> p c t", p=128)[:, :, 0:T], cell[:], reads=cell_b, writes=[])
            raise _Stop()

        try:
          if dbg == 0:
            dump([(cell[:, 0, 0:64], dv[:, 0, :], [par_b]), (cell[:, 1, 0:24], modT[:, 0, :], [par_b]),
                  (cell[:, 2, 0:128], ident[:], [const_b]), (cell[:, 3, 0:16], bifb[:], [par_b])])
          out_sem_keys = []
          for sidx, (ti, l) in enumerate(steps):
              t0 = ti * T
              if l == 0:
                  xsrc = xT_d.rearrange("(c p) t -> p c t", p=128)
                  fw.dma("sp", "xloadA", xT[:, 0:4, :], xsrc[:, 0:4, t0:t0 + T], reads=[], writes=xT_b[0:4] + [xl_b])
                  fw.dma("sp", "xloadB", xT[:, 4:8, :], xsrc[:, 4:8, t0:t0 + T], reads=[], writes=xT_b[4:8] + [xl_b])
              for (rs_, rp_) in deferred_rel:
                  release(rs_, rp_)
              del deferred_rel[:]
              if l == 0:
                  for kc in range(8):
                      ACT(sq[:, kc, :], xT[:, kc, :], AF.Square, [xT_b[kc]], [sq_b[kc]])
                      MM(psM[:, 128:128 + T], inv1024[:], sq[:, kc, :], kc == 0, kc == 7, [const_b, sq_b[kc]], [psM_b], signal=(kc == 7))
              ACT(lnr[:], psM[:, 128:128 + T], AF.Ln, [psM_b], [lnr_b], bias=EPS)
              ACT(rstd[:], lnr[:], AF.Exp, [lnr_b], [rstd_b], scale=-0.5)
              for c in range(8):
                  s = c % 2
                  t_ = tmp[s][0]
                  STT(t_[:], xT[:, c, :], dv[:, l, 40 + c:41 + c], rstd[:], ALU.mult, ALU.mult,
                      [xT_b[c], rstd_b, par_b], [tmp_b[s][0]])
                  ACT(hb[:, c, :], t_[:], AF.Identity, [tmp_b[s][0], par_b], [hb_b[c]], bias=modT[:, l, c:c + 1])

              if dbg == 1 and sidx == len(steps) - 1:
                  dump([(cell[:], hb[:], hb_b)])
              def win_chunk(piece, c):
                  r, rb = get_piece(sidx, piece)
                  w3 = r[:, 0:8192].rearrange("p (kc n) -> p kc n", kc=8)
                  pb_, pbb_ = gen_bank()
                  for kc in range(8):
                      MM(pb_[:, 0:T], w3[:, kc, c * 128:(c + 1) * 128], hb[:, kc, :], kc == 0, kc == 7,
                         [rb, hb_b[kc]], [pbb_], signal=(kc == 7))
                  return pb_, pbb_

              def sigmoid_from(out, src, reads, wbuf, nbias=None):
                  if nbias is None:
                      ACT(out, src, AF.Exp, reads, [wbuf], scale=-1.0)
                  else:
                      ACT(out, src, AF.Exp, reads + [par_b], [wbuf], scale=-1.0, bias=nbias)
                  ACT(out, out, AF.Ln, [wbuf], [wbuf], bias=1.0)
                  ACT(out, out, AF.Exp, [wbuf], [wbuf], scale=-1.0)

              CP("pool", rgx[:, :, 1:4], halo_rg[l][:], [halo_rg_b[l]], rgx_b)
              CP("pool", mlx[:, :, 1:4], halo_ml[l][:], [halo_ml_b[l]], mlx_b)

              smallw = [None]

              def get_small():
                  if smallw[0] is None:
                      smallw[0] = get_piece(sidx, 5)
                  return smallw[0]

              for c in range(8):
                  pb, pbb = win_chunk(0, c)
                  CP("act", rgx[:, c, 4:4 + T], pb[:, 0:T], [pbb], [rgx_b[c]])
              CP("pool", halo_rg[l][:], rgx[:, :, T + 1:T + 4], rgx_b, [halo_rg_b[l]])
              release(sidx, 0)
              if dbg == 2 and sidx == len(steps) - 1:
                  dump([(cell[:], rgx[:, :, 4:4 + T], rgx_b)])
              for c in range(8):
                  pb, pbb = win_chunk(1, c)
                  s = c % 2
                  sz = tmp[s][1]
                  sigmoid_from(sz[:], pb[:, 0:T], [pbb], tmp_b[s][1])
                  TT("dve", zsr[:, c, :], pb[:, 0:T], sz[:], ALU.mult, [pbb, tmp_b[s][1]], [zsr_b[c]])
              release(sidx, 1)
              for c in range(8):
                  pb, pbb = win_chunk(2, c)
                  CP("act", mlx[:, c, 4:4 + T], pb[:, 0:T], [pbb], [mlx_b[c]])
              CP("pool", halo_ml[l][:], mlx[:, :, T + 1:T + 4], mlx_b, [halo_ml_b[l]])
              release(sidx, 2)

              sw, swb = get_small()
              w_a3 = sw[:, 0:1024].rearrange("p (h e) -> p h e", h=8)
              w_x3 = sw[:, 1024:2048].rearrange("p (h e) -> p h e", h=8)
              w_q4 = sw[:, 2048:4096].rearrange("p (h dc e) -> p h dc e", h=4, dc=2)
              w_k4 = sw[:, 4096:6144].rearrange("p (h dc e) -> p h dc e", h=4, dc=2)
              w_v4 = sw[:, 6144:8192].rearrange("p (h dc e) -> p h dc e", h=4, dc=2)
              w_if3 = sw[:, 8192:8384].rearrange("p (kc g) -> p kc g", kc=24)

              def diag_build(ci, c, wcol_off):
                  for k in range(4):
                      col = l * LP + wcol_off + k * 8 + c
                      if k < 2:
                          ACT(dg[ci][:, k, :], ident[:], AF.Identity, [const_b, par_b], [dg_b[ci]], scale=pv[:, col:col + 1])
                      else:
                          TS("dve", dg[ci][:, k, :], ident[:], pv[:, col:col + 1], None, ALU.mult, None,
                             [const_b, par_b], [dg_b[ci]])

              def conv_mm(ci, c, src, src_b):
                  pb_, pbb_ = gen_bank()
                  for k in range(4):
                      MM(pb_[:, 0:T], dg[ci][:, k, :], src[:, c, 1 + k:1 + k + T], k == 0, k == 3, [dg_b[ci], src_b[c]], [pbb_],
                         signal=(k == 3))
                  return pb_, pbb_

              for bt in range(2):
                  cs = [bt * NB + ci for ci in range(NB)]
                  for ci, c in enumerate(cs):
                      diag_build(ci, c, O_MCW)
                  for ci, c in enumerate(cs):
                      pb, pbb = conv_mm(ci, c, mlx, mlx_b)
                      TS("dve", bR[:, ci, :], pb[:, 0:T], pcol(l, O_MCB)[:, c:c + 1], None, ALU.add, None, [pbb, par_b], [bR_b[ci]])
                  for hf in range(2):
                      sl = slice(2 * hf, 2 * hf + 2)
                      ACT(bX[:, sl, :], bR[:, sl, :], AF.Exp, bR_b[sl], bX_b[sl], scale=-1.0)
                  for hf in range(2):
                      sl = slice(2 * hf, 2 * hf + 2)
                      ACT(bX[:, sl, :], bX[:, sl, :], AF.Ln, bX_b[sl], bX_b[sl], bias=1.0)
                  for hf in range(2):
                      sl = slice(2 * hf, 2 * hf + 2)
                      ACT(bX[:, sl, :], bX[:, sl, :], AF.Exp, bX_b[sl], bX_b[sl], scale=-1.0)
                  for hf in range(2):
                      sl = slice(2 * hf, 2 * hf + 2)
                      c0_ = bt * NB + 2 * hf
                      TT("dve", mxc[:, c0_:c0_ + 2, :], bR[:, sl, :], bX[:, sl, :], ALU.mult, bR_b[sl] + bX_b[sl], mxc_b[c0_:c0_ + 2])
              if dbg == 4 and sidx == len(steps) - 1:
                  dump([(cell[:], mxc[:], mxc_b)])
              for wi, (w4, src, src_b, off) in enumerate(((w_q4, mxc, mxc_b, 0), (w_k4, mxc, mxc_b, 0), (w_v4, mlx, mlx_b, 4))):
                  for h in range(4):
                      for ecx in range(2):
                          pb, pbb = gen_bank()
                          for dc in range(2):
                              MM(pb[:, 0:T], w4[:, h, dc, ecx * 128:(ecx + 1) * 128], src[:, 2 * h + dc, off:off + T],
                                 dc == 0, dc == 1, [swb, src_b[2 * h + dc]], [pbb], signal=(dc == 1))
                          oc = wi * 8 + 2 * h + ecx
                          eng = "act" if (oc % 2 == 0) else "dve"
                          CP(eng, qkv[:, oc, :], pb[:, 0:T], [pbb], [qkv_b[oc]])
              for j in range(NCH):
                  for kc in range(24):
                      MM(psM[:, 32 + j * 8:32 + (j + 1) * 8], qkv[:, kc, j * 128:(j + 1) * 128], w_if3[:, kc, :], kc == 0, kc == 23,
                         [qkv_b[kc], swb], [psM_b], signal=(kc == 23))
              g_ps = psM[:, 32:32 + NCH * 8].rearrange("p (j g) -> p j g", j=NCH)
              TT("dve", gsb[:], g_ps, bifb[:, l * 8:(l + 1) * 8].unsqueeze(1).broadcast_to([128, NCH, 8]), ALU.add,
                 [psM_b, par_b], [gsb_b])
              if dbg == 5 and sidx == len(steps) - 1:
                  dump([(cell[:, 0, 0:NCH * 8], gsb[:].rearrange("p j g -> p (j g)"), [gsb_b])])
              ACT(nlf[:], gsb[:, :, 4:8], AF.Exp, [gsb_b], [nlf_b], scale=-1.0)
              ACT(nlf[:], nlf[:], AF.Ln, [nlf_b], [nlf_b], bias=1.0)
              nlf2 = nlf[:].rearrange("p j h -> p (j h)")
              MM(psM[:, 64:64 + NCH * 4], tri32[:], nlf2, True, True, [const_b, nlf_b], [psM_b])
              MM(psM[:, 96:96 + NCH * 4], ones32[:], nlf2, True, True, [const_b, nlf_b], [psM_b])
              nb_ps = psM[:, 64:64 + NCH * 4].rearrange("p (j h) -> p j h", j=NCH)
              nbl_ps = psM[:, 96:96 + NCH * 4].rearrange("p (j h) -> p j h", j=NCH)
              TT("dve", ctm[:], gsb[:, :, 0:4], nb_ps, ALU.add, [gsb_b, psM_b], [ctm_b])
              ACT(ec[:], ctm[:], AF.Exp, [ctm_b], [ec_b])
              TT("dve", wst[:], ctm[:], nbl_ps, ALU.subtract, [ctm_b, psM_b], [wst_b])
              ACT(wst[:], wst[:], AF.Exp, [wst_b], [wst_b])
              ACT(dec[:], nbl_ps, AF.Exp, [psM_b], [dec_b], scale=-1.0)

              kbank = {}

              def ml_prep(j):
                  for h in range(4):
                      TS("pool", lftri[:, h, :], tri32[:], nlf[:, j, h:h + 1], -1.0, ALU.mult, ALU.mult,
                         [const_b, nlf_b], [lftri_b])
                  pbb_, pbbb_ = gen_bank()
                  MM(pbb_[:, 0:512], ones32[:], lftri[:].rearrange("p h t -> p (h t)"), True, True, [const_b, lftri_b], [pbbb_])
                  ACT(eb16[:].rearrange("p h t -> p (h t)"), pbb_[:, 0:512], AF.Exp, [pbbb_], [eb16_b], bias=-math.log(16.0))

              def ml_A1(idx):
                  j, h = idx // 4, idx % 4
                  ts_ = slice(j * 128, (j + 1) * 128)
                  if h == 0:
                      ml_prep(j)
                  q0 = 2 * h
                  k0 = 8 + 2 * h
                  for dc in range(2):
                      MM(psS[:, 0:128], qkv[:, k0 + dc, ts_], qkv[:, q0 + dc, ts_], dc == 0, dc == 1,
                         [qkv_b[k0 + dc], qkv_b[q0 + dc]], [psS_b], signal=(dc == 1))
                  for dc in range(2):
                      MM(psVK[:, 0:256], mlx[:, 2 * h + dc, 4 + j * 128:4 + (j + 1) * 128], w_v4[:, h, dc, :], dc == 0, dc == 1,
                         [mlx_b[2 * h + dc], swb], [psV_b], signal=(dc == 1))
                  for dc in range(2):
                      MM(psVK[:, 256:512], mxc[:, 2 * h + dc, ts_], w_k4[:, h, dc, :], dc == 0, dc == 1,
                         [mxc_b[2 * h + dc], swb], [psV_b], signal=(dc == 1))

              def ml_A2(idx):
                  j, h = idx // 4, idx % 4
                  ts_ = slice(j * 128, (j + 1) * 128)
                  s = idx % 2
                  q0 = 2 * h
                  STT(wT[s][:], eb16[:, h, :], ec[:, j, h:h + 1], tri32[:], ALU.mult, ALU.mult,
                      [eb16_b, ec_b, const_b], [wT_b[s]])
                  CP("act", vx[s][:, 0:256], psVK[:, 0:256], [psV_b], [vx_b[s]])
                  TT("pool", qs[s][:], qkv[:, q0:q0 + 2, ts_], eb16[:, h, :].unsqueeze(1).broadcast_to([128, 2, 128]),
                     ALU.mult, [qkv_b[q0], qkv_b[q0 + 1], eb16_b], [qs_b[s]])
                  TS("dve", kw[s][:], psVK[:, 256:512], wst[:, j, h:h + 1], None, ALU.mult, None, [psV_b, wst_b], [kw_b[s]])
                  TT("dve", pT[s][:], psS[:, 0:128], wT[s][:], ALU.mult, [psS_b, wT_b[s]], [pT_b[s]])
                  CP("pool", nbc[s][:], Cb[l][:, h, :, 256:257].broadcast_to([128, 2, 128]), [Cb_b[l][h]], [nbc_b[s]])

              def ml_B1(idx):
                  j, h = idx // 4, idx % 4
                  s = idx % 2
                  for vc in range(2):
                      o = vc * 128
                      MM(psND[:, o:o + 128], vx[s][:, vc * 128:(vc + 1) * 128], pT[s][:], True, False,
                         [vx_b[s], pT_b[s]], [psND_b], signal=False)
                      for dc in range(2):
                          MM(psND[:, o:o + 128], Cb[l][:, h, dc, vc * 128:(vc + 1) * 128], qs[s][:, dc, :], False, dc == 1,
                             [Cb_b[l][h], qs_b[s]], [psND_b], signal=False)
                  MM(psND[:, 256:384], ones_b[:], pT[s][:], True, False, [const_b, pT_b[s]], [psND_b], signal=False)
                  for dc in range(2):
                      MM(psND[:, 256:384], nbc[s][:, dc, :], qs[s][:, dc, :], False, dc == 1, [nbc_b[s], qs_b[s]], [psND_b],
                         signal=False)
                  for dc in range(2):
                      MM(psND[:, 384 + 2 * dc:386 + 2 * dc], kw[s][:, dc * 128:(dc + 1) * 128], vx[s][:, 256:258], True, True,
                         [kw_b[s], vx_b[s]], [psND_b], signal=(dc == 1))
                  for dc in range(2):
                      MM(psC[:, dc * 256:(dc + 1) * 256], kw[s][:, dc * 128:(dc + 1) * 128], vx[s][:, 0:256], True, True,
                         [kw_b[s], vx_b[s]], [psC_b], signal=(dc == 1))

              def ml_B2(idx):
                  j, h = idx // 4, idx % 4
                  ts_ = slice(j * 128, (j + 1) * 128)
                  s = idx % 2
                  ACT(rec[s][:], psND[:, 256:384], AF.Abs, [psND_b], [rec_b[s]])
                  STT(C32[l][:, h, :, 0:256], C32[l][:, h, :, 0:256], dec[:, j, h:h + 1],
                      psC[:, 0:512].rearrange("p (dc e) -> p dc e", dc=2), ALU.mult, ALU.add,
                      [C32_b[l][h], dec_b, psC_b], [C32_b[l][h]])
                  TS("dve", rec[s][:], rec[s][:], 1.0, None, ALU.max, None, [rec_b[s]], [rec_b[s]])
                  STT(C32[l][:, h, :, 256:258], C32[l][:, h, :, 256:258], dec[:, j, h:h + 1],
                      psND[:, 384:388].rearrange("p (dc e) -> p dc e", dc=2), ALU.mult, ALU.add,
                      [C32_b[l][h], dec_b, psND_b], [C32_b[l][h]])
                  RECIP(rec[s][:], rec[s][:], [rec_b[s]], [rec_b[s]])
                  CP("dve", Cb[l][:, h, :, :], C32[l][:, h, :, :], [C32_b[l][h]], [Cb_b[l][h]])
                  TT("dve", cell[:, 2 * h:2 * h + 2, ts_], psND[:, 0:256].rearrange("p (v t) -> p v t", v=2),
                     rec[s][:].unsqueeze(1).broadcast_to([128, 2, 128]), ALU.mult, [psND_b, rec_b[s]],
                     [cell_b[2 * h], cell_b[2 * h + 1]])

              NIT = NCH * 4
              ml_sched = []
              for i in range(NIT + 1):
                  if i < NIT:
                      ml_sched.append((ml_A1, i))
                  if i >= 1:
                      ml_sched.append((ml_B1, i - 1))
                  if i < NIT:
                      ml_sched.append((ml_A2, i))
                  if i >= 1:
                      ml_sched.append((ml_B2, i - 1))
              ml_pos = [0]

              def ml_emit(n):
                  for _ in range(n):
                      if ml_pos[0] < len(ml_sched):
                          f_, a_ = ml_sched[ml_pos[0]]
                          f_(a_)
                          ml_pos[0] += 1

              fill_pos = [0]

              def fill_pair():
                  k = fill_pos[0]
                  if k >= 8:
                      return
                  fill_pos[0] += 1
                  piece = 3 if k < 4 else 4
                  c0 = (k % 4) * 2
                  banks = []
                  for c in (c0, c0 + 1):
                      banks.append(win_chunk(piece, c))
                  outs = []
                  for ii, c in enumerate((c0, c0 + 1)):
                      pb, pbb = banks[ii]
                      if piece == 3:
                          o_, ob_ = sgo[:, c, :], sgo_b[c]
                      else:
                          o_, ob_ = tmp[ii][1][:], tmp_b[ii][1]
                      outs.append((o_, ob_))
                      ACT(o_, pb[:, 0:T], AF.Exp, [pbb], [ob_], scale=-1.0)
                  for (o_, ob_) in outs:
                      ACT(o_, o_, AF.Ln, [ob_], [ob_], bias=1.0)
                  for (o_, ob_) in outs:
                      ACT(o_, o_, AF.Exp, [ob_], [ob_], scale=-1.0)
                  if piece == 4:
                      for ii, c in enumerate((c0, c0 + 1)):
                          pb, pbb = banks[ii]
                          TT("dve", zsm[:, c, :], pb[:, 0:T], outs[ii][0], ALU.mult, [pbb, outs[ii][1]], [zsm_b[c]])
                  if k == 3:
                      release(sidx, 3)
                  if k == 7:
                      release(sidx, 4)

              per_gap = 1
              for bt in range(2):
                  cs = [bt * NB + ci for ci in range(NB)]
                  for ci, c in enumerate(cs):
                      diag_build(ci, c, O_RCW)
                  ml_emit(per_gap)
                  fill_pair()
                  for ci, c in enumerate(cs):
                      pb, pbb = conv_mm(ci, c, rgx, rgx_b)
                      ACT(bX[:, ci, :], pb[:, 0:T], AF.Identity, [pbb, par_b], [bX_b[ci]], bias=pcol(l, O_RCB)[:, c:c + 1])
                      ml_emit(1)
                  ml_emit(per_gap)
                  fill_pair()
                  for ci, c in enumerate(cs):
                      CP("dve", xcb[:, ci, :], bX[:, ci, :], [bX_b[ci]], [xcb_b[ci]])
                  gb = []
                  for ci, c in enumerate(cs):
                      pr, prb = gen_bank()
                      MM(pr[:, 0:T], w_a3[:, c, :], xcb[:, ci, :], True, True, [swb, xcb_b[ci]], [prb])
                      ACT(bR[:, ci, :], pr[:, 0:T], AF.Exp, [prb, par_b], [bR_b[ci]], scale=-1.0, bias=dv[:, l, c:c + 1])
                      pi_, pib = gen_bank()
                      MM(pi_[:, 0:T], w_x3[:, c, :], xcb[:, ci, :], True, True, [swb, xcb_b[ci]], [pib])
                      ACT(bI[:, ci, :], pi_[:, 0:T], AF.Exp, [pib, par_b], [bI_b[ci]], scale=-1.0, bias=dv[:, l, 8 + c:9 + c])
                      ml_emit(1)
                  ml_emit(per_gap)
                  fill_pair()
                  for hf in range(2):
                      sl = slice(2 * hf, 2 * hf + 2)
                      ACT(bR[:, sl, :], bR[:, sl, :], AF.Ln, bR_b[sl], bR_b[sl], bias=1.0)
                      ACT(bI[:, sl, :], bI[:, sl, :], AF.Ln, bI_b[sl], bI_b[sl], bias=1.0)
                  for hf in range(2):
                      sl = slice(2 * hf, 2 * hf + 2)
                      ACT(bR[:, sl, :], bR[:, sl, :], AF.Exp, bR_b[sl], bR_b[sl], scale=-1.0)
                      ACT(bI[:, sl, :], bI[:, sl, :], AF.Exp, bI_b[sl], bI_b[sl], scale=-1.0)
                  ml_emit(per_gap)
                  fill_pair()
                  for ci, c in enumerate(cs):
                      ACT(bA[:, ci, :], bR[:, ci, :], AF.Exp, [bR_b[ci], par_b], [bA_b[ci]], scale=dv[:, l, 24 + c:25 + c])
                      ACT(bM[:, ci, :], bR[:, ci, :], AF.Exp, [bR_b[ci], par_b], [bM_b[ci]], scale=dv[:, l, 32 + c:33 + c])
                      ml_emit(1)
                  for hf in range(2):
                      sl = slice(2 * hf, 2 * hf + 2)
                      TT("dve", bI[:, sl, :], bI[:, sl, :], bX[:, sl, :], ALU.mult, bI_b[sl] + bX_b[sl], bI_b[sl])
                  ml_emit(per_gap)
                  fill_pair()
                  for hf in range(2):
                      sl = slice(2 * hf, 2 * hf + 2)
                      ACT(bM[:, sl, :], bM[:, sl, :], AF.Ln, bM_b[sl], bM_b[sl], scale=-1.0, bias=1.0)
                  for hf in range(2):
                      sl = slice(2 * hf, 2 * hf + 2)
                      ACT(bM[:, sl, :], bM[:, sl, :], AF.Exp, bM_b[sl], bM_b[sl], scale=0.5)
                  ml_emit(per_gap)
                  fill_pair()
                  for hf in range(2):
                      sl = slice(2 * hf, 2 * hf + 2)
                      TT("dve", bI[:, sl, :], bI[:, sl, :], bM[:, sl, :], ALU.mult, bI_b[sl] + bM_b[sl], bI_b[sl])
                  for ci, c in enumerate(cs):
                      fw.op("dve", lambda e: e.tensor_tensor_scan(out=bM[:, ci, :], data0=bA[:, ci, :], data1=bI[:, ci, :],
                                                                  initial=rgh[l][:, c:c + 1], op0=ALU.mult, op1=ALU.add),
                            [bA_b[ci], bI_b[ci], rgh_b[l][c]], [bM_b[ci]])
                  ml_emit(per_gap)
                  fill_pair()
                  for ci, c in enumerate(cs):
                      CP("act", rgh[l][:, c:c + 1], bM[:, ci, T - 1:T], [bM_b[ci]], [rgh_b[l][c]])
                      TT("dve", yT[:, c, :], bM[:, ci, :], zsr[:, c, :], ALU.mult, [bM_b[ci], zsr_b[c]], [yT_b[c]])
              ml_emit(len(ml_sched))
              while fill_pos[0] < 8:
                  fill_pair()
              release(sidx, 5)
              if dbg == 3 and sidx == len(steps) - 1:
                  dump([(cell[:], yT[:, 0:8, :], yT_b[0:8])])
              if dbg == 6 and sidx == len(steps) - 1:
                  fw.dma("sp", "ostore", oT_d.rearrange("(c p) t -> p c t", p=128)[:, :, 0:T], cell[:], reads=cell_b, writes=[])
                  raise _Stop()
              for h in range(4):
                  hs_ = slice(2 * h, 2 * h + 2)
                  TT("dve", cell[:, hs_, :], cell[:, hs_, :], sgo[:, hs_, :], ALU.mult, cell_b[hs_] + sgo_b[hs_], cell_b[hs_])
                  ACT(sq[:, hs_, :], cell[:, hs_, :], AF.Square, cell_b[hs_], sq_b[hs_])
              for h in range(4):
                  pb, pbb = gen_bank()
                  for dc in range(2):
                      MM(pb[:, 0:T], inv256[:], sq[:, 2 * h + dc, :], dc == 0, dc == 1, [const_b, sq_b[2 * h + dc]], [pbb],
                         signal=(dc == 1))
                  s = h % 2
                  ln_, rs_ = tmp[s][0], tmp[s][1]
                  ACT(ln_[:], pb[:, 0:T], AF.Ln, [pbb], [tmp_b[s][0]], bias=EPS)
                  ACT(rs_[:], ln_[:], AF.Exp, [tmp_b[s][0]], [tmp_b[s][1]], scale=-0.5)
                  for dc in range(2):
                      c = 2 * h + dc
                      t1 = tmp[s][2 + dc]
                      STT(t1[:], cell[:, c, :], pcol(l, O_MNG)[:, c:c + 1], zsm[:, c, :], ALU.mult, ALU.mult,
                          [cell_b[c], par_b, zsm_b[c]], [tmp_b[s][2 + dc]])
                      TT("dve", yT[:, 8 + c, :], t1[:], rs_[:], ALU.mult, [tmp_b[s][2 + dc], tmp_b[s][1]], [yT_b[8 + c]])

              if dbg == 7 and sidx == len(steps) - 1:
                  dump([(cell[:], yT[:, 8:16, :], yT_b[8:16])])
              pieces_o = [get_piece(sidx, 6), get_piece(sidx, 7)]
              for c in range(8):
                  pb, pbb = gen_bank()
                  for kc in range(16):
                      r, rb = pieces_o[kc // 8]
                      w3 = r[:, 0:8192].rearrange("p (kc n) -> p kc n", kc=8)
                      MM(pb[:, 0:T], w3[:, kc % 8, c * 128:(c + 1) * 128], yT[:, kc, :], kc == 0, kc == 15,
                         [rb, yT_b[kc]], [pbb], signal=(kc == 15))
                  if c >= 1:
                      MM(psM[:, 128:128 + T], inv1024[:], sq[:, c - 1, :], c == 1, False, [const_b, sq_b[c - 1]], [psM_b], signal=False)
                  if l == nl - 1 and dbg != 8:
                      STT(cell[:, c, :], pb[:, 0:T], modT[:, l, 16 + c:17 + c], xT[:, c, :], ALU.mult, ALU.add,
                          [pbb, par_b, xT_b[c]], [cell_b[c]])
                      ACT(sq[:, c, :], cell[:, c, :], AF.Square, [cell_b[c]], [sq_b[c]])
                  else:
                      STT(xT[:, c, :], pb[:, 0:T], modT[:, l, 16 + c:17 + c], xT[:, c, :], ALU.mult, ALU.add,
                          [pbb, par_b, xT_b[c]], [xT_b[c]])
                      ACT(sq[:, c, :], xT[:, c, :], AF.Square, [xT_b[c]], [sq_b[c]])
              MM(psM[:, 128:128 + T], inv1024[:], sq[:, 7, :], False, True, [const_b, sq_b[7]], [psM_b])
              if l == nl - 1 and sidx + 1 < len(steps):
                  deferred_rel.extend([(sidx, 6), (sidx, 7)])
              else:
                  release(sidx, 6)
                  release(sidx, 7)

              if dbg == 8 and sidx == len(steps) - 1:
                  dump([(cell[:], xT[:], xT_b)])
              if l == nl - 1:
                  ACT(lnr[:], psM[:, 128:128 + T], AF.Ln, [psM_b], [lnr_b], bias=EPS)
                  ACT(rstd[:], lnr[:], AF.Exp, [lnr_b], [rstd_b], scale=-0.5)
                  for c in range(8):
                      STT(cell[:, c, :], cell[:, c, :], pv[:, O_FG + c:O_FG + c + 1], rstd[:], ALU.mult, ALU.mult,
                          [cell_b[c], par_b, rstd_b], [cell_b[c]])
                  fw.dma("sp", "ostore", oT_d.rearrange("(c p) t -> p c t", p=128)[:, :, t0:t0 + T], cell[:],
                         reads=cell_b, writes=[])
                  if dbg == 9 and sidx == 0:
                      raise _Stop()
        except _Stop:
            pass
        nc._fw_counts = dict(fw.cnt)
        for key_ in list(fw.cnt.keys()):
            if key_ not in fw.eng and fw.cnt[key_] > 0:
                fw._wait("sp", key_, fw.cnt[key_])
    return nc


_NC_CACHE = {}


def _feat(v):
    return np.ascontiguousarray(np.asarray(v, np.float32).reshape(8, 128).T)


def kernel(x, c, norm_g, w_ada, b_ada, w_in, rg_conv_w, rg_conv_b, rg_w_a, rg_b_a, rg_w_x, rg_b_x, rg_lambda,
           ml_conv_w, ml_conv_b, ml_w_q, ml_w_k, ml_w_v, ml_w_if, ml_b_if, ml_norm_g, w_out, final_g):
    f32 = lambda a: np.ascontiguousarray(np.asarray(a, np.float32))
    x = f32(x)
    B = x.shape[0]
    n_cores = 8
    if "nc" not in _NC_CACHE:
        _NC_CACHE["nc"] = build_nc()
    nc = _NC_CACHE["nc"]
    tri = np.triu(np.ones((128, 128), np.float32))
    in_maps = []
    for core in range(n_cores):
        b = core % B
        cols = []
        for l in range(NL):
            cols.append(_feat(norm_g[l]))
            for k in range(4):
                cols.append(_feat(rg_conv_w[l][k]))
            cols.append(_feat(rg_conv_b[l]))
            cols.append(_feat(rg_b_a[l]))
            cols.append(_feat(rg_b_x[l]))
            cols.append(_feat(rg_lambda[l]))
            for k in range(4):
                cols.append(_feat(ml_conv_w[l][k]))
            cols.append(_feat(ml_conv_b[l]))
            cols.append(_feat(ml_norm_g[l]))
            ba = np.asarray(b_ada[l], np.float32)
            for j in range(3):
                cols.append(_feat(ba[j * D:(j + 1) * D]))
        cols.append(_feat(final_g))
        cols.append(_feat(np.asarray(c, np.float32)[b]))
        pvh = np.ascontiguousarray(np.concatenate(cols, axis=1))
        assert pvh.shape == (128, NPV)
        in_maps.append({
            "xT": np.ascontiguousarray(x[b].T),
            "pv": pvh,
            "bif": f32(ml_b_if).reshape(1, NL * 8),
            "w_ada": f32(w_ada), "w_in": f32(w_in), "rg_w_a": f32(rg_w_a), "rg_w_x": f32(rg_w_x),
            "ml_w_q": f32(ml_w_q), "ml_w_k": f32(ml_w_k), "ml_w_v": f32(ml_w_v), "ml_w_if": f32(ml_w_if),
            "w_out": f32(w_out), "tri": tri,
        })
    res = run_bass_kernel_spmd(nc, in_maps, core_ids=list(range(n_cores)))
    out = np.empty((B, S, D), np.float32)
    for b in range(B):
        out[b] = res.results[b]["oT"].T
    return out
```

```python
import contextlib
import math
import numpy as np
import concourse.bass as bass
import concourse.mybir as mybir
from concourse.bass_utils import run_bass_kernel_spmd

F32 = mybir.dt.float32
BF16 = mybir.dt.bfloat16
ALU = mybir.AluOpType
AF = mybir.ActivationFunctionType

D = 1024
S = 4096
NL = 2
T = 256
NT = S // T
NCH = T // 128
EPS = 1e-6
LP = 144
NPV = NL * LP + 16
SLOT = 8448
NSLOT = 3


class Buf:
    __slots__ = ("name", "lw", "rd", "excl", "norec")

    def __init__(self, name, excl=False):
        self.name = name
        self.lw = None
        self.rd = {}
        self.excl = excl
        self.norec = False


class FW:
    def __init__(self, nc):
        self.nc = nc
        self.eng = {"pe": nc.tensor, "act": nc.scalar, "dve": nc.vector, "pool": nc.gpsimd, "sp": nc.sync}
        self.sems = {}
        self.cnt = {}
        self.seen = {e: {} for e in self.eng}
        self.when = {}
        self.gidx = 0
        for e in self.eng:
            self.sems[e] = nc.alloc_semaphore("s_" + e)
            self.cnt[e] = 0

    def _wait(self, e, key, val):
        if self.seen[e].get(key, 0) >= val:
            return
        self.eng[e].wait_ge(self.sems[key], val)
        self.seen[e][key] = val

    def deps(self, e, reads, writes, emit=True):
        pend = {}
        seen = self.seen[e]

        def need(k, v):
            if seen.get(k, 0) >= v:
                return
            if pend.get(k, 0) < v:
                pend[k] = v

        for b in reads:
            if b.lw is not None:
                k, v = b.lw
                if not (k == e and e == "pe"):
                    need(k, v)
        for b in writes:
            if b.lw is not None:
                k, v = b.lw
                if k != e:
                    need(k, v)
            for k, v in b.rd.items():
                if k != e:
                    need(k, v)
        waits = sorted(pend.items(), key=lambda kv: self.when.get(kv, -1))
        if emit:
            for k, v in waits:
                self._wait(e, k, v)
            return []
        for k, v in waits:
            seen[k] = v
        return waits

    def op(self, e, ins_fn, reads=(), writes=(), signal=True):
        xr = [b for b in reads if b.excl]
        if xr:
            reads = [b for b in reads if not b.excl]
            writes = list(writes) + xr
        if e in ("act", "dve", "pool", "pe"):
            waits = self.deps(e, reads, writes, emit=False)
            for k, v in waits[:-1]:
                self.eng[e].wait_ge(self.sems[k], v)
            ins = ins_fn(self.eng[e])
            if waits:
                k, v = waits[-1]
                ins._wait_ge(self.sems[k], v)
        else:
            self.deps(e, reads, writes)
            ins = ins_fn(self.eng[e])
        n = self.cnt[e] + 1
        self.gidx += 1
        self.when[(e, n)] = self.gidx
        if signal:
            ins.then_inc(self.sems[e], 1)
            self.cnt[e] = n
        for b in writes:
            b.lw = (e, n)
            b.rd = {}
        for b in reads:
            if b.rd.get(e, 0) < n:
                b.rd[e] = n
        return ins

    def dma(self, q, key, out, in_, reads=(), writes=()):
        if key not in self.sems:
            self.sems[key] = self.nc.alloc_semaphore("d_" + key)
            self.cnt[key] = 0
        self.deps(q, reads, writes)
        ins = self.eng[q].dma_start(out=out, in_=in_)
        v = self.cnt[key] + 16
        ins.then_inc(self.sems[key], 16)
        self.cnt[key] = v
        self.gidx += 1
        self.when[(key, v)] = self.gidx
        for b in writes:
            b.lw = (key, v)
            b.rd = {}
        for b in reads:
            if not b.norec:
                b.rd[key] = v
        return ins


class _Stop(Exception):
    pass


def build_nc(nt=NT, nl=NL, dbg=None):
    nc = bass.Bass("TRN2", target_bir_lowering=False)
    dt_in = lambda name, shape: nc.dram_tensor(name, shape, F32, kind="ExternalInput").ap()
    xT_d = dt_in("xT", [D, S])
    pv_d = dt_in("pv", [128, NPV])
    bif_d = dt_in("bif", [1, NL * 8])
    w_ada_d = dt_in("w_ada", [NL, D, 3 * D])
    w_in_d = dt_in("w_in", [NL, D, 5 * D])
    w_a_d = dt_in("rg_w_a", [NL, 8, 128, 128])
    w_x_d = dt_in("rg_w_x", [NL, 8, 128, 128])
    w_q_d = dt_in("ml_w_q", [NL, 4, 256, 256])
    w_k_d = dt_in("ml_w_k", [NL, 4, 256, 256])
    w_v_d = dt_in("ml_w_v", [NL, 4, 256, 256])
    w_if_d = dt_in("ml_w_if", [NL, 3 * D, 8])
    w_out_d = dt_in("w_out", [NL, 2 * D, D])
    tri_d = dt_in("tri", [128, 128])
    oT_d = nc.dram_tensor("oT", [D, S], F32, kind="ExternalOutput").ap()

    fw = FW(nc)
    with contextlib.ExitStack() as es:
        def SB(name, shape, dt):
            return es.enter_context(nc.sbuf_tensor(name, shape, dt))

        def PS(name):
            return es.enter_context(nc.psum_tensor(name, [128, 512], F32))

        ring = [SB("ring%d" % i, [128, SLOT], BF16) for i in range(NSLOT)]
        ring_b = [Buf("ring%d" % i) for i in range(NSLOT)]
        xT = SB("xT_sb", [128, 8, T], F32)
        xT_b = [Buf("xT%d" % c) for c in range(8)]
        sq = SB("sq", [128, 8, T], BF16)
        sq_b = [Buf("sq%d" % c) for c in range(8)]
        hb = SB("hb", [128, 8, T], BF16)
        hb_b = [Buf("hb%d" % c) for c in range(8)]
        rstd = SB("rstd", [128, T], F32)
        rstd_b = Buf("rstd")
        lnr = SB("lnr", [128, T], F32)
        lnr_b = Buf("lnr")
        rgx = SB("rgx", [128, 8, 4 + T], BF16)
        rgx_b = [Buf("rgx%d" % c) for c in range(8)]
        mlx = SB("mlx", [128, 8, 4 + T], BF16)
        mlx_b = [Buf("mlx%d" % c) for c in range(8)]
        yT = SB("yT", [128, 16, T], BF16)
        yT_b = [Buf("yT%d" % c) for c in range(16)]
        mxc = SB("mxc", [128, 8, T], BF16)
        mxc_b = [Buf("mxc%d" % c) for c in range(8)]
        qkv = SB("qkv", [128, 24, T], BF16)
        qkv_b = [Buf("qkv%d" % c) for c in range(24)]
        sgo = SB("sgo", [128, 8, T], F32)
        sgo_b = [Buf("sgo%d" % c) for c in range(8)]
        zsm = SB("zsm", [128, 8, T], F32)
        zsm_b = [Buf("zsm%d" % c) for c in range(8)]
        zsr = SB("zsr", [128, 8, T], F32)
        zsr_b = [Buf("zsr%d" % c) for c in range(8)]
        cell = SB("cell", [128, 8, T], F32)
        cell_b = [Buf("cell%d" % c) for c in range(8)]
        NTMP = 4
        tmp = [[SB("tmp%d_%d" % (s, i), [128, T], F32) for i in range(NTMP)] for s in range(2)]
        tmp_b = [[Buf("tmp%d_%d" % (s, i)) for i in range(NTMP)] for s in range(2)]
        NB = 4
        bX = SB("bX", [128, NB, T], F32)
        bR = SB("bR", [128, NB, T], F32)
        bI = SB("bI", [128, NB, T], F32)
        bA = SB("bA", [128, NB, T], F32)
        bM = SB("bM", [128, NB, T], F32)
        bX_b = [Buf("bX%d" % i) for i in range(NB)]
        bR_b = [Buf("bR%d" % i) for i in range(NB)]
        bI_b = [Buf("bI%d" % i) for i in range(NB)]
        bA_b = [Buf("bA%d" % i) for i in range(NB)]
        bM_b = [Buf("bM%d" % i) for i in range(NB)]
        xcb = SB("xcb", [128, NB, T], BF16)
        xcb_b = [Buf("xcb%d" % i) for i in range(NB)]
        dg = [SB("dg%d" % s, [128, 4, 128], BF16) for s in range(NB)]
        dg_b = [Buf("dg%d" % s) for s in range(NB)]
        eb16 = SB("eb16", [128, 4, 128], F32)
        eb16_b = Buf("eb16")
        lftri = SB("lftri", [128, 4, 128], F32)
        lftri_b = Buf("lftri")
        wT = [SB("wT%d" % s, [128, 128], F32) for s in range(2)]
        wT_b = [Buf("wT%d" % s) for s in range(2)]
        pT = [SB("pT%d" % s, [128, 128], BF16) for s in range(2)]
        pT_b = [Buf("pT%d" % s) for s in range(2)]
        qs = [SB("qs%d" % s, [128, 2, 128], BF16) for s in range(2)]
        qs_b = [Buf("qs%d" % s) for s in range(2)]
        vx = [SB("vx%d" % s, [128, 258], BF16) for s in range(2)]
        vx_b = [Buf("vx%d" % s) for s in range(2)]
        kw = [SB("kw%d" % s, [128, 256], BF16) for s in range(2)]
        kw_b = [Buf("kw%d" % s) for s in range(2)]
        rec = [SB("rec%d" % s, [128, 128], F32) for s in range(2)]
        rec_b = [Buf("rec%d" % s) for s in range(2)]
        nbc = [SB("nbc%d" % s, [128, 2, 128], BF16) for s in range(2)]
        nbc_b = [Buf("nbc%d" % s) for s in range(2)]
        gsb = SB("gsb", [128, NCH, 8], F32)
        gsb_b = Buf("gsb")
        nlf = SB("nlf", [128, NCH, 4], F32)
        nlf_b = Buf("nlf")
        ctm = SB("ctm", [128, NCH, 4], F32)
        ctm_b = Buf("ctm")
        ec = SB("ec", [128, NCH, 4], F32)
        ec_b = Buf("ec")
        wst = SB("wst", [128, NCH, 4], F32)
        wst_b = Buf("wst")
        dec = SB("dec", [128, NCH, 4], F32)
        dec_b = Buf("dec")
        halo_rg = [SB("halo_rg%d" % l, [128, 8, 3], BF16) for l in range(NL)]
        halo_ml = [SB("halo_ml%d" % l, [128, 8, 3], BF16) for l in range(NL)]
        halo_rg_b = [Buf("halo_rg%d" % l) for l in range(NL)]
        halo_ml_b = [Buf("halo_ml%d" % l) for l in range(NL)]
        rgh = [SB("rgh%d" % l, [128, 8], F32) for l in range(NL)]
        rgh_b = [[Buf("rgh%d_%d" % (l, c)) for c in range(8)] for l in range(NL)]
        C32 = [SB("C32_%d" % l, [128, 4, 2, 258], F32) for l in range(NL)]
        Cb = [SB("Cb_%d" % l, [128, 4, 2, 258], BF16) for l in range(NL)]
        C32_b = [[Buf("C32_%d_%d" % (l, h)) for h in range(4)] for l in range(NL)]
        Cb_b = [[Buf("Cb_%d_%d" % (l, h)) for h in range(4)] for l in range(NL)]
        ident = SB("ident", [128, 128], BF16)
        ones_b = SB("ones_b", [128, 128], BF16)
        inv1024 = SB("inv1024", [128, 128], BF16)
        inv256 = SB("inv256", [128, 128], BF16)
        tri32 = SB("tri_sb", [128, 128], F32)
        ones32 = SB("ones32", [128, 128], F32)
        pv = SB("pv_sb", [128, NPV], F32)
        dv = SB("dv", [128, NL, 64], F32)
        modT = SB("modT", [128, NL, 24], F32)
        bifb = SB("bif_sb", [128, NL * 8], F32)
        cact = SB("cact", [128, 8], F32)
        ctmp = SB("ctmp", [128, 8], F32)
        const_b = Buf("consts")
        par_b = Buf("params")
        psg = [PS("psg%d" % i) for i in range(3)]
        psg_b = [Buf("psg%d" % i, True) for i in range(3)]
        psS = PS("psS")
        psS_b = Buf("psS_S", True)
        psND = PS("psND")
        psND_b = Buf("psND", True)
        psVK = PS("psVK")
        psV_b = Buf("psV", True)
        psC = PS("psC")
        psC_b = Buf("psC", True)
        psM = PS("psM")
        psM_b = Buf("psM", True)

        nc._sbuf_left = nc.sbuf_bytes_remaining
        gctr = [0]

        def gen_bank():
            i = gctr[0] % 3
            gctr[0] += 1
            return psg[i], psg_b[i]

        def ACT(out, in_, func, reads, writes, bias=None, scale=None):
            kw_ = {}
            if bias is not None:
                kw_["bias"] = bias
            if scale is not None:
                kw_["scale"] = scale
            return fw.op("act", lambda e: e.activation(out=out, in_=in_, func=func, **kw_), reads, writes)

        def TT(eng, out, in0, in1, op, reads, writes):
            return fw.op(eng, lambda e: e.tensor_tensor(out=out, in0=in0, in1=in1, op=op), reads, writes)

        def TS(eng, out, in0, s1, s2, op0, op1, reads, writes):
            if s2 is None:
                return fw.op(eng, lambda e: e.tensor_scalar(out=out, in0=in0, scalar1=s1, scalar2=None, op0=op0), reads, writes)
            return fw.op(eng, lambda e: e.tensor_scalar(out=out, in0=in0, scalar1=s1, scalar2=s2, op0=op0, op1=op1), reads, writes)

        def STT(out, in0, scalar, in1, op0, op1, reads, writes):
            return fw.op("dve", lambda e: e.scalar_tensor_tensor(out=out, in0=in0, scalar=scalar, in1=in1, op0=op0, op1=op1), reads, writes)

        def CP(eng, out, in_, reads, writes):
            if eng == "act":
                return fw.op("act", lambda e: e.copy(out=out, in_=in_), reads, writes)
            return fw.op(eng, lambda e: e.tensor_copy(out=out, in_=in_), reads, writes)

        def RECIP(out, in_, reads, writes):
            return fw.op("dve", lambda e: e.reciprocal(out=out, in_=in_), reads, writes)

        def MM(out, lhsT, rhs, start, stop, reads, writes, signal=True):
            return fw.op("pe", lambda e: e.matmul(out, lhsT=lhsT, rhs=rhs, start=start, stop=stop), reads, writes, signal=signal)

        def MSET(eng, ap, val, writes):
            return fw.op(eng, lambda e: e.memset(ap, val), (), writes)

        MSET("pool", ones_b[:], 1.0, [const_b])
        MSET("pool", inv1024[:], 1.0 / 1024.0, [const_b])
        MSET("pool", inv256[:], 1.0 / 256.0, [const_b])
        MSET("pool", ones32[:], 1.0, [const_b])
        for s in range(2):
            MSET("pool", vx[s][:], 1.0, [vx_b[s]])
        for l in range(NL):
            MSET("pool", halo_rg[l][:], 0.0, [halo_rg_b[l]])
            MSET("pool", halo_ml[l][:], 0.0, [halo_ml_b[l]])
            MSET("pool", rgh[l][:], 0.0, rgh_b[l])
            MSET("pool", C32[l][:], 0.0, C32_b[l])
            MSET("pool", Cb[l][:], 0.0, Cb_b[l])
        fw.dma("sp", "c_tri", tri32[:], tri_d, writes=[const_b])
        fw.dma("sp", "c_pv", pv[:], pv_d, writes=[par_b])
        fw.dma("sp", "c_bif", bifb[:], bif_d.partition_broadcast(128), writes=[par_b])
        CP("dve", ident[:, 0:1], tri32[:, 0:1], [const_b], [const_b])
        TT("dve", ident[:, 1:128], tri32[:, 1:128], tri32[:, 0:127], ALU.subtract, [const_b], [const_b])

        def pcol(l, off, n=8):
            b = l * LP + off
            return pv[:, b:b + n]

        O_NG, O_RCW, O_RCB, O_BA, O_BX, O_LAM, O_MCW, O_MCB, O_MNG, O_BADA = 0, 8, 40, 48, 56, 64, 72, 104, 112, 120
        O_FG = NL * LP
        O_CT = NL * LP + 8
        for l in range(NL):
            TS("dve", dv[:, l, 0:8], pcol(l, O_BA), -1.0, None, ALU.mult, None, [par_b], [par_b])
            TS("dve", dv[:, l, 8:16], pcol(l, O_BX), -1.0, None, ALU.mult, None, [par_b], [par_b])
            TS("dve", dv[:, l, 16:24], pcol(l, O_MCB), -1.0, None, ALU.mult, None, [par_b], [par_b])
            ACT(dv[:, l, 48:56], pcol(l, O_LAM), AF.Exp, [par_b], [par_b], scale=-1.0)
            ACT(dv[:, l, 48:56], dv[:, l, 48:56], AF.Ln, [par_b], [par_b], bias=1.0)
            TS("dve", dv[:, l, 24:32], dv[:, l, 48:56], -8.0, None, ALU.mult, None, [par_b], [par_b])
            TS("dve", dv[:, l, 32:40], dv[:, l, 48:56], -16.0, None, ALU.mult, None, [par_b], [par_b])
        ACT(ctmp[:], pv[:, O_CT:O_CT + 8], AF.Exp, [par_b], [par_b], scale=-1.0)
        TS("dve", ctmp[:], ctmp[:], 1.0, None, ALU.add, None, [par_b], [par_b])
        RECIP(ctmp[:], ctmp[:], [par_b], [par_b])
        TT("dve", cact[:], pv[:, O_CT:O_CT + 8], ctmp[:], ALU.mult, [par_b], [par_b])
        slot_ctr = [0]

        def next_slot():
            i = slot_ctr[0] % NSLOT
            slot_ctr[0] += 1
            return i

        for l in range(NL):
            for pi in range(6):
                si = next_slot()
                r32 = ring[si][:].bitcast(F32)
                dst = r32[:, 0:4096].rearrange("p (kc n) -> p kc n", kc=8)
                src = w_ada_d[l].rearrange("(kc p) n -> p kc n", p=128)[:, :, pi * 512:(pi + 1) * 512]
                fw.dma("sp", "ada%d" % si, dst, src, writes=[ring_b[si]])
                for mm_ in range(4):
                    m = pi * 4 + mm_
                    for kc in range(8):
                        MM(psM[:, m:m + 1], dst[:, kc, mm_ * 128:(mm_ + 1) * 128], cact[:, kc:kc + 1], kc == 0, kc == 7,
                           [ring_b[si], par_b], [psM_b], signal=(kc == 7))
            TT("dve", modT[:, l, :], psM[:, 0:24], pcol(l, O_BADA, 24), ALU.add, [psM_b, par_b], [par_b])
            STT(dv[:, l, 40:48], modT[:, l, 8:16], 1.0, pcol(l, O_NG), ALU.add, ALU.mult, [par_b], [par_b])

        def load_piece(si, l, piece):
            key = "ring%d" % si
            r = ring[si]
            if piece < 5:
                dst = r[:, 0:8192].rearrange("p (kc n) -> p kc n", kc=8)
                src = w_in_d[l].rearrange("(kc p) n -> p kc n", p=128)[:, :, piece * 1024:(piece + 1) * 1024]
                fw.dma("pool", key, dst, src, reads=[xl_b], writes=[ring_b[si]])
            elif piece == 5:
                fw.dma("pool", key, r[:, 0:1024].rearrange("p (h e) -> p h e", h=8),
                       w_a_d[l].rearrange("h d e -> d h e"), reads=[xl_b], writes=[ring_b[si]])
                fw.dma("pool", key, r[:, 1024:2048].rearrange("p (h e) -> p h e", h=8),
                       w_x_d[l].rearrange("h d e -> d h e"), writes=[])
                for wi, wd in enumerate((w_q_d, w_k_d, w_v_d)):
                    for h in range(4):
                        o = 2048 + wi * 2048 + h * 512
                        fw.dma("pool", key, r[:, o:o + 512].rearrange("p (dc e) -> p dc e", dc=2),
                               wd[l, h].rearrange("(dc p) e -> p dc e", p=128), writes=[])
                fw.dma("pool", key, r[:, 8192:8384].rearrange("p (kc g) -> p kc g", kc=24),
                       w_if_d[l].rearrange("(kc p) g -> p kc g", p=128), writes=[])
                ring_b[si].lw = (key, fw.cnt[key])
            else:
                hf = piece - 6
                dst = r[:, 0:8192].rearrange("p (kc n) -> p kc n", kc=8)
                src = w_out_d[l].rearrange("(kc p) n -> p kc n", p=128)[:, hf * 8:(hf + 1) * 8, :]
                fw.dma("pool", key, dst, src, reads=[xl_b], writes=[ring_b[si]])

        steps = [(ti, l) for ti in range(nt) for l in range(nl)]
        ORDER = [0, 1, 2, 5, 3, 4, 6, 7]
        piece_seq = [(si_, p) for si_ in range(len(steps)) for p in ORDER]
        slot_of = {}
        free_slots = list(range(NSLOT))
        next_load = [0]

        xl_b = Buf("xload_done")
        xl_b.norec = True
        deferred_rel = []

        def pump():
            while next_load[0] < len(piece_seq) and free_slots:
                key_ = piece_seq[next_load[0]]
                si = free_slots.pop(0)
                load_piece(si, steps[key_[0]][1], key_[1])
                slot_of[key_] = si
                next_load[0] += 1

        def get_piece(sidx, p):
            pump()
            assert (sidx, p) in slot_of, ("ring deadlock", sidx, p)
            si = slot_of[(sidx, p)]
            return ring[si], ring_b[si]

        def release(sidx, p):
            si = slot_of.pop((sidx, p))
            free_slots.append(si)
            pump()

        def dump(srcs):
            MSET("dve", cell[:], 0.0, cell_b)
            for (dst, ap_, bufs) in srcs:
                CP("dve", dst, ap_, bufs, cell_b)
            fw.dma("sp", "ostore", oT_d.rearrange("(c p) t -> p c t", p=128)[:, :, 0:T], cell[:], reads=cell_b, writes=[])
            raise _Stop()

        try:
          if dbg == 0:
            dump([(cell[:, 0, 0:64], dv[:, 0, :], [par_b]), (cell[:, 1, 0:24], modT[:, 0, :], [par_b]),
                  (cell[:, 2, 0:128], ident[:], [const_b]), (cell[:, 3, 0:16], bifb[:], [par_b])])
          out_sem_keys = []
          for sidx, (ti, l) in enumerate(steps):
              t0 = ti * T
              if l == 0:
                  xsrc = xT_d.rearrange("(c p) t -> p c t", p=128)
                  fw.dma("sp", "xloadA", xT[:, 0:4, :], xsrc[:, 0:4, t0:t0 + T], reads=[], writes=xT_b[0:4] + [xl_b])
                  fw.dma("sp", "xloadB", xT[:, 4:8, :], xsrc[:, 4:8, t0:t0 + T], reads=[], writes=xT_b[4:8] + [xl_b])
              for (rs_, rp_) in deferred_rel:
                  release(rs_, rp_)
              del deferred_rel[:]
              if l == 0:
                  for kc in range(8):
                      ACT(sq[:, kc, :], xT[:, kc, :], AF.Square, [xT_b[kc]], [sq_b[kc]])
                      MM(psM[:, 128:128 + T], inv1024[:], sq[:, kc, :], kc == 0, kc == 7, [const_b, sq_b[kc]], [psM_b], signal=(kc == 7))
              ACT(lnr[:], psM[:, 128:128 + T], AF.Ln, [psM_b], [lnr_b], bias=EPS)
              ACT(rstd[:], lnr[:], AF.Exp, [lnr_b], [rstd_b], scale=-0.5)
              for c in range(8):
                  s = c % 2
                  t_ = tmp[s][0]
                  STT(t_[:], xT[:, c, :], dv[:, l, 40 + c:41 + c], rstd[:], ALU.mult, ALU.mult,
                      [xT_b[c], rstd_b, par_b], [tmp_b[s][0]])
                  ACT(hb[:, c, :], t_[:], AF.Identity, [tmp_b[s][0], par_b], [hb_b[c]], bias=modT[:, l, c:c + 1])

              if dbg == 1 and sidx == len(steps) - 1:
                  dump([(cell[:], hb[:], hb_b)])
              def win_chunk(piece, c):
                  r, rb = get_piece(sidx, piece)
                  w3 = r[:, 0:8192].rearrange("p (kc n) -> p kc n", kc=8)
                  pb_, pbb_ = gen_bank()
                  for kc in range(8):
                      MM(pb_[:, 0:T], w3[:, kc, c * 128:(c + 1) * 128], hb[:, kc, :], kc == 0, kc == 7,
                         [rb, hb_b[kc]], [pbb_], signal=(kc == 7))
                  return pb_, pbb_

              def sigmoid_from(out, src, reads, wbuf, nbias=None):
                  if nbias is None:
                      ACT(out, src, AF.Exp, reads, [wbuf], scale=-1.0)
                  else:
                      ACT(out, src, AF.Exp, reads + [par_b], [wbuf], scale=-1.0, bias=nbias)
                  ACT(out, out, AF.Ln, [wbuf], [wbuf], bias=1.0)
                  ACT(out, out, AF.Exp, [wbuf], [wbuf], scale=-1.0)

              CP("pool", rgx[:, :, 1:4], halo_rg[l][:], [halo_rg_b[l]], rgx_b)
              CP("pool", mlx[:, :, 1:4], halo_ml[l][:], [halo_ml_b[l]], mlx_b)

              smallw = [None]

              def get_small():
                  if smallw[0] is None:
                      smallw[0] = get_piece(sidx, 5)
                  return smallw[0]

              for c in range(8):
                  pb, pbb = win_chunk(0, c)
                  CP("act", rgx[:, c, 4:4 + T], pb[:, 0:T], [pbb], [rgx_b[c]])
              CP("pool", halo_rg[l][:], rgx[:, :, T + 1:T + 4], rgx_b, [halo_rg_b[l]])
              release(sidx, 0)
              if dbg == 2 and sidx == len(steps) - 1:
                  dump([(cell[:], rgx[:, :, 4:4 + T], rgx_b)])
              for c in range(8):
                  pb, pbb = win_chunk(1, c)
                  s = c % 2
                  sz = tmp[s][1]
                  sigmoid_from(sz[:], pb[:, 0:T], [pbb], tmp_b[s][1])
                  TT("dve", zsr[:, c, :], pb[:, 0:T], sz[:], ALU.mult, [pbb, tmp_b[s][1]], [zsr_b[c]])
              release(sidx, 1)
              for c in range(8):
                  pb, pbb = win_chunk(2, c)
                  CP("act", mlx[:, c, 4:4 + T], pb[:, 0:T], [pbb], [mlx_b[c]])
              CP("pool", halo_ml[l][:], mlx[:, :, T + 1:T + 4], mlx_b, [halo_ml_b[l]])
              release(sidx, 2)

              sw, swb = get_small()
              w_a3 = sw[:, 0:1024].rearrange("p (h e) -> p h e", h=8)
              w_x3 = sw[:, 1024:2048].rearrange("p (h e) -> p h e", h=8)
              w_q4 = sw[:, 2048:4096].rearrange("p (h dc e) -> p h dc e", h=4, dc=2)
              w_k4 = sw[:, 4096:6144].rearrange("p (h dc e) -> p h dc e", h=4, dc=2)
              w_v4 = sw[:, 6144:8192].rearrange("p (h dc e) -> p h dc e", h=4, dc=2)
              w_if3 = sw[:, 8192:8384].rearrange("p (kc g) -> p kc g", kc=24)

              def diag_build(ci, c, wcol_off):
                  for k in range(4):
                      col = l * LP + wcol_off + k * 8 + c
                      if k < 2:
                          ACT(dg[ci][:, k, :], ident[:], AF.Identity, [const_b, par_b], [dg_b[ci]], scale=pv[:, col:col + 1])
                      else:
                          TS("dve", dg[ci][:, k, :], ident[:], pv[:, col:col + 1], None, ALU.mult, None,
                             [const_b, par_b], [dg_b[ci]])

              def conv_mm(ci, c, src, src_b):
                  pb_, pbb_ = gen_bank()
                  for k in range(4):
                      MM(pb_[:, 0:T], dg[ci][:, k, :], src[:, c, 1 + k:1 + k + T], k == 0, k == 3, [dg_b[ci], src_b[c]], [pbb_],
                         signal=(k == 3))
                  return pb_, pbb_

              for bt in range(2):
                  cs = [bt * NB + ci for ci in range(NB)]
                  for ci, c in enumerate(cs):
                      diag_build(ci, c, O_MCW)
                  for ci, c in enumerate(cs):
                      pb, pbb = conv_mm(ci, c, mlx, mlx_b)
                      TS("dve", bR[:, ci, :], pb[:, 0:T], pcol(l, O_MCB)[:, c:c + 1], None, ALU.add, None, [pbb, par_b], [bR_b[ci]])
                  for hf in range(2):
                      sl = slice(2 * hf, 2 * hf + 2)
                      ACT(bX[:, sl, :], bR[:, sl, :], AF.Exp, bR_b[sl], bX_b[sl], scale=-1.0)
                  for hf in range(2):
                      sl = slice(2 * hf, 2 * hf + 2)
                      ACT(bX[:, sl, :], bX[:, sl, :], AF.Ln, bX_b[sl], bX_b[sl], bias=1.0)
                  for hf in range(2):
                      sl = slice(2 * hf, 2 * hf + 2)
                      ACT(bX[:, sl, :], bX[:, sl, :], AF.Exp, bX_b[sl], bX_b[sl], scale=-1.0)
                  for hf in range(2):
                      sl = slice(2 * hf, 2 * hf + 2)
                      c0_ = bt * NB + 2 * hf
                      TT("dve", mxc[:, c0_:c0_ + 2, :], bR[:, sl, :], bX[:, sl, :], ALU.mult, bR_b[sl] + bX_b[sl], mxc_b[c0_:c0_ + 2])
              if dbg == 4 and sidx == len(steps) - 1:
                  dump([(cell[:], mxc[:], mxc_b)])
              for wi, (w4, src, src_b, off) in enumerate(((w_q4, mxc, mxc_b, 0), (w_k4, mxc, mxc_b, 0), (w_v4, mlx, mlx_b, 4))):
                  for h in range(4):
                      for ecx in range(2):
                          pb, pbb = gen_bank()
                          for dc in range(2):
                              MM(pb[:, 0:T], w4[:, h, dc, ecx * 128:(ecx + 1) * 128], src[:, 2 * h + dc, off:off + T],
                                 dc == 0, dc == 1, [swb, src_b[2 * h + dc]], [pbb], signal=(dc == 1))
                          oc = wi * 8 + 2 * h + ecx
                          eng = "act" if (oc % 2 == 0) else "dve"
                          CP(eng, qkv[:, oc, :], pb[:, 0:T], [pbb], [qkv_b[oc]])
              for j in range(NCH):
                  for kc in range(24):
                      MM(psM[:, 32 + j * 8:32 + (j + 1) * 8], qkv[:, kc, j * 128:(j + 1) * 128], w_if3[:, kc, :], kc == 0, kc == 23,
                         [qkv_b[kc], swb], [psM_b], signal=(kc == 23))
              g_ps = psM[:, 32:32 + NCH * 8].rearrange("p (j g) -> p j g", j=NCH)
              TT("dve", gsb[:], g_ps, bifb[:, l * 8:(l + 1) * 8].unsqueeze(1).broadcast_to([128, NCH, 8]), ALU.add,
                 [psM_b, par_b], [gsb_b])
              if dbg == 5 and sidx == len(steps) - 1:
                  dump([(cell[:, 0, 0:NCH * 8], gsb[:].rearrange("p j g -> p (j g)"), [gsb_b])])
              ACT(nlf[:], gsb[:, :, 4:8], AF.Exp, [gsb_b], [nlf_b], scale=-1.0)
              ACT(nlf[:], nlf[:], AF.Ln, [nlf_b], [nlf_b], bias=1.0)
              nlf2 = nlf[:].rearrange("p j h -> p (j h)")
              MM(psM[:, 64:64 + NCH * 4], tri32[:], nlf2, True, True, [const_b, nlf_b], [psM_b])
              MM(psM[:, 96:96 + NCH * 4], ones32[:], nlf2, True, True, [const_b, nlf_b], [psM_b])
              nb_ps = psM[:, 64:64 + NCH * 4].rearrange("p (j h) -> p j h", j=NCH)
              nbl_ps = psM[:, 96:96 + NCH * 4].rearrange("p (j h) -> p j h", j=NCH)
              TT("dve", ctm[:], gsb[:, :, 0:4], nb_ps, ALU.add, [gsb_b, psM_b], [ctm_b])
              ACT(ec[:], ctm[:], AF.Exp, [ctm_b], [ec_b])
              TT("dve", wst[:], ctm[:], nbl_ps, ALU.subtract, [ctm_b, psM_b], [wst_b])
              ACT(wst[:], wst[:], AF.Exp, [wst_b], [wst_b])
              ACT(dec[:], nbl_ps, AF.Exp, [psM_b], [dec_b], scale=-1.0)

              kbank = {}

              def ml_prep(j):
                  for h in range(4):
                      TS("pool", lftri[:, h, :], tri32[:], nlf[:, j, h:h + 1], -1.0, ALU.mult, ALU.mult,
                         [const_b, nlf_b], [lftri_b])
                  pbb_, pbbb_ = gen_bank()
                  MM(pbb_[:, 0:512], ones32[:], lftri[:].rearrange("p h t -> p (h t)"), True, True, [const_b, lftri_b], [pbbb_])
                  ACT(eb16[:].rearrange("p h t -> p (h t)"), pbb_[:, 0:512], AF.Exp, [pbbb_], [eb16_b], bias=-math.log(16.0))

              def ml_A1(idx):
                  j, h = idx // 4, idx % 4
                  ts_ = slice(j * 128, (j + 1) * 128)
                  if h == 0:
                      ml_prep(j)
                  q0 = 2 * h
                  k0 = 8 + 2 * h
                  for dc in range(2):
                      MM(psS[:, 0:128], qkv[:, k0 + dc, ts_], qkv[:, q0 + dc, ts_], dc == 0, dc == 1,
                         [qkv_b[k0 + dc], qkv_b[q0 + dc]], [psS_b], signal=(dc == 1))
                  for dc in range(2):
                      MM(psVK[:, 0:256], mlx[:, 2 * h + dc, 4 + j * 128:4 + (j + 1) * 128], w_v4[:, h, dc, :], dc == 0, dc == 1,
                         [mlx_b[2 * h + dc], swb], [psV_b], signal=(dc == 1))
                  for dc in range(2):
                      MM(psVK[:, 256:512], mxc[:, 2 * h + dc, ts_], w_k4[:, h, dc, :], dc == 0, dc == 1,
                         [mxc_b[2 * h + dc], swb], [psV_b], signal=(dc == 1))

              def ml_A2(idx):
                  j, h = idx // 4, idx % 4
                  ts_ = slice(j * 128, (j + 1) * 128)
                  s = idx % 2
                  q0 = 2 * h
                  STT(wT[s][:], eb16[:, h, :], ec[:, j, h:h + 1], tri32[:], ALU.mult, ALU.mult,
                      [eb16_b, ec_b, const_b], [wT_b[s]])
                  CP("act", vx[s][:, 0:256], psVK[:, 0:256], [psV_b], [vx_b[s]])
                  TT("pool", qs[s][:], qkv[:, q0:q0 + 2, ts_], eb16[:, h, :].unsqueeze(1).broadcast_to([128, 2, 128]),
                     ALU.mult, [qkv_b[q0], qkv_b[q0 + 1], eb16_b], [qs_b[s]])
                  TS("dve", kw[s][:], psVK[:, 256:512], wst[:, j, h:h + 1], None, ALU.mult, None, [psV_b, wst_b], [kw_b[s]])
                  TT("dve", pT[s][:], psS[:, 0:128], wT[s][:], ALU.mult, [psS_b, wT_b[s]], [pT_b[s]])
                  CP("pool", nbc[s][:], Cb[l][:, h, :, 256:257].broadcast_to([128, 2, 128]), [Cb_b[l][h]], [nbc_b[s]])

              def ml_B1(idx):
                  j, h = idx // 4, idx % 4
                  s = idx % 2
                  for vc in range(2):
                      o = vc * 128
                      MM(psND[:, o:o + 128], vx[s][:, vc * 128:(vc + 1) * 128], pT[s][:], True, False,
                         [vx_b[s], pT_b[s]], [psND_b], signal=False)
                      for dc in range(2):
                          MM(psND[:, o:o + 128], Cb[l][:, h, dc, vc * 128:(vc + 1) * 128], qs[s][:, dc, :], False, dc == 1,
                             [Cb_b[l][h], qs_b[s]], [psND_b], signal=False)
                  MM(psND[:, 256:384], ones_b[:], pT[s][:], True, False, [const_b, pT_b[s]], [psND_b], signal=False)
                  for dc in range(2):
                      MM(psND[:, 256:384], nbc[s][:, dc, :], qs[s][:, dc, :], False, dc == 1, [nbc_b[s], qs_b[s]], [psND_b],
                         signal=False)
                  for dc in range(2):
                      MM(psND[:, 384 + 2 * dc:386 + 2 * dc], kw[s][:, dc * 128:(dc + 1) * 128], vx[s][:, 256:258], True, True,
                         [kw_b[s], vx_b[s]], [psND_b], signal=(dc == 1))
                  for dc in range(2):
                      MM(psC[:, dc * 256:(dc + 1) * 256], kw[s][:, dc * 128:(dc + 1) * 128], vx[s][:, 0:256], True, True,
                         [kw_b[s], vx_b[s]], [psC_b], signal=(dc == 1))

              def ml_B2(idx):
                  j, h = idx // 4, idx % 4
                  ts_ = slice(j * 128, (j + 1) * 128)
                  s = idx % 2
                  ACT(rec[s][:], psND[:, 256:384], AF.Abs, [psND_b], [rec_b[s]])
                  STT(C32[l][:, h, :, 0:256], C32[l][:, h, :, 0:256], dec[:, j, h:h + 1],
                      psC[:, 0:512].rearrange("p (dc e) -> p dc e", dc=2), ALU.mult, ALU.add,
                      [C32_b[l][h], dec_b, psC_b], [C32_b[l][h]])
                  TS("dve", rec[s][:], rec[s][:], 1.0, None, ALU.max, None, [rec_b[s]], [rec_b[s]])
                  STT(C32[l][:, h, :, 256:258], C32[l][:, h, :, 256:258], dec[:, j, h:h + 1],
                      psND[:, 384:388].rearrange("p (dc e) -> p dc e", dc=2), ALU.mult, ALU.add,
                      [C32_b[l][h], dec_b, psND_b], [C32_b[l][h]])
                  RECIP(rec[s][:], rec[s][:], [rec_b[s]], [rec_b[s]])
                  CP("dve", Cb[l][:, h, :, :], C32[l][:, h, :, :], [C32_b[l][h]], [Cb_b[l][h]])
                  TT("dve", cell[:, 2 * h:2 * h + 2, ts_], psND[:, 0:256].rearrange("p (v t) -> p v t", v=2),
                     rec[s][:].unsqueeze(1).broadcast_to([128, 2, 128]), ALU.mult, [psND_b, rec_b[s]],
                     [cell_b[2 * h], cell_b[2 * h + 1]])

              NIT = NCH * 4
              ml_sched = []
              for i in range(NIT + 1):
                  if i < NIT:
                      ml_sched.append((ml_A1, i))
                  if i >= 1:
                      ml_sched.append((ml_B1, i - 1))
                  if i < NIT:
                      ml_sched.append((ml_A2, i))
                  if i >= 1:
                      ml_sched.append((ml_B2, i - 1))
              ml_pos = [0]

              def ml_emit(n):
                  for _ in range(n):
                      if ml_pos[0] < len(ml_sched):
                          f_, a_ = ml_sched[ml_pos[0]]
                          f_(a_)
                          ml_pos[0] += 1

              fill_pos = [0]

              def fill_pair():
                  k = fill_pos[0]
                  if k >= 8:
                      return
                  fill_pos[0] += 1
                  piece = 3 if k < 4 else 4
                  c0 = (k % 4) * 2
                  banks = []
                  for c in (c0, c0 + 1):
                      banks.append(win_chunk(piece, c))
                  outs = []
                  for ii, c in enumerate((c0, c0 + 1)):
                      pb, pbb = banks[ii]
                      if piece == 3:
                          o_, ob_ = sgo[:, c, :], sgo_b[c]
                      else:
                          o_, ob_ = tmp[ii][1][:], tmp_b[ii][1]
                      outs.append((o_, ob_))
                      ACT(o_, pb[:, 0:T], AF.Exp, [pbb], [ob_], scale=-1.0)
                  for (o_, ob_) in outs:
                      ACT(o_, o_, AF.Ln, [ob_], [ob_], bias=1.0)
                  for (o_, ob_) in outs:
                      ACT(o_, o_, AF.Exp, [ob_], [ob_], scale=-1.0)
                  if piece == 4:
                      for ii, c in enumerate((c0, c0 + 1)):
                          pb, pbb = banks[ii]
                          TT("dve", zsm[:, c, :], pb[:, 0:T], outs[ii][0], ALU.mult, [pbb, outs[ii][1]], [zsm_b[c]])
                  if k == 3:
                      release(sidx, 3)
                  if k == 7:
                      release(sidx, 4)

              per_gap = 1
              for bt in range(2):
                  cs = [bt * NB + ci for ci in range(NB)]
                  for ci, c in enumerate(cs):
                      diag_build(ci, c, O_RCW)
                  ml_emit(per_gap)
                  fill_pair()
                  for ci, c in enumerate(cs):
                      pb, pbb = conv_mm(ci, c, rgx, rgx_b)
                      ACT(bX[:, ci, :], pb[:, 0:T], AF.Identity, [pbb, par_b], [bX_b[ci]], bias=pcol(l, O_RCB)[:, c:c + 1])
                      ml_emit(1)
                  ml_emit(per_gap)
                  fill_pair()
                  for ci, c in enumerate(cs):
                      CP("dve", xcb[:, ci, :], bX[:, ci, :], [bX_b[ci]], [xcb_b[ci]])
                  gb = []
                  for ci, c in enumerate(cs):
                      pr, prb = gen_bank()
                      MM(pr[:, 0:T], w_a3[:, c, :], xcb[:, ci, :], True, True, [swb, xcb_b[ci]], [prb])
                      ACT(bR[:, ci, :], pr[:, 0:T], AF.Exp, [prb, par_b], [bR_b[ci]], scale=-1.0, bias=dv[:, l, c:c + 1])
                      pi_, pib = gen_bank()
                      MM(pi_[:, 0:T], w_x3[:, c, :], xcb[:, ci, :], True, True, [swb, xcb_b[ci]], [pib])
                      ACT(bI[:, ci, :], pi_[:, 0:T], AF.Exp, [pib, par_b], [bI_b[ci]], scale=-1.0, bias=dv[:, l, 8 + c:9 + c])
                      ml_emit(1)
                  ml_emit(per_gap)
                  fill_pair()
                  for hf in range(2):
                      sl = slice(2 * hf, 2 * hf + 2)
                      ACT(bR[:, sl, :], bR[:, sl, :], AF.Ln, bR_b[sl], bR_b[sl], bias=1.0)
                      ACT(bI[:, sl, :], bI[:, sl, :], AF.Ln, bI_b[sl], bI_b[sl], bias=1.0)
                  for hf in range(2):
                      sl = slice(2 * hf, 2 * hf + 2)
                      ACT(bR[:, sl, :], bR[:, sl, :], AF.Exp, bR_b[sl], bR_b[sl], scale=-1.0)
                      ACT(bI[:, sl, :], bI[:, sl, :], AF.Exp, bI_b[sl], bI_b[sl], scale=-1.0)
                  ml_emit(per_gap)
                  fill_pair()
                  for ci, c in enumerate(cs):
                      ACT(bA[:, ci, :], bR[:, ci, :], AF.Exp, [bR_b[ci], par_b], [bA_b[ci]], scale=dv[:, l, 24 + c:25 + c])
                      ACT(bM[:, ci, :], bR[:, ci, :], AF.Exp, [bR_b[ci], par_b], [bM_b[ci]], scale=dv[:, l, 32 + c:33 + c])
                      ml_emit(1)
                  for hf in range(2):
                      sl = slice(2 * hf, 2 * hf + 2)
                      TT("dve", bI[:, sl, :], bI[:, sl, :], bX[:, sl, :], ALU.mult, bI_b[sl] + bX_b[sl], bI_b[sl])
                  ml_emit(per_gap)
                  fill_pair()
                  for hf in range(2):
                      sl = slice(2 * hf, 2 * hf + 2)
                      ACT(bM[:, sl, :], bM[:, sl, :], AF.Ln, bM_b[sl], bM_b[sl], scale=-1.0, bias=1.0)
                  for hf in range(2):
                      sl = slice(2 * hf, 2 * hf + 2)
                      ACT(bM[:, sl, :], bM[:, sl, :], AF.Exp, bM_b[sl], bM_b[sl], scale=0.5)
                  ml_emit(per_gap)
                  fill_pair()
                  for hf in range(2):
                      sl = slice(2 * hf, 2 * hf + 2)
                      TT("dve", bI[:, sl, :], bI[:, sl, :], bM[:, sl, :], ALU.mult, bI_b[sl] + bM_b[sl], bI_b[sl])
                  for ci, c in enumerate(cs):
                      fw.op("dve", lambda e: e.tensor_tensor_scan(out=bM[:, ci, :], data0=bA[:, ci, :], data1=bI[:, ci, :],
                                                                  initial=rgh[l][:, c:c + 1], op0=ALU.mult, op1=ALU.add),
                            [bA_b[ci], bI_b[ci], rgh_b[l][c]], [bM_b[ci]])
                  ml_emit(per_gap)
                  fill_pair()
                  for ci, c in enumerate(cs):
                      CP("act", rgh[l][:, c:c + 1], bM[:, ci, T - 1:T], [bM_b[ci]], [rgh_b[l][c]])
                      TT("dve", yT[:, c, :], bM[:, ci, :], zsr[:, c, :], ALU.mult, [bM_b[ci], zsr_b[c]], [yT_b[c]])
              ml_emit(len(ml_sched))
              while fill_pos[0] < 8:
                  fill_pair()
              release(sidx, 5)
              if dbg == 3 and sidx == len(steps) - 1:
                  dump([(cell[:], yT[:, 0:8, :], yT_b[0:8])])
              if dbg == 6 and sidx == len(steps) - 1:
                  fw.dma("sp", "ostore", oT_d.rearrange("(c p) t -> p c t", p=128)[:, :, 0:T], cell[:], reads=cell_b, writes=[])
                  raise _Stop()
              for h in range(4):
                  hs_ = slice(2 * h, 2 * h + 2)
                  TT("dve", cell[:, hs_, :], cell[:, hs_, :], sgo[:, hs_, :], ALU.mult, cell_b[hs_] + sgo_b[hs_], cell_b[hs_])
                  ACT(sq[:, hs_, :], cell[:, hs_, :], AF.Square, cell_b[hs_], sq_b[hs_])
              for h in range(4):
                  pb, pbb = gen_bank()
                  for dc in range(2):
                      MM(pb[:, 0:T], inv256[:], sq[:, 2 * h + dc, :], dc == 0, dc == 1, [const_b, sq_b[2 * h + dc]], [pbb],
                         signal=(dc == 1))
                  s = h % 2
                  ln_, rs_ = tmp[s][0], tmp[s][1]
                  ACT(ln_[:], pb[:, 0:T], AF.Ln, [pbb], [tmp_b[s][0]], bias=EPS)
                  ACT(rs_[:], ln_[:], AF.Exp, [tmp_b[s][0]], [tmp_b[s][1]], scale=-0.5)
                  for dc in range(2):
                      c = 2 * h + dc
                      t1 = tmp[s][2 + dc]
                      STT(t1[:], cell[:, c, :], pcol(l, O_MNG)[:, c:c + 1], zsm[:, c, :], ALU.mult, ALU.mult,
                          [cell_b[c], par_b, zsm_b[c]], [tmp_b[s][2 + dc]])
                      TT("dve", yT[:, 8 + c, :], t1[:], rs_[:], ALU.mult, [tmp_b[s][2 + dc], tmp_b[s][1]], [yT_b[8 + c]])

              if dbg == 7 and sidx == len(steps) - 1:
                  dump([(cell[:], yT[:, 8:16, :], yT_b[8:16])])
              pieces_o = [get_piece(sidx, 6), get_piece(sidx, 7)]
              for c in range(8):
                  pb, pbb = gen_bank()
                  for kc in range(16):
                      r, rb = pieces_o[kc // 8]
                      w3 = r[:, 0:8192].rearrange("p (kc n) -> p kc n", kc=8)
                      MM(pb[:, 0:T], w3[:, kc % 8, c * 128:(c + 1) * 128], yT[:, kc, :], kc == 0, kc == 15,
                         [rb, yT_b[kc]], [pbb], signal=(kc == 15))
                  if c >= 1:
                      MM(psM[:, 128:128 + T], inv1024[:], sq[:, c - 1, :], c == 1, False, [const_b, sq_b[c - 1]], [psM_b], signal=False)
                  if l == nl - 1 and dbg != 8:
                      STT(cell[:, c, :], pb[:, 0:T], modT[:, l, 16 + c:17 + c], xT[:, c, :], ALU.mult, ALU.add,
                          [pbb, par_b, xT_b[c]], [cell_b[c]])
                      ACT(sq[:, c, :], cell[:, c, :], AF.Square, [cell_b[c]], [sq_b[c]])
                  else:
                      STT(xT[:, c, :], pb[:, 0:T], modT[:, l, 16 + c:17 + c], xT[:, c, :], ALU.mult, ALU.add,
                          [pbb, par_b, xT_b[c]], [xT_b[c]])
                      ACT(sq[:, c, :], xT[:, c, :], AF.Square, [xT_b[c]], [sq_b[c]])
              MM(psM[:, 128:128 + T], inv1024[:], sq[:, 7, :], False, True, [const_b, sq_b[7]], [psM_b])
              if l == nl - 1 and sidx + 1 < len(steps):
                  deferred_rel.extend([(sidx, 6), (sidx, 7)])
              else:
                  release(sidx, 6)
                  release(sidx, 7)

              if dbg == 8 and sidx == len(steps) - 1:
                  dump([(cell[:], xT[:], xT_b)])
              if l == nl - 1:
                  ACT(lnr[:], psM[:, 128:128 + T], AF.Ln, [psM_b], [lnr_b], bias=EPS)
                  ACT(rstd[:], lnr[:], AF.Exp, [lnr_b], [rstd_b], scale=-0.5)
                  for c in range(8):
                      STT(cell[:, c, :], cell[:, c, :], pv[:, O_FG + c:O_FG + c + 1], rstd[:], ALU.mult, ALU.mult,
                          [cell_b[c], par_b, rstd_b], [cell_b[c]])
                  fw.dma("sp", "ostore", oT_d.rearrange("(c p) t -> p c t", p=128)[:, :, t0:t0 + T], cell[:],
                         reads=cell_b, writes=[])
                  if dbg == 9 and sidx == 0:
                      raise _Stop()
        except _Stop:
            pass
        nc._fw_counts = dict(fw.cnt)
        for key_ in list(fw.cnt.keys()):
            if key_ not in fw.eng and fw.cnt[key_] > 0:
                fw._wait("sp", key_, fw.cnt[key_])
    return nc


_NC_CACHE = {}


def _feat(v):
    return np.ascontiguousarray(np.asarray(v, np.float32).reshape(8, 128).T)


def kernel(x, c, norm_g, w_ada, b_ada, w_in, rg_conv_w, rg_conv_b, rg_w_a, rg_b_a, rg_w_x, rg_b_x, rg_lambda,
           ml_conv_w, ml_conv_b, ml_w_q, ml_w_k, ml_w_v, ml_w_if, ml_b_if, ml_norm_g, w_out, final_g):
    f32 = lambda a: np.ascontiguousarray(np.asarray(a, np.float32))
    x = f32(x)
    B = x.shape[0]
    n_cores = 8
    if "nc" not in _NC_CACHE:
        _NC_CACHE["nc"] = build_nc()
    nc = _NC_CACHE["nc"]
    tri = np.triu(np.ones((128, 128), np.float32))
    in_maps = []
    for core in range(n_cores):
        b = core % B
        cols = []
        for l in range(NL):
            cols.append(_feat(norm_g[l]))
            for k in range(4):
                cols.append(_feat(rg_conv_w[l][k]))
            cols.append(_feat(rg_conv_b[l]))
            cols.append(_feat(rg_b_a[l]))
            cols.append(_feat(rg_b_x[l]))
            cols.append(_feat(rg_lambda[l]))
            for k in range(4):
                cols.append(_feat(ml_conv_w[l][k]))
            cols.append(_feat(ml_conv_b[l]))
            cols.append(_feat(ml_norm_g[l]))
            ba = np.asarray(b_ada[l], np.float32)
            for j in range(3):
                cols.append(_feat(ba[j * D:(j + 1) * D]))
        cols.append(_feat(final_g))
        cols.append(_feat(np.asarray(c, np.float32)[b]))
        pvh = np.ascontiguousarray(np.concatenate(cols, axis=1))
        assert pvh.shape == (128, NPV)
        in_maps.append({
            "xT": np.ascontiguousarray(x[b].T),
            "pv": pvh,
            "bif": f32(ml_b_if).reshape(1, NL * 8),
            "w_ada": f32(w_ada), "w_in": f32(w_in), "rg_w_a": f32(rg_w_a), "rg_w_x": f32(rg_w_x),
            "ml_w_q": f32(ml_w_q), "ml_w_k": f32(ml_w_k), "ml_w_v": f32(ml_w_v), "ml_w_if": f32(ml_w_if),
            "w_out": f32(w_out), "tri": tri,
        })
    res = run_bass_kernel_spmd(nc, in_maps, core_ids=list(range(n_cores)))
    out = np.empty((B, S, D), np.float32)
    for b in range(B):
        out[b] = res.results[b]["oT"].T
    return out
```
